# Optimizing a Trainium2 kernel written in Bass

```python
import math
import jax, jax.numpy as jnp
from jax import lax
import numpy as np

D_MODEL = 4096
BATCH = 2
SEQ = 4096
DEPTH = 1

CHUNK = 64
W_LRU = D_MODEL // 2
LRU_BLOCKS = 16
LRU_BLOCK = W_LRU // LRU_BLOCKS
LRU_CONV = 4
LRU_C = 8.0
W_SSM = D_MODEL // 2
SSM_GROUP = 16
SSM_GROUPS = W_SSM // SSM_GROUP
SSM_STATE = 64
D_FF = ((8 * D_MODEL // 3 + 255) // 256) * 256
FFN_CONV = 3
N_IN = 2 * W_LRU + W_SSM + 2 * D_MODEL
LN_EPS = 1e-5
DN_ALPHA = (2.0 * DEPTH) ** 0.25
DN_BETA = (8.0 * DEPTH) ** -0.25

kernel_name = "hybrid_rglru_s5_convffn_deepnorm_adaln"


def _layernorm(x):
    xf = x.astype(jnp.float32)
    mu = jnp.mean(xf, axis=-1, keepdims=True)
    var = jnp.mean(jnp.square(xf - mu), axis=-1, keepdims=True)
    return ((xf - mu) * lax.rsqrt(var + LN_EPS)).astype(x.dtype)


def _causal_dwconv(x, w, b):
    k, ch = w.shape
    y = lax.conv_general_dilated(
        x, w[:, None, :], window_strides=(1,), padding=[(k - 1, 0)],
        dimension_numbers=("NWC", "WIO", "NWC"), feature_group_count=ch)
    return y + b


def _linear_scan_combine(left, right):
    a1, b1 = left
    a2, b2 = right
    return a1 * a2, a2 * b1 + b2


def _rglru(x, w_r, b_r, w_i, b_i, lam):
    bn, sl, w = x.shape
    xb = x.reshape(bn, sl, LRU_BLOCKS, LRU_BLOCK)
    r = jax.nn.sigmoid(jnp.einsum("blhi,hij->blhj", xb, w_r) + b_r).reshape(bn, sl, w)
    i = jax.nn.sigmoid(jnp.einsum("blhi,hij->blhj", xb, w_i) + b_i).reshape(bn, sl, w)
    log_a = -LRU_C * r.astype(jnp.float32) * jax.nn.softplus(-lam.astype(jnp.float32))
    a = jnp.exp(log_a)
    b = jnp.sqrt(-jnp.expm1(2.0 * log_a)) * (i * x).astype(jnp.float32)
    _, h = lax.associative_scan(_linear_scan_combine, (a, b), axis=1)
    return h.astype(x.dtype)


def _s5(u, lam_re, lam_im, b_re, b_im, c_re, c_im, d_skip, log_step, w_glu, b_glu):
    bn, sl, w = u.shape
    f32 = jnp.float32
    lam = lax.complex(lam_re.astype(f32), lam_im.astype(f32))
    step = jnp.exp(log_step.astype(f32))[:, None]
    lam_bar = jnp.exp(lam * step)
    bmat = lax.complex(b_re.astype(f32), b_im.astype(f32))
    b_bar = ((lam_bar - 1.0) / lam)[..., None] * bmat
    cmat = lax.complex(c_re.astype(f32), c_im.astype(f32))
    ug = u.astype(f32).reshape(bn, sl, SSM_GROUPS, SSM_GROUP)
    bu = lax.complex(jnp.einsum("blgn,gpn->blgp", ug, jnp.real(b_bar)),
                     jnp.einsum("blgn,gpn->blgp", ug, jnp.imag(b_bar)))
    a = jnp.broadcast_to(lam_bar[None, None], (1, sl, SSM_GROUPS, SSM_STATE))
    _, h = lax.associative_scan(_linear_scan_combine, (a, bu), axis=1)
    y = jnp.real(jnp.einsum("blgp,gnp->blgn", h, cmat)).reshape(bn, sl, w)
    y = (y + d_skip.astype(f32) * u.astype(f32)).astype(u.dtype)
    y = jax.nn.gelu(y)
    return y * jax.nn.sigmoid(y @ w_glu + b_glu)


def setup_inputs(seed: int = 0) -> dict:
    key = jax.random.key(seed)
    ks = jax.random.split(key, 40)
    f32 = jnp.float32
    L_ = DEPTH

    def nrm(k, shape, scale):
        return jax.random.normal(k, shape, f32) * scale

    x = jax.random.normal(ks[0], (BATCH, SEQ, D_MODEL), f32)
    c = jax.random.normal(ks[1], (BATCH, D_MODEL), f32)
    w_ada = nrm(ks[2], (L_, D_MODEL, 6 * D_MODEL), 0.1 * D_MODEL ** -0.5)
    b_ada = nrm(ks[3], (L_, 6 * D_MODEL), 0.01)
    w_in = nrm(ks[4], (L_, D_MODEL, N_IN), D_MODEL ** -0.5)
    conv_lru_w = nrm(ks[5], (L_, LRU_CONV, W_LRU), LRU_CONV ** -0.5)
    conv_lru_b = nrm(ks[6], (L_, W_LRU), 0.01)
    lru_wr = nrm(ks[7], (L_, LRU_BLOCKS, LRU_BLOCK, LRU_BLOCK), LRU_BLOCK ** -0.5)
    lru_br = nrm(ks[8], (L_, LRU_BLOCKS, LRU_BLOCK), 0.01)
    lru_wi = nrm(ks[9], (L_, LRU_BLOCKS, LRU_BLOCK, LRU_BLOCK), LRU_BLOCK ** -0.5)
    lru_bi = nrm(ks[10], (L_, LRU_BLOCKS, LRU_BLOCK), 0.01)
    a_c = jax.random.uniform(ks[11], (L_, W_LRU), f32, 0.9, 0.999)
    sig = a_c ** (1.0 / LRU_C)
    lru_lambda = jnp.log(sig) - jnp.log1p(-sig)
    n_idx = jnp.arange(SSM_STATE, dtype=f32)
    ssm_lam_re = -0.5 + nrm(ks[12], (L_, SSM_GROUPS, SSM_STATE), 0.01)
    ssm_lam_im = math.pi * n_idx + nrm(ks[13], (L_, SSM_GROUPS, SSM_STATE), 0.01)
    bsc = (2.0 * SSM_GROUP) ** -0.5
    csc = (2.0 * SSM_STATE) ** -0.5
    ssm_b_re = nrm(ks[14], (L_, SSM_GROUPS, SSM_STATE, SSM_GROUP), bsc)
    ssm_b_im = nrm(ks[15], (L_, SSM_GROUPS, SSM_STATE, SSM_GROUP), bsc)
    ssm_c_re = nrm(ks[16], (L_, SSM_GROUPS, SSM_GROUP, SSM_STATE), csc)
    ssm_c_im = nrm(ks[17], (L_, SSM_GROUPS, SSM_GROUP, SSM_STATE), csc)
    ssm_d = nrm(ks[18], (L_, W_SSM), 1.0)
    ssm_log_step = jax.random.uniform(ks[19], (L_, SSM_GROUPS), f32,
                                      math.log(0.001), math.log(0.1))
    w_glu = nrm(ks[20], (L_, W_SSM, W_SSM), W_SSM ** -0.5)
    b_glu = nrm(ks[21], (L_, W_SSM), 0.01)
    w_proj_lru = nrm(ks[22], (L_, W_LRU, D_MODEL), DN_BETA * W_LRU ** -0.5)
    w_proj_ssm = nrm(ks[23], (L_, W_SSM, D_MODEL), DN_BETA * W_SSM ** -0.5)
    w_out = nrm(ks[24], (L_, D_MODEL, D_MODEL), DN_BETA * D_MODEL ** -0.5)
    ln1_g = 1.0 + nrm(ks[25], (L_, D_MODEL), 0.01)
    ln1_b = nrm(ks[26], (L_, D_MODEL), 0.01)
    ffn_w_up = nrm(ks[27], (L_, D_MODEL, 2 * D_FF), D_MODEL ** -0.5)
    ffn_conv_w = nrm(ks[28], (L_, FFN_CONV, 2 * D_FF), FFN_CONV ** -0.5)
    ffn_conv_b = nrm(ks[29], (L_, 2 * D_FF), 0.01)
    ffn_w_down = nrm(ks[30], (L_, D_FF, D_MODEL), DN_BETA * D_FF ** -0.5)
    ln2_g = 1.0 + nrm(ks[31], (L_, D_MODEL), 0.01)
    ln2_b = nrm(ks[32], (L_, D_MODEL), 0.01)
    return {"x": x, "c": c, "w_ada": w_ada, "b_ada": b_ada, "w_in": w_in,
            "conv_lru_w": conv_lru_w, "conv_lru_b": conv_lru_b,
            "lru_wr": lru_wr, "lru_br": lru_br, "lru_wi": lru_wi, "lru_bi": lru_bi,
            "lru_lambda": lru_lambda, "ssm_lam_re": ssm_lam_re, "ssm_lam_im": ssm_lam_im,
            "ssm_b_re": ssm_b_re, "ssm_b_im": ssm_b_im, "ssm_c_re": ssm_c_re,
            "ssm_c_im": ssm_c_im, "ssm_d": ssm_d, "ssm_log_step": ssm_log_step,
            "w_glu": w_glu, "b_glu": b_glu, "w_proj_lru": w_proj_lru,
            "w_proj_ssm": w_proj_ssm, "w_out": w_out, "ln1_g": ln1_g, "ln1_b": ln1_b,
            "ffn_w_up": ffn_w_up, "ffn_conv_w": ffn_conv_w, "ffn_conv_b": ffn_conv_b,
            "ffn_w_down": ffn_w_down, "ln2_g": ln2_g, "ln2_b": ln2_b}


def reference(x, c, w_ada, b_ada, w_in, conv_lru_w, conv_lru_b, lru_wr, lru_br, lru_wi,
              lru_bi, lru_lambda, ssm_lam_re, ssm_lam_im, ssm_b_re, ssm_b_im, ssm_c_re,
              ssm_c_im, ssm_d, ssm_log_step, w_glu, b_glu, w_proj_lru, w_proj_ssm, w_out,
              ln1_g, ln1_b, ffn_w_up, ffn_conv_w, ffn_conv_b, ffn_w_down, ln2_g, ln2_b):
    split_in = [W_LRU, 2 * W_LRU, 2 * W_LRU + W_SSM, 2 * W_LRU + W_SSM + D_MODEL]
    c_act = jax.nn.silu(c)
    for l in range(DEPTH):
        cond = (c_act @ w_ada[l] + b_ada[l])[:, None, :]
        sh1, sc1, g1, sh2, sc2, g2 = jnp.split(cond, 6, axis=-1)

        h = _layernorm(x) * (1.0 + sc1) + sh1
        z = h @ w_in[l]
        xa, ga, xs, gate_a, gate_b = jnp.split(z, split_in, axis=-1)
        xa = _causal_dwconv(xa, conv_lru_w[l], conv_lru_b[l])
        ya = _rglru(xa, lru_wr[l], lru_br[l], lru_wi[l], lru_bi[l], lru_lambda[l])
        ya = ya * jax.nn.gelu(ga)
        ys = _s5(xs, ssm_lam_re[l], ssm_lam_im[l], ssm_b_re[l], ssm_b_im[l],
                 ssm_c_re[l], ssm_c_im[l], ssm_d[l], ssm_log_step[l],
                 w_glu[l], b_glu[l])
        merged = (jax.nn.sigmoid(gate_a) * (ya @ w_proj_lru[l])
                  + jax.nn.sigmoid(gate_b) * (ys @ w_proj_ssm[l]))
        mix = merged @ w_out[l]
        x = _layernorm(DN_ALPHA * x + (1.0 + g1) * mix) * ln1_g[l] + ln1_b[l]

        h = _layernorm(x) * (1.0 + sc2) + sh2
        up = _causal_dwconv(h @ ffn_w_up[l], ffn_conv_w[l], ffn_conv_b[l])
        gate, val = jnp.split(up, 2, axis=-1)
        f = (jax.nn.gelu(gate) * val) @ ffn_w_down[l]
        x = _layernorm(DN_ALPHA * x + (1.0 + g2) * f) * ln2_g[l] + ln2_b[l]
    return x
```

```python
import contextlib
import math
import numpy as np
import concourse.bass as bass
import concourse.mybir as mybir
from concourse.bass_utils import run_bass_kernel_spmd

F32 = mybir.dt.float32
BF16 = mybir.dt.bfloat16
AF = mybir.ActivationFunctionType
ALU = mybir.AluOpType
AX = mybir.AxisListType

NCORE = 8
D = 4096
KT = 32
NT = 1032
HALO = 8
CW = 344
WL = 2048
DFF = 11008
NFT = 86
NIN = 14336
LN_EPS = 1e-5
ALPHA = 2.0 ** 0.25
CB = 24
NBK = NT // CB
EXP_POS = 1026
S5_EXP = 1023

ENGS = ("pe", "act", "dve", "pool", "sp")
SEG = 12000
NDSEM = 12


class _Rec:
    def __init__(self):
        self.call = None

    def __getattr__(self, name):
        def f(*a, **k):
            assert self.__dict__["call"] is None
            self.__dict__["call"] = (name, a, k)
            return self
        return f


def _freeze(fn):
    if fn is None:
        return None
    rec = _Rec()
    fn(rec)
    name, a, k = rec.call
    return lambda h: getattr(h, name)(*a, **k)


class Sched:
    def __init__(self, nc):
        self.nc = nc
        self.ops = {e: [] for e in ENGS}
        self.lastw = {}
        self.readers = {}
        self.ndma = {e: 0 for e in ENGS}

    def _add(self, eng, fn, reads, writes, dma):
        writes = list(writes)
        for k in reads:
            if isinstance(k, tuple) and k[0] == "ps" and k not in writes:
                writes.append(k)
        deps = []
        rawset = set()
        for k in reads:
            t = self.lastw.get(k)
            if t is not None:
                deps.append(t)
                rawset.add(t)
        for k in writes:
            t = self.lastw.get(k)
            if t is not None:
                deps.append(t)
            deps.extend(self.readers.get(k, ()))
        idx = len(self.ops[eng])
        if dma:
            k = self.ndma[eng]
            self.ndma[eng] += 1
            tok = ("d", eng, k)
            if k >= NDSEM:
                deps.append(("d", eng, k - NDSEM))
        else:
            tok = ("e", eng, idx)
        d2 = []
        for t in set(deps):
            if t[0] == "e" and t[1] == eng:
                if dma:
                    d2.append(t)
                    continue
                if t not in rawset or eng == "pe":
                    continue
            d2.append(t)
        self.ops[eng].append(dict(fn=_freeze(fn), deps=d2, dma=dma, inc=False, tok=tok))
        for k in reads:
            self.readers.setdefault(k, []).append(tok)
        for k in writes:
            self.lastw[k] = tok
            self.readers[k] = []
        return tok

    def op(self, eng, fn, reads=(), writes=()):
        return self._add(eng, fn, reads, writes, False)

    def barrier(self):
        toks = []
        for e in ENGS:
            for i in range(len(self.ops[e]) - 1, -1, -1):
                o = self.ops[e][i]
                if not o["dma"] and o["fn"] is not None:
                    toks.append(("e", e, i))
                    break
            n = self.ndma[e]
            for k in range(max(0, n - NDSEM), n):
                toks.append(("d", e, k))
        for e in ENGS:
            self.ops[e].append(dict(fn=None, deps=[t for t in toks if not (t[0] == "e" and t[1] == e)],
                                    dma=False, inc=False, tok=None))
        self.lastw = {}
        self.readers = {}

    def dma(self, eng, fn, reads=(), writes=()):
        return self._add(eng, fn, reads, writes, True)

    def emit(self, final_wait_eng="sp"):
        nc = self.nc
        for e in ENGS:
            for o in self.ops[e]:
                for t in o["deps"]:
                    if t[0] == "e":
                        self.ops[t[1]][t[2]]["inc"] = True
        for e in ENGS:
            for o in reversed(self.ops[e]):
                if not o["dma"] and o["fn"] is not None:
                    o["inc"] = True
                    break
        cnt = {}
        nseg = {}
        for e in ENGS:
            c = 0
            for i, o in enumerate(self.ops[e]):
                if o["inc"] and not o["dma"]:
                    cnt[(e, i)] = c
                    c += 1
            nseg[e] = max(1, (c + SEG - 1) // SEG)
        esem = {e: [nc.alloc_semaphore(name=nc.make_name(f"s_{e}_{j}", True)) for j in range(nseg[e])]
                for e in ENGS}
        dsem = {e: [nc.alloc_semaphore(name=nc.make_name(f"d_{e}_{j}", True)) for j in range(NDSEM)]
                for e in ENGS if self.ndma[e] > 0}

        def target(t):
            if t[0] == "e":
                c = cnt[(t[1], t[2])]
                return esem[t[1]][c // SEG], (c % SEG) + 1, ("e", t[1], c // SEG)
            k = t[2]
            return dsem[t[1]][k % NDSEM], 16 * (k // NDSEM + 1), ("d", t[1], k % NDSEM)

        def run(e, h):
            waited = {}
            for i, o in enumerate(self.ops[e]):
                for t in o["deps"]:
                    s, v, key = target(t)
                    if waited.get(key, 0) >= v:
                        continue
                    waited[key] = v
                    h.wait_ge(s, v)
                if o["fn"] is None:
                    continue
                ins = o["fn"](h)
                if o["dma"]:
                    s, v, _ = target(o["tok"])
                    ins.then_inc(s, 16)
                elif o["inc"]:
                    c = cnt[(e, i)]
                    ins.then_inc(esem[e][c // SEG], 1)
            if e == final_wait_eng:
                for e2 in ENGS:
                    for i2 in range(len(self.ops[e2]) - 1, -1, -1):
                        if not self.ops[e2][i2]["dma"] and self.ops[e2][i2]["fn"] is not None:
                            s, v, _ = target(("e", e2, i2))
                            h.wait_ge(s, v)
                            break
                    n = self.ndma[e2]
                    for k in range(max(0, n - NDSEM), n):
                        s, v, _ = target(("d", e2, k))
                        h.wait_ge(s, v)

        with nc.Block() as block:
            hmap = {"pe": block.tensor, "act": block.scalar, "dve": block.vector,
                    "pool": block.gpsimd, "sp": block.sync}
            for e in ENGS:
                if not self.ops[e] and e != final_wait_eng:
                    continue
                hmap[e](lambda h, e=e: run(e, h))


def v3(ap2d):
    return ap2d.rearrange("p (b c) -> p b c", c=CW)


class Builder:
    def __init__(self, stage=99, debug=(), ncore=NCORE):
        self.ncore = ncore
        self.stage = stage
        self.debug = set(debug)
        self.nc = nc = bass.Bass("TRN2", target_bir_lowering=False)
        self.dbg_out = {}
        self._ishape = {
            "xh": [NT, D],
            "cT": [128, KT],
            "flags": [1, 17],
            "w_ada": [1, D, 6 * D],
            "b_ada": [1, 6 * D],
            "w_in": [1, D, NIN],
            "conv_lru_w": [1, 4, WL],
            "conv_lru_b": [1, WL],
            "lru_wr": [1, 16, 128, 128],
            "lru_br": [1, 16, 128],
            "lru_wi": [1, 16, 128, 128],
            "lru_bi": [1, 16, 128],
            "lru_lambda": [1, WL],
            "ssm_lam_re": [1, 128, 64],
            "ssm_lam_im": [1, 128, 64],
            "ssm_b_re": [1, 128, 64, 16],
            "ssm_b_im": [1, 128, 64, 16],
            "ssm_c_re": [1, 128, 16, 64],
            "ssm_c_im": [1, 128, 16, 64],
            "ssm_d": [1, WL],
            "ssm_log_step": [1, 128],
            "w_glu": [1, WL, WL],
            "b_glu": [1, WL],
            "w_proj_lru": [1, WL, D],
            "w_proj_ssm": [1, WL, D],
            "w_out": [1, D, D],
            "ln1_g": [1, D],
            "ln1_b": [1, D],
            "ffn_w_up": [1, D, 2 * DFF],
            "ffn_conv_w": [1, 3, 2 * DFF],
            "ffn_conv_b": [1, 2 * DFF],
            "ffn_w_down": [1, DFF, D],
            "ln2_g": [1, D],
            "ln2_b": [1, D],
        }
        self._idecl = {}
        self.out = nc.dram_tensor("out", [1024, D], F32, kind="ExternalOutput").ap()
        self.ag_in = nc.dram_tensor("ag_in", [128, 288], F32)
        self.ag_out = nc.dram_tensor("ag_out", [ncore * 128, 288], F32)
        self.mg_d = nc.dram_tensor("mg_d", [KT, 128, NT], BF16).ap()
        self.y1_d = nc.dram_tensor("y1_d", [NT, D], F32).ap()
        self.x1_d = nc.dram_tensor("x1_d", [NT, D], F32).ap()
        self.y2_d = nc.dram_tensor("y2_d", [NT, D], F32).ap()
        self.wa_n = 0
        self.ps_n = 0

    def __getattr__(self, name):
        ish = self.__dict__.get("_ishape", {})
        if name in ish:
            d = self.__dict__["_idecl"]
            if name not in d:
                d[name] = self.nc.dram_tensor(name, list(ish[name]), F32, kind="ExternalInput").ap()
            return d[name]
        raise AttributeError(name)

    def T(self, name, shape, dt=F32):
        return self.nc.alloc_sbuf_tensor(self.nc.make_name(name, True), list(shape), dt)

    def dbg(self, S, name, src_ap, shape, dt=F32, reads=()):
        if name not in self.debug:
            return
        t = self.nc.dram_tensor("dbg_" + name, list(shape), dt, kind="ExternalOutput").ap()
        self.dbg_out[name] = "dbg_" + name
        if len(shape) == 3 and shape[1] * shape[2] > 4096:
            for i in range(shape[1]):
                S.dma("sp", lambda h, i=i: h.dma_start(out=t[:, i, :], in_=src_ap[:, i, :]), reads=list(reads))
        else:
            S.dma("sp", lambda h: h.dma_start(out=t, in_=src_ap), reads=list(reads))

    @contextlib.contextmanager
    def scope(self):
        nc = self.nc
        saved = (nc.sbuf_base, nc.sbuf_top)
        yield
        self.S.barrier()
        nc.sbuf_base, nc.sbuf_top = saved

    @contextlib.contextmanager
    def phase(self):
        with self.scope():
            yield self.S

    def wslot(self):
        s = self.wa_n % 2
        self.wa_n += 1
        return s

    def wload(self, S, dst, src, slot):
        S.dma("pool", lambda h: h.dma_start(out=dst, in_=src), writes=[("wa", slot)])

    def load_cols(self, S, dst, src_rows, n, key, tmpname):
        PS, ident = self.PS, self.ident
        done = 0
        i = 0
        while done < n:
            m = min(128, n - done)
            tmp = self.T(f"{tmpname}{i}", [128, 128])
            k1 = (tmpname, i)
            S.dma("sp", lambda h, tmp=tmp, m=m, d=done: h.dma_start(out=tmp[0:m, :], in_=src_rows[d:d + m, :]),
                  writes=[k1])
            S.op("pe", lambda h, tmp=tmp, m=m: h.transpose(out=PS[:, 7, 0:m], in_=tmp[0:m, :], identity=ident[0:m, 0:m]),
                 reads=[k1, "ident"], writes=[("ps", 7)])
            S.op("dve", lambda h, m=m, d=done: h.tensor_copy(out=dst[:, d:d + m], in_=PS[:, 7, 0:m]),
                 reads=[("ps", 7)], writes=[key])
            done += m
            i += 1

    def fm_matmul(self, S, psb, wv, nk, rhs_fn, rkeys, slot):
        PS = self.PS
        for kt in range(nk):
            for b in range(3):
                S.op("pe", lambda h, kt=kt, b=b: h.matmul(PS[:, psb + b, 0:CW], wv[:, kt, :], rhs_fn(kt, b),
                                                          start=(kt == 0), stop=(kt == nk - 1)),
                     reads=[("wa", slot)] + list(rkeys), writes=[("ps", psb + b)])

    def big(self):
        b = 3 * (self.ps_n % 2)
        self.ps_n += 1
        return b

    def ln_rows(self, S, X, M, tag, affine=None):
        J = self.lnJ
        J2 = self.lnJ2
        st = self.lnst
        kx = ("X", tag)
        S.op("dve", lambda h: h.tensor_scalar(out=J[0:M, :], in0=X[0:M, :], scalar1=1.0 / D, scalar2=None,
                                              op0=ALU.mult, op1=ALU.add, accum_out=st[0:M, 0:1]),
             reads=[kx], writes=["lnJ", "st0"])
        S.op("dve", lambda h: h.tensor_scalar(out=X[0:M, :], in0=X[0:M, :], scalar1=st[0:M, 0:1], scalar2=None,
                                              op0=ALU.subtract),
             reads=[kx, "st0"], writes=[kx])
        S.op("act", lambda h: h.activation(out=J2[0:M, :], in_=X[0:M, :], func=AF.Square),
             reads=[kx], writes=["lnJ2"])
        S.op("dve", lambda h: h.tensor_scalar(out=J[0:M, :], in0=J2[0:M, :], scalar1=1.0 / D, scalar2=None,
                                              op0=ALU.mult, op1=ALU.add, accum_out=st[0:M, 1:2]),
             reads=["lnJ2"], writes=["lnJ", "st1"])
        S.op("act", lambda h: h.activation(out=st[0:M, 2:3], in_=st[0:M, 1:2], func=AF.Sqrt, bias=self.epsc[0:M, :]),
             reads=["st1"], writes=["st2"])
        S.op("dve", lambda h: h.reciprocal(out=st[0:M, 3:4], in_=st[0:M, 2:3]), reads=["st2"], writes=["st3"])
        S.op("act", lambda h: h.activation(out=X[0:M, :], in_=X[0:M, :], func=AF.Identity, scale=st[0:M, 3:4]),
             reads=[kx, "st3"], writes=[kx])
        if affine is not None:
            G, Bt = affine
            S.op("dve", lambda h: h.tensor_tensor(out=X[0:M, :], in0=X[0:M, :], in1=G[0:M, :], op=ALU.mult),
                 reads=[kx, "lnG"], writes=[kx])
            S.op("pool", lambda h: h.tensor_tensor(out=X[0:M, :], in0=X[0:M, :], in1=Bt[0:M, :], op=ALU.add),
                 reads=[kx, "lnB"], writes=[kx])

    def rows_to_T(self, S, X, M, tag, c0, sc_off, sh_off, n_evac):
        PS, ident, BIG1, COND = self.PS, self.ident, self.BIG1, self.COND
        kx = ("X", tag)
        for f4 in range(8):
            bank = 6 + (f4 % 2)
            for q in range(4):
                ft = f4 * 4 + q
                S.op("pe", lambda h, ft=ft, q=q, bank=bank: h.transpose(out=PS[:, bank, q * 128:q * 128 + M],
                                                                     in_=X[0:M, ft * 128:(ft + 1) * 128],
                                                                     identity=ident[0:M, 0:M]),
                     reads=[kx, "ident"], writes=[("ps", bank)])
            for q in range(4):
                ft = f4 * 4 + q
                if bank == 7:
                    S.op("act", lambda h, ft=ft, q=q, bank=bank: h.activation(
                        out=BIG1[:, ft, c0:c0 + M], in_=PS[:, bank, q * 128:q * 128 + M], func=AF.Identity,
                        scale=COND[:, sc_off + ft:sc_off + ft + 1], bias=COND[:, sh_off + ft:sh_off + ft + 1]),
                         reads=[("ps", bank), "COND"], writes=[("big1", ft)])
                else:
                    S.op("dve", lambda h, ft=ft, q=q, bank=bank: h.tensor_scalar(
                        out=BIG1[:, ft, c0:c0 + M], in0=PS[:, bank, q * 128:q * 128 + M],
                        scalar1=COND[:, sc_off + ft:sc_off + ft + 1], scalar2=COND[:, sh_off + ft:sh_off + ft + 1],
                        op0=ALU.mult, op1=ALU.add),
                         reads=[("ps", bank), "COND"], writes=[("big1", ft)])

    def bcast_cond(self, S, GB, off):
        PS, ident, ones, COND = self.PS, self.ident, self.ones, self.COND
        Dg = self.T("bc_diag", [128, 2, 128])
        for ft in range(KT):
            s = ft % 2
            S.op("dve", lambda h, ft=ft, s=s: h.tensor_scalar(out=Dg[:, s, :], in0=ident[:, :],
                                                              scalar1=COND[:, off + ft:off + ft + 1], scalar2=None,
                                                              op0=ALU.mult),
                 reads=["ident", "COND"], writes=[("bcd", s)])
            S.op("pe", lambda h, s=s: h.matmul(PS[:, 6 + s, 0:128], ones[:, :], Dg[:, s, :], start=True, stop=True),
                 reads=[("bcd", s), "ones"], writes=[("ps", 6 + s)])
            S.op("act", lambda h, ft=ft, s=s: h.activation(out=GB[:, ft * 128:(ft + 1) * 128], in_=PS[:, 6 + s, 0:128],
                                                           func=AF.Identity),
                 reads=[("ps", 6 + s)], writes=["GB"])

    def build(self):
        nc = self.nc
        stage = self.stage
        self.PS = PS = nc.alloc_psum_tensor("PS", [128, 8, 512], F32)
        self.ident = ident = self.T("ident", [128, 128])
        self.ones = ones = self.T("ones", [128, 128])
        self.FL = FL = self.T("FL", [128, 17])
        self.COND = COND = self.T("COND", [128, 192])
        self.GATH = GATH = self.T("GATH", [128, 288])
        self.epsc = self.T("epsc", [128, 1])
        self.CW4 = self.T("CW4", [128, 64])
        self.CWB = self.T("CWB", [128, 16])
        self.BR = self.T("BR", [128, 16])
        self.BI = self.T("BI", [128, 16])
        self.C1 = self.T("C1", [128, 32])
        self.WR = self.T("WR", [128, 16, 128], BF16)
        self.WI = self.T("WI", [128, 16, 128], BF16)
        self.H0L = self.T("H0L", [128, 16])
        self.BIG1 = BIG1 = self.T("BIG1", [128, KT, NT], BF16)
        self.WA = WA = self.T("WA", [128, 2, 4096], BF16)

        self.S = Sched(nc)
        self.p_consts_ada()
        if stage >= 1:
            self.p_ln1()
        if stage >= 2:
            self.p_lru(1)
        if stage >= 3:
            with self.scope():
                self.U = self.T("U", [128, 16, NT], BF16)
                with self.scope():
                    self.s5_alloc()
                    self.p_s5_prep()
                    self.p_xs()
                    if stage >= 4:
                        self.p_s5(1)
                    if stage >= 5:
                        self.p_gather()
                    if stage >= 6:
                        self.p_s5(2)
                if stage >= 7:
                    self.p_glu()
                    self.ys = self.U
                    self.ya = self.T("ya", [128, 16, NT], BF16)
                    self.p_lru(2)
                if stage >= 8:
                    self.p_merge()
        if stage >= 9:
            self.p_wout()
        if stage >= 10:
            self.p_ln_mid()
        if stage >= 11:
            self.p_ffn()
        if stage >= 12:
            self.p_ln_final()
        self.S.emit()
        return nc

    def p_consts_ada(self):
        nc = self.nc
        PS, ident, ones, FL, COND, WA = self.PS, self.ident, self.ones, self.FL, self.COND, self.WA
        with self.phase() as S:
            S.op("pool", lambda h: h.memset(ident[:, :], 0.0), writes=["ident"])
            S.op("pool", lambda h: h.affine_select(out=ident[:, :], in_=ident[:, :], pattern=[[-1, 128]],
                                                   compare_op=ALU.not_equal, fill=1.0, base=0, channel_multiplier=1),
                 reads=["ident"], writes=["ident"])
            S.op("pool", lambda h: h.memset(ones[:, :], 1.0), writes=["ones"])
            S.op("pool", lambda h: h.memset(self.epsc[:, :], LN_EPS), writes=["epsc"])
            S.dma("sp", lambda h: h.dma_start(out=FL[:, :], in_=self.flags.partition_broadcast(128)), writes=["FL"])
            ct32 = self.T("ct32", [128, KT])
            cact = self.T("cact", [128, KT], BF16)
            S.dma("sp", lambda h: h.dma_start(out=ct32[:, :], in_=self.cT), writes=["ct32"])
            S.op("act", lambda h: h.activation(out=cact[:, :], in_=ct32[:, :], func=AF.Silu), reads=["ct32"], writes=["cact"])
            bada = self.T("bada", [128, 192])
            self.load_cols(S, bada, self.b_ada[0].rearrange("(o p) -> o p", p=128), 192, "bada", "tb")
            self.load_cols(S, self.CW4, self.conv_lru_w[0].rearrange("k (c p) -> (k c) p", p=128), 64, "CW4", "tcw")
            self.load_cols(S, self.CWB, self.conv_lru_b[0].rearrange("(c p) -> c p", p=128), 16, "CWB", "tcb")
            self.load_cols(S, self.BR, self.lru_br[0], 16, "BR", "tbr")
            self.load_cols(S, self.BI, self.lru_bi[0], 16, "BI", "tbi")
            lam = self.T("lamc", [128, 16])
            self.load_cols(S, lam, self.lru_lambda[0].rearrange("(c p) -> c p", p=128), 16, "lamc", "tlm")
            e1 = self.T("e1", [128, 16])
            S.op("act", lambda h: h.activation(out=e1[:, :], in_=lam[:, :], func=AF.Exp, scale=-1.0), reads=["lamc"], writes=["e1"])
            S.op("act", lambda h: h.activation(out=e1[:, :], in_=e1[:, :], func=AF.Ln, bias=ones[:, 0:1]), reads=["e1", "ones"], writes=["e1"])
            S.op("dve", lambda h: h.tensor_scalar(out=self.C1[:, 0:16], in0=e1[:, :], scalar1=-8.0, scalar2=None, op0=ALU.mult),
                 reads=["e1"], writes=["C1"])
            S.op("dve", lambda h: h.tensor_scalar(out=self.C1[:, 16:32], in0=e1[:, :], scalar1=-16.0, scalar2=None, op0=ALU.mult),
                 reads=["e1"], writes=["C1"])
            S.dma("pool", lambda h: h.dma_start(out=self.WR[:, :, :], in_=self.lru_wr[0].rearrange("h i j -> i h j")), writes=["WR"])
            S.dma("pool", lambda h: h.dma_start(out=self.WI[:, :, :], in_=self.lru_wi[0].rearrange("h i j -> i h j")), writes=["WI"])
            import os
            for ot in range(0 if os.environ.get('SKIP_ADA') else 192):
                slot = self.wslot()
                wv = WA[:, slot, :].rearrange("p (k c) -> p k c", c=128)
                self.wload(S, wv, self.w_ada[0][:, ot * 128:(ot + 1) * 128].rearrange("(k p) c -> p k c", p=128), slot)
                for kt in range(KT):
                    S.op("pe", lambda h, wv=wv, kt=kt, ot=ot: h.matmul(PS[:, 5, ot:ot + 1], wv[:, kt, :], cact[:, kt:kt + 1],
                                                                       start=(kt == 0), stop=(kt == KT - 1)),
                         reads=[("wa", slot), "cact"], writes=[("ps", 5)])
            S.op("dve", lambda h: h.tensor_tensor(out=COND[:, :], in0=PS[:, 5, 0:192], in1=bada[:, :], op=ALU.add),
                 reads=[("ps", 5), "bada"], writes=["COND"])
            for sec in (1, 2, 4, 5):
                S.op("dve", lambda h, sec=sec: h.tensor_scalar(out=COND[:, sec * 32:(sec + 1) * 32], in0=COND[:, sec * 32:(sec + 1) * 32],
                                                               scalar1=1.0, scalar2=None, op0=ALU.add),
                     reads=["COND"], writes=["COND"])
            self.dbg(S, "cond", COND[:, :], [128, 192], reads=["COND"])

    def tok_tiles(self):
        tiles = [(0, HALO)]
        for i in range(8):
            tiles.append((HALO + 128 * i, 128))
        return tiles

    def p_ln1(self):
        BIG1, FL = self.BIG1, self.FL
        with self.phase() as S:
            self.lnJ = self.T("lnJ", [128, D], BF16)
            self.lnJ2 = self.T("lnJ2", [128, D])
            self.lnst = self.T("lnst", [128, 4])
            XT = [self.T(f"XT{i}", [128, D]) for i in range(2)]
            nev = [0]
            for ti, (r0, M) in enumerate(self.tok_tiles()):
                X = XT[ti % 2]
                tag = ti % 2
                S.dma("act", lambda h, X=X, r0=r0, M=M: h.dma_start(out=X[0:M, :], in_=self.xh[r0:r0 + M, :]),
                      writes=[("X", tag)])
                import os
                cut = int(os.environ.get("LN1_CUT", "9"))
                if cut >= 1:
                    self.ln_rows(S, X, M, tag)
                if cut >= 2:
                    self.rows_to_T(S, X, M, tag, r0, 32, 0, nev)
            if cut >= 3:
              S.op("dve", lambda h: h.tensor_scalar(out=BIG1[:, :, 0:HALO], in0=BIG1[:, :, 0:HALO], scalar1=FL[:, 0:1],
                                                  scalar2=None, op0=ALU.mult),
                 reads=[("big1", ft) for ft in range(KT)] + ["FL"], writes=[("big1", ft) for ft in range(KT)])
            self.dbg(S, "hT", BIG1[:, :, :], [128, KT, NT], BF16, reads=[("big1", ft) for ft in range(KT)])

    def p_lru(self, pas):
        nc = self.nc
        PS, BIG1, WA, FL, GATH = self.PS, self.BIG1, self.WA, self.FL, self.GATH
        w_in = self.w_in[0]
        with self.phase() as S:
            xap = self.T("xap", [128, NT + 3])
            xc = self.T("xc", [128, NT])
            xcb = self.T("xcb", [128, NT], BF16)
            rr = self.T("rr", [128, NT])
            ii = self.T("ii", [128, NT])
            aa = self.T("aa", [128, NT])
            bb = self.T("bb", [128, NT])
            sm = self.T("sm", [128, 2])
            S.op("pool", lambda h: h.memset(xap[:, 0:3], 0.0), writes=["xap_pad"])
            hk = [("big1", ft) for ft in range(KT)]
            hh = xap[:, 3:NT + 3]
            gg = rr
            for ct in range(16):
                slot = self.wslot()
                wv = WA[:, slot, :].rearrange("p (k c) -> p k c", c=128)
                self.wload(S, wv, w_in[:, ct * 128:(ct + 1) * 128].rearrange("(k p) c -> p k c", p=128), slot)
                pb = self.big()
                self.fm_matmul(S, pb, wv, KT, lambda kt, b: BIG1[:, kt, b * CW:(b + 1) * CW], hk, slot)
                pk = [("ps", pb + b) for b in range(3)]
                S.op("act", lambda h, pb=pb: h.activation(out=v3(xap[:, 3:NT + 3]), in_=PS[:, pb:pb + 3, 0:CW], func=AF.Identity),
                     reads=pk, writes=["xap"])
                cw = lambda k, ct=ct: self.CW4[:, k * 16 + ct:k * 16 + ct + 1]
                S.op("dve", lambda h, ct=ct, cw=cw: h.tensor_scalar(out=xc[:, :], in0=xap[:, 0:NT], scalar1=cw(0),
                                                                    scalar2=self.CWB[:, ct:ct + 1], op0=ALU.mult, op1=ALU.add),
                     reads=["xap", "xap_pad", "CW4", "CWB"], writes=["xc"])
                for k in (1, 2, 3):
                    S.op("dve", lambda h, k=k, cw=cw: h.scalar_tensor_tensor(out=xc[:, :], in0=xap[:, k:k + NT], scalar=cw(k),
                                                                             in1=xc[:, :], op0=ALU.mult, op1=ALU.add),
                         reads=["xap", "xap_pad", "xc", "CW4"], writes=["xc"])
                S.op("pool", lambda h: h.tensor_copy(out=xcb[:, :], in_=xc[:, :]), reads=["xc"], writes=["xcb"])
                pr = self.big()
                for b in range(3):
                    S.op("pe", lambda h, b=b, ct=ct, pr=pr: h.matmul(PS[:, pr + b, 0:CW], self.WR[:, ct, :], xcb[:, b * CW:(b + 1) * CW],
                                                                     start=True, stop=True),
                         reads=["xcb", "WR"], writes=[("ps", pr + b)])
                S.op("act", lambda h, ct=ct, pr=pr: h.activation(out=v3(rr[:, :]), in_=PS[:, pr:pr + 3, 0:CW], func=AF.Sigmoid,
                                                                 bias=self.BR[:, ct:ct + 1]),
                     reads=[("ps", pr + b) for b in range(3)] + ["BR"], writes=["rr"])
                pi = self.big()
                for b in range(3):
                    S.op("pe", lambda h, b=b, ct=ct, pi=pi: h.matmul(PS[:, pi + b, 0:CW], self.WI[:, ct, :], xcb[:, b * CW:(b + 1) * CW],
                                                                     start=True, stop=True),
                         reads=["xcb", "WI"], writes=[("ps", pi + b)])
                S.op("act", lambda h, ct=ct, pi=pi: h.activation(out=v3(ii[:, :]), in_=PS[:, pi:pi + 3, 0:CW], func=AF.Sigmoid,
                                                                 bias=self.BI[:, ct:ct + 1]),
                     reads=[("ps", pi + b) for b in range(3)] + ["BI"], writes=["ii"])
                S.op("act", lambda h, ct=ct: h.activation(out=aa[:, :], in_=rr[:, :], func=AF.Exp, scale=self.C1[:, ct:ct + 1]),
                     reads=["rr", "C1"], writes=["aa"])
                S.op("act", lambda h, ct=ct: h.activation(out=bb[:, :], in_=rr[:, :], func=AF.Exp, scale=self.C1[:, 16 + ct:17 + ct]),
                     reads=["rr", "C1"], writes=["bb"])
                S.op("act", lambda h: h.activation(out=bb[:, :], in_=bb[:, :], func=AF.Sqrt, scale=-1.0, bias=self.ones[:, 0:1]),
                     reads=["bb", "ones"], writes=["bb"])
                S.op("pool", lambda h: h.tensor_tensor(out=ii[:, :], in0=ii[:, :], in1=xc[:, :], op=ALU.mult),
                     reads=["ii", "xc"], writes=["ii"])
                S.op("dve", lambda h: h.tensor_tensor(out=bb[:, :], in0=bb[:, :], in1=ii[:, :], op=ALU.mult),
                     reads=["bb", "ii"], writes=["bb"])
                S.op("dve", lambda h: h.tensor_tensor(out=bb[:, 0:8], in0=bb[:, 0:8], in1=FL[:, 1:9], op=ALU.mult),
                     reads=["bb", "FL"], writes=["bb"])
                if pas == 2:
                    S.op("dve", lambda h, ct=ct: h.tensor_copy(out=bb[:, 2:3], in_=self.H0L[:, ct:ct + 1]),
                         reads=["bb", "H0L"], writes=["bb"])
                S.op("dve", lambda h: h.tensor_tensor_scan(out=hh, data0=aa[:, :], data1=bb[:, :], initial=0.0,
                                                           op0=ALU.mult, op1=ALU.add),
                     reads=["aa", "bb", "xc"], writes=["xap"])
                if pas == 1:
                    S.op("dve", lambda h, ct=ct: h.tensor_copy(out=GATH[:, 16 + ct:17 + ct], in_=xap[:, 3 + EXP_POS:4 + EXP_POS]),
                         reads=["xap"], writes=["GATH"])
                    S.op("dve", lambda h: h.tensor_reduce(out=sm[:, 0:1], in_=rr[:, 3:EXP_POS + 1], axis=AX.X, op=ALU.add),
                         reads=["rr"], writes=["sm"])
                    S.op("act", lambda h, ct=ct: h.activation(out=GATH[:, ct:ct + 1], in_=sm[:, 0:1], func=AF.Exp,
                                                              scale=self.C1[:, ct:ct + 1]),
                         reads=["sm", "C1"], writes=["GATH"])
                    if ct == 0:
                        self.dbg(S, "lru_h0", hh, [128, NT], reads=["xap"])
                        self.dbg(S, "lru_a0", aa[:, :], [128, NT], reads=["aa"])
                        self.dbg(S, "lru_b0", bb[:, :], [128, NT], reads=["bb"])
                        self.dbg(S, "lru_xc0", xc[:, :], [128, NT], reads=["xc"])
                else:
                    slot = self.wslot()
                    wv2 = WA[:, slot, :].rearrange("p (k c) -> p k c", c=128)
                    self.wload(S, wv2, w_in[:, 2048 + ct * 128:2048 + (ct + 1) * 128].rearrange("(k p) c -> p k c", p=128), slot)
                    pg = self.big()
                    self.fm_matmul(S, pg, wv2, KT, lambda kt, b: BIG1[:, kt, b * CW:(b + 1) * CW], hk, slot)
                    S.op("act", lambda h, pg=pg: h.activation(out=v3(gg[:, :]), in_=PS[:, pg:pg + 3, 0:CW], func=AF.Gelu),
                         reads=[("ps", pg + b) for b in range(3)], writes=["rr"])
                    S.op("dve", lambda h, ct=ct: h.tensor_tensor(out=self.ya[:, ct, :], in0=hh, in1=gg[:, :], op=ALU.mult),
                         reads=["xap", "rr"], writes=[("ya", ct)])
            if pas == 1:
                self.dbg(S, "gath_lru", GATH[:, 0:32], [128, 32], reads=["GATH"])
            else:
                self.dbg(S, "ya", self.ya[:, :, :], [128, 16, NT], BF16, reads=[("ya", ct) for ct in range(16)])

    def s5_alloc(self):
        self.CT = self.T("CT", [64, 2, 128, 16])
        self.BTb = self.T("BTb", [128, 16, 128], BF16)
        self.LD1 = self.T("LD1", [64, 2, 128])
        self.LD2 = self.T("LD2", [64, 2, 128])
        self.PM = self.T("PM", [128, 8])
        self.DCOL = self.T("DCOL", [128, 16])
        self.QRE = self.T("QRE", [64, 128])
        self.QIM = self.T("QIM", [64, 128])
        self.H0S = self.T("H0S", [64, 2, 128])
        self.LBAR = self.T("LBAR", [64, 2, 128])

    def p_s5_prep(self):
        PS, ident = self.PS, self.ident
        with self.phase() as S:
            T = self.T
            lr_n = T("lr_n", [128, 64]); li_n = T("li_n", [128, 64])
            S.dma("sp", lambda h: h.dma_start(out=lr_n[:, :], in_=self.ssm_lam_re[0]), writes=["lr_n"])
            S.dma("sp", lambda h: h.dma_start(out=li_n[:, :], in_=self.ssm_lam_im[0]), writes=["li_n"])
            LRE = T("LRE", [64, 128]); LIM = T("LIM", [64, 128])
            for src, dst, k in ((lr_n, LRE, "LRE"), (li_n, LIM, "LIM")):
                S.op("pe", lambda h, src=src: h.transpose(out=PS[0:64, 7, 0:128], in_=src[:, :], identity=ident[:, :]),
                     reads=[src is lr_n and "lr_n" or "li_n", "ident"], writes=[("ps", 7)])
                S.op("dve", lambda h, dst=dst: h.tensor_copy(out=dst[:, :], in_=PS[0:64, 7, 0:128]), reads=[("ps", 7)], writes=[k])
            STEP = T("STEP", [64, 128])
            S.dma("sp", lambda h: h.dma_start(out=STEP[:, :], in_=self.ssm_log_step.partition_broadcast(64)), writes=["STEP"])
            S.op("act", lambda h: h.activation(out=STEP[:, :], in_=STEP[:, :], func=AF.Exp), reads=["STEP"], writes=["STEP"])
            A_ = T("A_", [64, 128]); TH = T("TH", [64, 128]); MAG = T("MAG", [64, 128])
            S.op("dve", lambda h: h.tensor_tensor(out=A_[:, :], in0=LRE[:, :], in1=STEP[:, :], op=ALU.mult), reads=["LRE", "STEP"], writes=["A_"])
            S.op("dve", lambda h: h.tensor_tensor(out=TH[:, :], in0=LIM[:, :], in1=STEP[:, :], op=ALU.mult), reads=["LIM", "STEP"], writes=["TH"])
            S.op("act", lambda h: h.activation(out=MAG[:, :], in_=A_[:, :], func=AF.Exp), reads=["A_"], writes=["MAG"])
            hp = T("hpi", [64, 1])
            S.op("pool", lambda h: h.memset(hp[:, :], math.pi / 2), writes=["hpi"])
            sn = [T("sn0", [64, 128]), T("sn1", [64, 128])]
            cs = [T("cs0", [64, 128]), T("cs1", [64, 128])]
            tq = T("tq", [64, 128])
            S.op("act", lambda h: h.activation(out=sn[0][:, :], in_=TH[:, :], func=AF.Sin, scale=1.0 / 32), reads=["TH"], writes=["sn0"])
            S.op("act", lambda h: h.activation(out=cs[0][:, :], in_=TH[:, :], func=AF.Sin, scale=1.0 / 32, bias=hp[:, :]),
                 reads=["TH", "hpi"], writes=["cs0"])
            cur = 0
            for it in range(5):
                nx = 1 - cur
                S.op("dve", lambda h, cur=cur: h.tensor_tensor(out=tq[:, :], in0=sn[cur][:, :], in1=sn[cur][:, :], op=ALU.mult),
                     reads=[f"sn{cur}"], writes=["tq"])
                S.op("dve", lambda h, nx=nx: h.tensor_scalar(out=cs[nx][:, :], in0=tq[:, :], scalar1=-2.0, scalar2=1.0,
                                                             op0=ALU.mult, op1=ALU.add),
                     reads=["tq"], writes=[f"cs{nx}"])
                S.op("dve", lambda h, cur=cur, nx=nx: h.scalar_tensor_tensor(out=sn[nx][:, :], in0=sn[cur][:, :], scalar=2.0,
                                                                             in1=cs[cur][:, :], op0=ALU.mult, op1=ALU.mult),
                     reads=[f"sn{cur}", f"cs{cur}"], writes=[f"sn{nx}"])
                cur = nx
            LB = self.LBAR
            S.op("dve", lambda h, cur=cur: h.tensor_tensor(out=LB[:, 0, :], in0=MAG[:, :], in1=cs[cur][:, :], op=ALU.mult),
                 reads=["MAG", f"cs{cur}"], writes=["LB"])
            S.op("dve", lambda h, cur=cur: h.tensor_tensor(out=LB[:, 1, :], in0=MAG[:, :], in1=sn[cur][:, :], op=ALU.mult),
                 reads=["MAG", f"sn{cur}"], writes=["LB"])
            LD1, LD2 = self.LD1, self.LD2
            S.op("dve", lambda h: h.tensor_copy(out=LD1[:, 0, :], in_=LB[:, 0, :]), reads=["LB"], writes=["LD1"])
            S.op("dve", lambda h: h.tensor_copy(out=LD1[:, 1, :], in_=LB[:, 0, :]), reads=["LB"], writes=["LD1"])
            S.op("dve", lambda h: h.tensor_scalar(out=LD2[:, 0, :], in0=LB[:, 1, :], scalar1=-1.0, scalar2=None, op0=ALU.mult),
                 reads=["LB"], writes=["LD2"])
            S.op("dve", lambda h: h.tensor_copy(out=LD2[:, 1, :], in_=LB[:, 1, :]), reads=["LB"], writes=["LD2"])
            den = T("den", [64, 128]); t2 = T("t2", [64, 128]); xr = T("xr", [64, 128])
            qre, qim = self.QRE, self.QIM
            S.op("dve", lambda h: h.tensor_tensor(out=den[:, :], in0=LRE[:, :], in1=LRE[:, :], op=ALU.mult), reads=["LRE"], writes=["den"])
            S.op("dve", lambda h: h.tensor_tensor(out=t2[:, :], in0=LIM[:, :], in1=LIM[:, :], op=ALU.mult), reads=["LIM"], writes=["t2"])
            S.op("dve", lambda h: h.tensor_tensor(out=den[:, :], in0=den[:, :], in1=t2[:, :], op=ALU.add), reads=["den", "t2"], writes=["den"])
            S.op("dve", lambda h: h.reciprocal(out=den[:, :], in_=den[:, :]), reads=["den"], writes=["den"])
            S.op("dve", lambda h: h.tensor_scalar(out=xr[:, :], in0=LB[:, 0, :], scalar1=-1.0, scalar2=None, op0=ALU.add), reads=["LB"], writes=["xr"])
            S.op("dve", lambda h: h.tensor_tensor(out=qre[:, :], in0=xr[:, :], in1=LRE[:, :], op=ALU.mult), reads=["xr", "LRE"], writes=["qre"])
            S.op("dve", lambda h: h.tensor_tensor(out=t2[:, :], in0=LB[:, 1, :], in1=LIM[:, :], op=ALU.mult), reads=["LB", "LIM", "den"], writes=["t2"])
            S.op("dve", lambda h: h.tensor_tensor(out=qre[:, :], in0=qre[:, :], in1=t2[:, :], op=ALU.add), reads=["qre", "t2"], writes=["qre"])
            S.op("dve", lambda h: h.tensor_tensor(out=qre[:, :], in0=qre[:, :], in1=den[:, :], op=ALU.mult), reads=["qre", "den"], writes=["qre"])
            S.op("dve", lambda h: h.tensor_tensor(out=qim[:, :], in0=LB[:, 1, :], in1=LRE[:, :], op=ALU.mult), reads=["LB", "LRE"], writes=["qim"])
            S.op("dve", lambda h: h.tensor_tensor(out=t2[:, :], in0=xr[:, :], in1=LIM[:, :], op=ALU.mult), reads=["xr", "LIM", "qre"], writes=["t2"])
            S.op("dve", lambda h: h.tensor_tensor(out=qim[:, :], in0=qim[:, :], in1=t2[:, :], op=ALU.subtract), reads=["qim", "t2"], writes=["qim"])
            S.op("dve", lambda h: h.tensor_tensor(out=qim[:, :], in0=qim[:, :], in1=den[:, :], op=ALU.mult), reads=["qim", "den"], writes=["qim"])
            self.dbg(S, "lbar", LB[:, :, :], [64, 2, 128], reads=["LB"])
            self.dbg(S, "qre", qre[:, :], [64, 128], reads=["qre"])
            self.dbg(S, "qim", qim[:, :], [64, 128], reads=["qim"])
        for hb in range(2):
          with self.phase() as S:
            T = self.T
            qre, qim = self.QRE, self.QIM
            g0 = hb * 64
            bre = T("bre", [64, 64, 16]); bim = T("bim", [64, 64, 16])
            bnat = [T("bnat0", [128, 1024]), T("bnat1", [128, 1024])]
            for a, (srcd, dst, k) in enumerate(((self.ssm_b_re, bre, "bre"), (self.ssm_b_im, bim, "bim"))):
                S.dma("sp" if a == 0 else "act", lambda h, a=a, srcd=srcd: h.dma_start(out=bnat[a][:, :], in_=srcd[0].rearrange("g p n -> g (p n)")),
                      writes=[f"bnat{a}"])
                for n in range(16):
                    bank = 6 + (n % 2)
                    S.op("pe", lambda h, a=a, n=n, bank=bank: h.transpose(
                        out=PS[0:64, bank, 0:64], in_=bnat[a][g0:g0 + 64, :].rearrange("g (p n) -> g p n", n=16)[:, :, n],
                        identity=ident[g0:g0 + 64, g0:g0 + 64]),
                         reads=[f"bnat{a}", "ident"], writes=[("ps", bank)])
                    S.op("act", lambda h, dst=dst, n=n, bank=bank: h.activation(out=dst[:, :, n], in_=PS[0:64, bank, 0:64], func=AF.Identity),
                         reads=[("ps", bank)], writes=[k])
            BB = T("BB", [64, 2, 64, 16]); tb = T("tbb", [64, 64, 16])
            qb = lambda q: q[:, g0:g0 + 64, None].broadcast_to([64, 64, 16])
            S.op("dve", lambda h: h.tensor_tensor(out=BB[:, 0, :, :], in0=bre[:, :, :], in1=qb(qre), op=ALU.mult), reads=["bre"], writes=["BB0"])
            S.op("dve", lambda h: h.tensor_tensor(out=tb[:, :, :], in0=bim[:, :, :], in1=qb(qim), op=ALU.mult), reads=["bim"], writes=["tbb"])
            S.op("dve", lambda h: h.tensor_tensor(out=BB[:, 0, :, :], in0=BB[:, 0, :, :], in1=tb[:, :, :], op=ALU.subtract), reads=["BB0", "tbb"], writes=["BB0"])
            S.op("dve", lambda h: h.tensor_tensor(out=BB[:, 1, :, :], in0=bim[:, :, :], in1=qb(qre), op=ALU.mult), reads=["bim"], writes=["BB1"])
            S.op("dve", lambda h: h.tensor_tensor(out=tb[:, :, :], in0=bre[:, :, :], in1=qb(qim), op=ALU.mult), reads=["bre", "BB0"], writes=["tbb"])
            S.op("dve", lambda h: h.tensor_tensor(out=BB[:, 1, :, :], in0=BB[:, 1, :, :], in1=tb[:, :, :], op=ALU.add), reads=["BB1", "tbb"], writes=["BB1"])
            for gbl in range(8):
                gb = hb * 8 + gbl
                for ri in range(2):
                    bank = 6 + (ri % 2)
                    S.op("pe", lambda h, gbl=gbl, ri=ri, bank=bank: h.transpose(
                        out=PS[:, bank, 0:64], in_=BB[:, ri, gbl * 8:(gbl + 1) * 8, :].rearrange("p g n -> p (g n)"),
                        identity=ident[0:64, 0:64]),
                         reads=[f"BB{ri}", "ident"], writes=[("ps", bank)])
                    S.op("act", lambda h, gb=gb, ri=ri, bank=bank: h.activation(out=self.BTb[:, gb, ri * 64:(ri + 1) * 64],
                                                                                 in_=PS[:, bank, 0:64], func=AF.Identity),
                         reads=[("ps", bank)], writes=["BTb"])
            self.dbg(S, f"BB{hb}", BB[:, :, :, :], [64, 2, 64, 16], reads=["BB0", "BB1"])
            self.dbg(S, f"bre{hb}", bre[:, :, :], [64, 64, 16], reads=["bre"])
        with self.phase() as S:
            T = self.T
            cnat = [T("cnat0", [128, 1024]), T("cnat1", [128, 1024])]
            for ri, srcd in enumerate((self.ssm_c_re, self.ssm_c_im)):
                S.dma("sp" if ri == 0 else "act", lambda h, ri=ri, srcd=srcd: h.dma_start(out=cnat[ri][:, :], in_=srcd[0].rearrange("g n p -> g (n p)")),
                      writes=[f"cnat{ri}"])
                for n in range(16):
                    bank = 6 + (n % 2)
                    S.op("pe", lambda h, ri=ri, n=n, bank=bank: h.transpose(out=PS[0:64, bank, 0:128], in_=cnat[ri][:, n * 64:(n + 1) * 64],
                                                                            identity=ident[:, :]),
                         reads=[f"cnat{ri}", "ident"], writes=[("ps", bank)])
                    S.op("act", lambda h, ri=ri, n=n, bank=bank: h.activation(out=self.CT[:, ri, :, n], in_=PS[0:64, bank, 0:128],
                                                                              func=AF.Identity, scale=(1.0 if ri == 0 else -1.0)),
                         reads=[("ps", bank)], writes=["CT"])
            S.op("dve", lambda h: h.tensor_reduce(out=self.PM[:, :], in_=ident[:, :].rearrange("p (a b) -> p a b", b=16), axis=AX.X, op=ALU.add),
                 reads=["ident"], writes=["PM"])
            self.load_cols(S, self.DCOL, self.ssm_d[0].rearrange("(c p) -> c p", p=128), 16, "DCOL", "tsd")
            self.dbg(S, "CT", self.CT[:, :, :, :], [64, 2, 128, 16], reads=["CT"])

    def p_xs(self):
        PS, BIG1, WA, U = self.PS, self.BIG1, self.WA, self.U
        w_in = self.w_in[0]
        hk = [("big1", ft) for ft in range(KT)]
        with self.phase() as S:
            for ct in range(16):
                slot = self.wslot()
                wv = WA[:, slot, :].rearrange("p (k c) -> p k c", c=128)
                self.wload(S, wv, w_in[:, 4096 + ct * 128:4096 + (ct + 1) * 128].rearrange("(k p) c -> p k c", p=128), slot)
                pb = self.big()
                self.fm_matmul(S, pb, wv, KT, lambda kt, b: BIG1[:, kt, b * CW:(b + 1) * CW], hk, slot)
                pk = [("ps", pb + b) for b in range(3)]
                if ct % 2 == 0:
                    S.op("act", lambda h, pb=pb, ct=ct: h.activation(out=v3(U[:, ct, :]), in_=PS[:, pb:pb + 3, 0:CW], func=AF.Identity),
                         reads=pk, writes=[("U", ct)])
                else:
                    S.op("dve", lambda h, pb=pb, ct=ct: h.tensor_copy(out=v3(U[:, ct, :]), in_=PS[:, pb:pb + 3, 0:CW]),
                         reads=pk, writes=[("U", ct)])
            self.dbg(S, "U", U[:, :, :], [128, 16, NT], BF16, reads=[("U", ct) for ct in range(16)])

    def p_s5(self, pas):
        PS, ident, U, CT, BTb, PM = self.PS, self.ident, self.U, self.CT, self.BTb, self.PM
        LD1, LD2 = self.LD1, self.LD2
        with self.phase() as S:
            hist = self.T("hist", [64, 2, 128, CB + 1])
            T1 = self.T("T1", [64, 2, 128]); T2 = self.T("T2", [64, 2, 128])
            Um = [self.T(f"Um{i}", [128, 8, CB], BF16) for i in range(4)]
            ytm = self.T("ytm", [CB, WL])
            ysb = self.T("ysb", [128, 16, CB])
            yv = self.T("yv", [128, 16, CB])
            uk = [("U", ct) for ct in range(16)]
            if pas == 1:
                S.op("dve", lambda h: h.memset(hist[:, :, :, 0], 0.0), writes=[("hs", 0)])
            else:
                S.op("dve", lambda h: h.tensor_copy(out=hist[:, :, :, 0], in_=self.H0S[:, :, :]), reads=["H0S"], writes=[("hs", 0)])
            for blk in range(NBK):
                t0 = blk * CB
                slots = [("hs", t + 1) for t in range(CB)]
                for gb in range(16):
                    um = Um[gb % 4]
                    uk1 = ("Um", gb % 4)
                    S.op("pool", lambda h, um=um, gb=gb, t0=t0: h.tensor_tensor(
                        out=um[:, :, :], in0=U[:, gb, None, t0:t0 + CB].broadcast_to([128, 8, CB]),
                        in1=PM[:, :, None].broadcast_to([128, 8, CB]), op=ALU.mult),
                         reads=[("U", gb), "PM"], writes=[uk1])
                    bank = gb % 6
                    for ri in range(2):
                        for gl in range(8):
                            col = (ri * 8 + gl) * CB
                            S.op("pe", lambda h, um=um, gb=gb, ri=ri, gl=gl, col=col, bank=bank: h.matmul(
                                PS[0:64, bank, col:col + CB], BTb[:, gb, ri * 64:(ri + 1) * 64], um[:, gl, :], start=True, stop=True),
                                 reads=[uk1, "BTb"], writes=[("ps", bank)])
                    S.op("act", lambda h, gb=gb, bank=bank: h.activation(
                        out=hist[:, :, gb * 8:(gb + 1) * 8, 1:CB + 1],
                        in_=PS[0:64, bank, 0:16 * CB].rearrange("p (r g t) -> p r g t", r=2, g=8), func=AF.Identity),
                         reads=[("ps", bank)], writes=slots)
                for t in range(CB):
                    kp, kc = ("hs", t), ("hs", t + 1)
                    prev = hist[:, :, :, t]
                    curv = hist[:, :, :, t + 1]
                    S.op("dve", lambda h, prev=prev: h.tensor_tensor(out=T1[:, :, :], in0=prev, in1=LD1[:, :, :], op=ALU.mult),
                         reads=[kp, "LD1"], writes=["T1"])
                    S.op("dve", lambda h, t=t: h.tensor_tensor(out=T2[:, 0, :], in0=hist[:, 1, :, t], in1=LD2[:, 0, :], op=ALU.mult),
                         reads=[kp, "LD2"], writes=["T2a"])
                    S.op("dve", lambda h, t=t: h.tensor_tensor(out=T2[:, 1, :], in0=hist[:, 0, :, t], in1=LD2[:, 1, :], op=ALU.mult),
                         reads=[kp, "LD2"], writes=["T2b"])
                    S.op("dve", lambda h: h.tensor_tensor(out=T1[:, :, :], in0=T1[:, :, :], in1=T2[:, :, :], op=ALU.add),
                         reads=["T1", "T2a", "T2b"], writes=["T1"])
                    S.op("dve", lambda h, curv=curv: h.tensor_tensor(out=curv, in0=curv, in1=T1[:, :, :], op=ALU.add),
                         reads=[kc, "T1"], writes=[kc])
                if pas == 1:
                    if blk == NBK - 1:
                        sl = S5_EXP - t0 + 1
                        S.op("dve", lambda h, sl=sl: h.tensor_copy(out=self.GATH[0:64, 32:288].rearrange("p (r g) -> p r g", r=2),
                                                                   in_=hist[:, :, :, sl]),
                             reads=[("hs", sl)], writes=["GATH"])
                else:
                    for g in range(128):
                        bank = g // 32
                        for ri in range(2):
                            S.op("pe", lambda h, g=g, ri=ri, bank=bank: h.matmul(
                                PS[0:CB, bank, (g % 32) * 16:(g % 32) * 16 + 16], hist[:, ri, g, 1:CB + 1], CT[:, ri, g, :],
                                start=(ri == 0), stop=(ri == 1)),
                                 reads=slots + ["CT"], writes=[("ps", bank)])
                    S.op("act", lambda h: h.activation(out=ytm[:, :].rearrange("p (b c) -> p b c", c=512), in_=PS[0:CB, 0:4, :],
                                                       func=AF.Identity),
                         reads=[("ps", b) for b in range(4)], writes=["ytm"])
                    for ct in range(16):
                        S.op("pe", lambda h, ct=ct: h.transpose(out=PS[:, 4, ct * CB:(ct + 1) * CB], in_=ytm[0:CB, ct * 128:(ct + 1) * 128],
                                                                identity=ident[0:CB, 0:CB]),
                             reads=["ytm", "ident"], writes=[("ps", 4)])
                    S.op("act", lambda h: h.activation(out=ysb[:, :, :], in_=PS[:, 4, 0:16 * CB].rearrange("p (c t) -> p c t", t=CB),
                                                       func=AF.Identity),
                         reads=[("ps", 4)], writes=["ysb"])
                    S.op("pool", lambda h, t0=t0: h.tensor_tensor(out=yv[:, :, :], in0=U[:, :, t0:t0 + CB],
                                                                   in1=self.DCOL[:, :, None].broadcast_to([128, 16, CB]), op=ALU.mult),
                         reads=uk + ["DCOL"], writes=["yv"])
                    S.op("pool", lambda h: h.tensor_tensor(out=yv[:, :, :], in0=yv[:, :, :], in1=ysb[:, :, :], op=ALU.add),
                         reads=["yv", "ysb"], writes=["yv"])
                    S.op("act", lambda h, t0=t0: h.activation(out=U[:, :, t0:t0 + CB], in_=yv[:, :, :], func=AF.Gelu),
                         reads=["yv"], writes=uk)
                if blk < NBK - 1:
                    S.op("dve", lambda h: h.tensor_copy(out=hist[:, :, :, 0], in_=hist[:, :, :, CB]),
                         reads=[("hs", CB)], writes=[("hs", 0)])
            if pas == 1:
                self.dbg(S, "gath_s5", self.GATH[0:64, 32:288], [64, 256], reads=["GATH"])
            else:
                self.dbg(S, "yg", U[:, :, :], [128, 16, NT], BF16, reads=uk)

    def p_gather(self):
        nc = self.nc
        GATH, FL = self.GATH, self.FL
        with self.phase() as S:
            NCORE = self.ncore
            G8 = self.T("G8", [128, NCORE, 288])
            S.dma("pool", lambda h: h.dma_start(out=self.ag_in.ap(), in_=GATH[:, :]), writes=["ag_in"])
            S.op("pool", lambda h: h.collective_compute("AllGather", ALU.bypass, replica_groups=[list(range(NCORE))],
                                                        ins=[self.ag_in.ap().opt()], outs=[self.ag_out.ap().opt()]),
                 reads=["ag_in"], writes=["ag_out"])
            S.dma("pool", lambda h: h.dma_start(out=G8[:, :, :], in_=self.ag_out.ap().rearrange("(r p) c -> p r c", p=128)),
                  reads=["ag_out"], writes=["G8"])
            H0L = self.H0L
            tl = self.T("tl", [128, 16])
            S.op("dve", lambda h: h.memset(H0L[:, :], 0.0), writes=["H0L"])
            for r in range(NCORE):
                S.op("dve", lambda h, r=r: h.tensor_tensor(out=tl[:, :], in0=G8[:, r, 0:16], in1=H0L[:, :], op=ALU.mult),
                     reads=["G8", "H0L"], writes=["tl"])
                S.op("dve", lambda h, r=r: h.tensor_tensor(out=tl[:, :], in0=tl[:, :], in1=G8[:, r, 16:32], op=ALU.add),
                     reads=["tl", "G8"], writes=["tl"])
                S.op("dve", lambda h: h.tensor_tensor(out=tl[:, :], in0=tl[:, :], in1=H0L[:, :], op=ALU.subtract),
                     reads=["tl", "H0L"], writes=["tl"])
                S.op("dve", lambda h, r=r: h.scalar_tensor_tensor(out=H0L[:, :], in0=tl[:, :], scalar=FL[:, 9 + r:10 + r], in1=H0L[:, :],
                                                                  op0=ALU.mult, op1=ALU.add),
                     reads=["tl", "H0L", "FL"], writes=["H0L"])
            AP_ = [self.T("AP0", [64, 2, 128]), self.T("AP1", [64, 2, 128])]
            tq = self.T("tqs", [64, 128]); tq2 = self.T("tqs2", [64, 128])
            S.op("dve", lambda h: h.tensor_copy(out=AP_[0][:, :, :], in_=self.LBAR[:, :, :]), writes=["AP0"])
            cur = 0
            for it in range(10):
                nx = 1 - cur
                a, b = AP_[cur], AP_[nx]
                S.op("dve", lambda h, a=a: h.tensor_tensor(out=tq[:, :], in0=a[:, 0, :], in1=a[:, 0, :], op=ALU.mult), reads=[f"AP{cur}"], writes=["tq"])
                S.op("dve", lambda h, a=a: h.tensor_tensor(out=tq2[:, :], in0=a[:, 1, :], in1=a[:, 1, :], op=ALU.mult), reads=[f"AP{cur}"], writes=["tq2"])
                S.op("dve", lambda h, b=b: h.tensor_tensor(out=b[:, 0, :], in0=tq[:, :], in1=tq2[:, :], op=ALU.subtract), reads=["tq", "tq2"], writes=[f"AP{nx}"])
                S.op("dve", lambda h, a=a, b=b: h.scalar_tensor_tensor(out=b[:, 1, :], in0=a[:, 0, :], scalar=2.0, in1=a[:, 1, :],
                                                                       op0=ALU.mult, op1=ALU.mult), reads=[f"AP{cur}"], writes=[f"AP{nx}"])
                cur = nx
            A = AP_[cur]
            ak = f"AP{cur}"
            H0S = self.H0S
            ts = self.T("ts5", [64, 2, 128]); tu = self.T("tu5", [64, 128])
            S.op("dve", lambda h: h.memset(H0S[:, :, :], 0.0), writes=["H0S"])
            for r in range(NCORE):
                E = G8[0:64, r, 32:288].rearrange("p (r g) -> p r g", r=2)
                S.op("dve", lambda h: h.tensor_tensor(out=ts[:, 0, :], in0=A[:, 0, :], in1=H0S[:, 0, :], op=ALU.mult), reads=[ak, "H0S"], writes=["ts0"])
                S.op("dve", lambda h: h.tensor_tensor(out=tu[:, :], in0=A[:, 1, :], in1=H0S[:, 1, :], op=ALU.mult), reads=[ak, "H0S"], writes=["tu"])
                S.op("dve", lambda h: h.tensor_tensor(out=ts[:, 0, :], in0=ts[:, 0, :], in1=tu[:, :], op=ALU.subtract), reads=["ts0", "tu"], writes=["ts0"])
                S.op("dve", lambda h: h.tensor_tensor(out=ts[:, 1, :], in0=A[:, 0, :], in1=H0S[:, 1, :], op=ALU.mult), reads=[ak, "H0S"], writes=["ts1"])
                S.op("dve", lambda h: h.tensor_tensor(out=tu[:, :], in0=A[:, 1, :], in1=H0S[:, 0, :], op=ALU.mult), reads=[ak, "H0S", "ts0"], writes=["tu"])
                S.op("dve", lambda h: h.tensor_tensor(out=ts[:, 1, :], in0=ts[:, 1, :], in1=tu[:, :], op=ALU.add), reads=["ts1", "tu"], writes=["ts1"])
                S.op("dve", lambda h, E=E: h.tensor_tensor(out=ts[:, :, :], in0=ts[:, :, :], in1=E, op=ALU.add), reads=["ts0", "ts1", "G8"], writes=["ts0", "ts1"])
                S.op("dve", lambda h: h.tensor_tensor(out=ts[:, :, :], in0=ts[:, :, :], in1=H0S[:, :, :], op=ALU.subtract), reads=["ts0", "ts1", "H0S"], writes=["ts0", "ts1"])
                S.op("dve", lambda h, r=r: h.scalar_tensor_tensor(out=H0S[:, :, :], in0=ts[:, :, :], scalar=FL[0:64, 9 + r:10 + r], in1=H0S[:, :, :],
                                                                  op0=ALU.mult, op1=ALU.add), reads=["ts0", "ts1", "H0S", "FL"], writes=["H0S"])
            self.dbg(S, "h0l", H0L[:, :], [128, 16], reads=["H0L"])
            self.dbg(S, "h0s", H0S[:, :, :], [64, 2, 128], reads=["H0S"])

    def p_glu(self):
        PS, WA, U = self.PS, self.WA, self.U
        ys_d = self.nc.dram_tensor("ys_d", [16, 128, NT], BF16).ap()
        with self.phase() as S:
            yst = [self.T("yst0", [128, NT], BF16), self.T("yst1", [128, NT], BF16)]
            BG = self.T("BG", [128, 16])
            self.load_cols(S, BG, self.b_glu[0].rearrange("(c p) -> c p", p=128), 16, "BG", "tbg")
            sg = [self.T("sg0", [128, NT]), self.T("sg1", [128, NT])]
            uk = [("U", ct) for ct in range(16)]
            for ot in range(16):
                slot = self.wslot()
                wv = WA[:, slot, :].rearrange("p (k c) -> p k c", c=128)
                self.wload(S, wv[:, 0:16, :], self.w_glu[0][:, ot * 128:(ot + 1) * 128].rearrange("(k p) c -> p k c", p=128), slot)
                pb = self.big()
                self.fm_matmul(S, pb, wv, 16, lambda kt, b: U[:, kt, b * CW:(b + 1) * CW], uk, slot)
                s = sg[ot % 2]
                S.op("act", lambda h, pb=pb, ot=ot, s=s: h.activation(out=v3(s[:, :]), in_=PS[:, pb:pb + 3, 0:CW], func=AF.Sigmoid,
                                                                      bias=BG[:, ot:ot + 1]),
                     reads=[("ps", pb + b) for b in range(3)] + ["BG"], writes=[("sg", ot % 2)])
                yt = yst[ot % 2]
                S.op("dve", lambda h, ot=ot, s=s, yt=yt: h.tensor_tensor(out=yt[:, :], in0=s[:, :], in1=U[:, ot, :], op=ALU.mult),
                     reads=[("sg", ot % 2), ("U", ot)], writes=[("yst", ot % 2)])
                S.dma("sp", lambda h, ot=ot, yt=yt: h.dma_start(out=ys_d[ot], in_=yt[:, :]), reads=[("yst", ot % 2)], writes=["ys_d"])
        with self.phase() as S:
            S.dma("sp", lambda h: h.dma_start(out=U[:, :, :], in_=ys_d.rearrange("k p t -> p k t")), writes=[("ys", ct) for ct in range(16)])
            self.dbg(S, "ys", U[:, :, :], [128, 16, NT], BF16, reads=[("ys", ct) for ct in range(16)])

    def p_merge(self):
        PS, WA, BIG1, ya, ys = self.PS, self.WA, self.BIG1, self.ya, self.ys
        w_in = self.w_in[0]
        hk = [("big1", ft) for ft in range(KT)]
        with self.phase() as S:
            sa = self.T("sa", [128, NT]); sb_ = self.T("sb", [128, NT])
            m1 = self.T("m1", [128, NT]); m2 = self.T("m2", [128, NT])
            mg = [self.T("mg0", [128, NT], BF16), self.T("mg1", [128, NT], BF16)]
            yak = [("ya", ct) for ct in range(16)]
            ysk = [("ys", ct) for ct in range(16)]
            for mt in range(KT):
                def gate(col0, dst, key):
                    slot = self.wslot()
                    wv = WA[:, slot, :].rearrange("p (k c) -> p k c", c=128)
                    self.wload(S, wv, w_in[:, col0 + mt * 128:col0 + (mt + 1) * 128].rearrange("(k p) c -> p k c", p=128), slot)
                    pb = self.big()
                    self.fm_matmul(S, pb, wv, KT, lambda kt, b: BIG1[:, kt, b * CW:(b + 1) * CW], hk, slot)
                    S.op("act", lambda h, pb=pb: h.activation(out=v3(dst[:, :]), in_=PS[:, pb:pb + 3, 0:CW], func=AF.Sigmoid),
                         reads=[("ps", pb + b) for b in range(3)], writes=[key])

                def proj(wd, src, skeys, gt, gkey, dst, dkey):
                    slot = self.wslot()
                    wv = WA[:, slot, :].rearrange("p (k c) -> p k c", c=128)
                    self.wload(S, wv[:, 0:16, :], wd[:, mt * 128:(mt + 1) * 128].rearrange("(k p) c -> p k c", p=128), slot)
                    pb = self.big()
                    self.fm_matmul(S, pb, wv, 16, lambda kt, b: src[:, kt, b * CW:(b + 1) * CW], skeys, slot)
                    S.op("dve", lambda h, pb=pb: h.tensor_tensor(out=v3(dst[:, :]), in0=PS[:, pb:pb + 3, 0:CW], in1=v3(gt[:, :]), op=ALU.mult),
                         reads=[("ps", pb + b) for b in range(3)] + [gkey], writes=[dkey])

                gate(6144, sa, "sa")
                proj(self.w_proj_lru[0], ya, yak, sa, "sa", m1, "m1")
                gate(10240, sb_, "sb")
                proj(self.w_proj_ssm[0], ys, ysk, sb_, "sb", m2, "m2")
                mo = mg[mt % 2]
                S.op("pool", lambda h, mo=mo: h.tensor_tensor(out=mo[:, :], in0=m1[:, :], in1=m2[:, :], op=ALU.add),
                     reads=["m1", "m2"], writes=[("mg", mt % 2)])
                S.dma("sp", lambda h, mo=mo, mt=mt: h.dma_start(out=self.mg_d[mt], in_=mo[:, :]), reads=[("mg", mt % 2)], writes=["mg_d"])

    def p_wout(self):
        PS, WA, BIG1 = self.PS, self.WA, self.BIG1
        w_out = self.w_out[0]
        with self.phase() as S:
            S.dma("sp", lambda h: h.dma_start(out=BIG1[:, :, :], in_=self.mg_d.rearrange("k p t -> p k t")), writes=["mgall"])
            GB = self.T("GB", [128, D])
            self.bcast_cond(S, GB, 64)
            xr = [self.T(f"xres{i}", [128, 512]) for i in range(3)]
            yo = [self.T(f"yo{i}", [128, 512]) for i in range(3)]
            tiles = [(6, 2)] + [(HALO + 128 * i, 128) for i in range(8)]
            cnt = 0
            for batch in (tiles[0:5], tiles[5:9]):
                for nch in range(8):
                    for kg in range(4):
                        slot = self.wslot()
                        wv = WA[:, slot, :].rearrange("p (k c) -> p k c", c=512)
                        self.wload(S, wv, w_out[kg * 1024:(kg + 1) * 1024, nch * 512:(nch + 1) * 512].rearrange("(k p) c -> p k c", p=128), slot)
                        for ti, (r0, M) in enumerate(batch):
                            for kt in range(8):
                                k = kg * 8 + kt
                                S.op("pe", lambda h, ti=ti, r0=r0, M=M, k=k, kt=kt, wv=wv, kg=kg: h.matmul(
                                    PS[0:M, ti, :], BIG1[:, k, r0:r0 + M], wv[:, kt, :], start=(k == 0), stop=(k == KT - 1)),
                                     reads=["mgall", ("wa", slot)], writes=[("ps", ti)])
                    for ti, (r0, M) in enumerate(batch):
                        i3 = cnt % 3
                        cnt += 1
                        xt, yt = xr[i3], yo[i3]
                        S.dma("act", lambda h, xt=xt, r0=r0, M=M, nch=nch: h.dma_start(out=xt[0:M, :], in_=self.xh[r0:r0 + M, nch * 512:(nch + 1) * 512]),
                              writes=[("xres", i3)])
                        S.op("dve", lambda h, yt=yt, ti=ti, M=M, nch=nch: h.tensor_tensor(out=yt[0:M, :], in0=PS[0:M, ti, :],
                                                                                          in1=GB[0:M, nch * 512:(nch + 1) * 512], op=ALU.mult),
                             reads=[("ps", ti), "GB"], writes=[("yo", i3)])
                        S.op("dve", lambda h, yt=yt, xt=xt, M=M: h.scalar_tensor_tensor(out=yt[0:M, :], in0=xt[0:M, :], scalar=ALPHA, in1=yt[0:M, :],
                                                                                          op0=ALU.mult, op1=ALU.add),
                             reads=[("yo", i3), ("xres", i3)], writes=[("yo", i3)])
                        S.dma("sp", lambda h, yt=yt, r0=r0, M=M, nch=nch: h.dma_start(out=self.y1_d[r0:r0 + M, nch * 512:(nch + 1) * 512], in_=yt[0:M, :]),
                              reads=[("yo", i3)], writes=["y1_d"])

    def p_ln_mid(self):
        with self.phase() as S:
            self.lnJ = self.T("lnJ", [128, D], BF16)
            self.lnJ2 = self.T("lnJ2", [128, D])
            self.lnst = self.T("lnst", [128, 4])
            G = self.T("lnG", [128, D]); Bt = self.T("lnB", [128, D])
            S.dma("sp", lambda h: h.dma_start(out=G[:, :], in_=self.ln1_g.partition_broadcast(128)), writes=["lnG"])
            S.dma("sp", lambda h: h.dma_start(out=Bt[:, :], in_=self.ln1_b.partition_broadcast(128)), writes=["lnB"])
            XT = [self.T(f"XT{i}", [128, D]) for i in range(2)]
            nev = [0]
            tiles = [(6, 2)] + [(HALO + 128 * i, 128) for i in range(8)]
            for ti, (r0, M) in enumerate(tiles):
                X = XT[ti % 2]
                tag = ti % 2
                S.dma("act", lambda h, X=X, r0=r0, M=M: h.dma_start(out=X[0:M, :], in_=self.y1_d[r0:r0 + M, :]), writes=[("X", tag)])
                self.ln_rows(S, X, M, tag, affine=(G, Bt))
                S.dma("sp", lambda h, X=X, r0=r0, M=M: h.dma_start(out=self.x1_d[r0:r0 + M, :], in_=X[0:M, :]), reads=[("X", tag)], writes=["x1_d"])
                self.ln_rows(S, X, M, tag)
                self.rows_to_T(S, X, M, tag, r0, 128, 96, nev)
            self.dbg(S, "h2T", self.BIG1[:, :, :], [128, KT, NT], BF16, reads=[("big1", ft) for ft in range(KT)])

    def p_ffn(self):
        PS, WA, BIG1, FL = self.PS, self.WA, self.BIG1, self.FL
        w_up = self.ffn_w_up[0]
        w_dn = self.ffn_w_down[0]
        NH = 514
        HC = 257
        hk = [("big1", ft) for ft in range(KT)]
        with self.phase() as S:
            FCW = self.T("FCW", [128, 516])
            FCB = self.T("FCB", [128, 172])
            self.load_cols(S, FCW, self.ffn_conv_w[0].rearrange("k (c p) -> (k c) p", p=128), 516, "FCW", "tfw")
            self.load_cols(S, FCB, self.ffn_conv_b[0].rearrange("(c p) -> c p", p=128), 172, "FCB", "tfb")
            GBc = self.T("GBc", [128, 512])
            Dg = self.T("ffn_diag", [128, 2, 128])
            aT = self.T("aT", [128, NFT, 512], BF16)
            us = [self.T(f"us{i}", [128, NH]) for i in range(2)]
            cg = self.T("cg", [128, 512]); cv = self.T("cv", [128, 512])
            xr = [self.T(f"xres{i}", [128, 512]) for i in range(2)]
            yo = [self.T(f"yo{i}", [128, 512]) for i in range(2)]
            cnt = 0
            for hf in range(2):
                c0 = 6 + 512 * hf
                for j in range(NFT):
                    for which in range(2):
                        tile = which * NFT + j
                        slot = self.wslot()
                        wv = WA[:, slot, :].rearrange("p (k c) -> p k c", c=128)
                        self.wload(S, wv, w_up[:, tile * 128:(tile + 1) * 128].rearrange("(k p) c -> p k c", p=128), slot)
                        pb = 2 * (self.ps_n % 3)
                        self.ps_n += 1
                        for kt in range(KT):
                            for b in range(2):
                                S.op("pe", lambda h, kt=kt, b=b, pb=pb, wv=wv: h.matmul(
                                    PS[:, pb + b, 0:HC], wv[:, kt, :], BIG1[:, kt, c0 + b * HC:c0 + (b + 1) * HC],
                                    start=(kt == 0), stop=(kt == KT - 1)),
                                     reads=[("wa", slot)] + hk, writes=[("ps", pb + b)])
                        u = us[which]
                        uk = ("us", which)
                        S.op("act", lambda h, u=u, pb=pb: h.activation(out=u[:, :].rearrange("p (b c) -> p b c", c=HC), in_=PS[:, pb:pb + 2, 0:HC],
                                                                       func=AF.Identity),
                             reads=[("ps", pb), ("ps", pb + 1)], writes=[uk])
                        if hf == 0:
                            S.op("pool", lambda h, u=u: h.tensor_scalar(out=u[:, 0:2], in0=u[:, 0:2], scalar1=FL[:, 0:1], scalar2=None, op0=ALU.mult),
                                 reads=[uk, "FL"], writes=[uk])
                        dst = cg if which == 0 else cv
                        dk = "cg" if which == 0 else "cv"
                        fw = lambda k, tile=tile: FCW[:, k * 172 + tile:k * 172 + tile + 1]
                        S.op("dve", lambda h, u=u, dst=dst, fw=fw, tile=tile: h.tensor_scalar(out=dst[:, :], in0=u[:, 0:512], scalar1=fw(0),
                                                                                              scalar2=FCB[:, tile:tile + 1], op0=ALU.mult, op1=ALU.add),
                             reads=[uk, "FCW", "FCB"], writes=[dk])
                        for k in (1, 2):
                            S.op("dve", lambda h, u=u, dst=dst, fw=fw, k=k: h.scalar_tensor_tensor(out=dst[:, :], in0=u[:, k:k + 512], scalar=fw(k),
                                                                                                   in1=dst[:, :], op0=ALU.mult, op1=ALU.add),
                                 reads=[uk, dk, "FCW"], writes=[dk])
                    S.op("act", lambda h: h.activation(out=cg[:, :], in_=cg[:, :], func=AF.Gelu), reads=["cg"], writes=["cg"])
                    S.op("pool", lambda h, j=j: h.tensor_tensor(out=aT[:, j, :], in0=cg[:, :], in1=cv[:, :], op=ALU.mult),
                         reads=["cg", "cv"], writes=[("aT", j)])
                ak = [("aT", j) for j in range(NFT)]
                for nch in range(8):
                    ngr = (NFT + 7) // 8
                    for kg in range(ngr):
                        nk = min(8, NFT - kg * 8)
                        slot = self.wslot()
                        wv = WA[:, slot, :].rearrange("p (k c) -> p k c", c=512)
                        self.wload(S, wv[:, 0:nk, :], w_dn[kg * 1024:kg * 1024 + nk * 128, nch * 512:(nch + 1) * 512].rearrange("(k p) c -> p k c", p=128), slot)
                        for ti in range(4):
                            for kt in range(nk):
                                k = kg * 8 + kt
                                S.op("pe", lambda h, ti=ti, k=k, kt=kt, wv=wv: h.matmul(
                                    PS[:, 4 + ti, :], aT[:, k, ti * 128:(ti + 1) * 128], wv[:, kt, :], start=(k == 0), stop=(k == NFT - 1)),
                                     reads=ak + [("wa", slot)], writes=[("ps", 4 + ti)])
                    for q4 in range(4):
                        ft = nch * 4 + q4
                        s2 = q4 % 2
                        S.op("dve", lambda h, ft=ft, s2=s2: h.tensor_scalar(out=Dg[:, s2, :], in0=self.ident[:, :],
                                                                             scalar1=self.COND[:, 160 + ft:161 + ft], scalar2=None, op0=ALU.mult),
                             reads=["ident", "COND"], writes=[("fdg", s2)])
                        S.op("pe", lambda h, s2=s2: h.matmul(PS[:, 2 + s2, 0:128], self.ones[:, :], Dg[:, s2, :], start=True, stop=True),
                             reads=[("fdg", s2), "ones"], writes=[("ps", 2 + s2)])
                        S.op("act", lambda h, q4=q4, s2=s2: h.activation(out=GBc[:, q4 * 128:(q4 + 1) * 128], in_=PS[:, 2 + s2, 0:128], func=AF.Identity),
                             reads=[("ps", 2 + s2)], writes=["GBc"])
                    for ti in range(4):
                        r0 = HALO + 512 * hf + 128 * ti
                        i3 = cnt % 2
                        cnt += 1
                        xt, yt = xr[i3], yo[i3]
                        S.dma("act", lambda h, xt=xt, r0=r0, nch=nch: h.dma_start(out=xt[:, :], in_=self.x1_d[r0:r0 + 128, nch * 512:(nch + 1) * 512]),
                              writes=[("xres", i3)])
                        S.op("dve", lambda h, yt=yt, ti=ti, nch=nch: h.tensor_tensor(out=yt[:, :], in0=PS[:, 4 + ti, :],
                                                                                     in1=GBc[:, :], op=ALU.mult),
                             reads=[("ps", 4 + ti), "GBc"], writes=[("yo", i3)])
                        S.op("dve", lambda h, yt=yt, xt=xt: h.scalar_tensor_tensor(out=yt[:, :], in0=xt[:, :], scalar=ALPHA, in1=yt[:, :],
                                                                                     op0=ALU.mult, op1=ALU.add),
                             reads=[("yo", i3), ("xres", i3)], writes=[("yo", i3)])
                        S.dma("sp", lambda h, yt=yt, r0=r0, nch=nch: h.dma_start(out=self.y2_d[r0:r0 + 128, nch * 512:(nch + 1) * 512], in_=yt[:, :]),
                              reads=[("yo", i3)], writes=["y2_d"])

    def p_ln_final(self):
        with self.phase() as S:
            self.lnJ = self.T("lnJ", [128, D], BF16)
            self.lnJ2 = self.T("lnJ2", [128, D])
            self.lnst = self.T("lnst", [128, 4])
            G = self.T("lnG", [128, D]); Bt = self.T("lnB", [128, D])
            S.dma("sp", lambda h: h.dma_start(out=G[:, :], in_=self.ln2_g.partition_broadcast(128)), writes=["lnG"])
            S.dma("sp", lambda h: h.dma_start(out=Bt[:, :], in_=self.ln2_b.partition_broadcast(128)), writes=["lnB"])
            XT = [self.T(f"XT{i}", [128, D]) for i in range(2)]
            for ti in range(8):
                r0 = HALO + 128 * ti
                X = XT[ti % 2]
                tag = ti % 2
                S.dma("act", lambda h, X=X, r0=r0: h.dma_start(out=X[:, :], in_=self.y2_d[r0:r0 + 128, :]), writes=[("X", tag)])
                self.ln_rows(S, X, 128, tag, affine=(G, Bt))
                S.dma("sp", lambda h, X=X, ti=ti: h.dma_start(out=self.out[ti * 128:(ti + 1) * 128, :], in_=X[:, :]), reads=[("X", tag)], writes=["out"])


WEIGHT_NAMES = ["w_ada", "b_ada", "w_in", "conv_lru_w", "conv_lru_b", "lru_wr", "lru_br", "lru_wi", "lru_bi",
                "lru_lambda", "ssm_lam_re", "ssm_lam_im", "ssm_b_re", "ssm_b_im", "ssm_c_re", "ssm_c_im", "ssm_d",
                "ssm_log_step", "w_glu", "b_glu", "w_proj_lru", "w_proj_ssm", "w_out", "ln1_g", "ln1_b",
                "ffn_w_up", "ffn_conv_w", "ffn_conv_b", "ffn_w_down", "ln2_g", "ln2_b"]


def make_in_maps(inputs, used=None, ncore=NCORE):
    x = np.asarray(inputs["x"], dtype=np.float32)
    c = np.asarray(inputs["c"], dtype=np.float32)
    used = set(WEIGHT_NAMES + ["xh", "cT", "flags"]) if used is None else set(used)
    shared = {k: np.ascontiguousarray(np.asarray(inputs[k], dtype=np.float32)) for k in WEIGHT_NAMES if k in used}
    in_maps = []
    for k in range(ncore):
        b, j = k // 4, k % 4
        s = 1024 * j
        xh = np.zeros((NT, D), np.float32)
        if j == 0:
            xh[HALO:] = x[b, 0:1024]
        else:
            xh[:] = x[b, s - HALO:s + 1024]
        cT = np.ascontiguousarray(c[b].reshape(KT, 128).T)
        flags = np.zeros((1, 17), np.float32)
        flags[0, 0] = 0.0 if j == 0 else 1.0
        nmask = 8 if j == 0 else 3
        flags[0, 1:9] = 1.0
        flags[0, 1:1 + nmask] = 0.0
        for r in range(NCORE):
            if r // 4 == b and r < k:
                flags[0, 9 + r] = 1.0
        m = dict(shared)
        for nm, arr in (("xh", xh), ("cT", cT), ("flags", flags)):
            if nm in used:
                m[nm] = arr
        in_maps.append(m)
    return in_maps


def kernel(**inputs):
    bld = Builder()
    nc = bld.build()
    in_maps = make_in_maps(inputs)
    res = run_bass_kernel_spmd(nc, in_maps, core_ids=list(range(NCORE)))
    out = np.zeros((2, 4096, D), np.float32)
    for k in range(NCORE):
        b, j = k // 4, k % 4
        out[b, 1024 * j:1024 * (j + 1)] = np.asarray(res.results[k]["out"], dtype=np.float32)
    return out
```

```python
import contextlib
import math
import numpy as np
import concourse.bass as bass
import concourse.mybir as mybir
from concourse.bass_utils import run_bass_kernel_spmd

F32 = mybir.dt.float32
BF16 = mybir.dt.bfloat16
AF = mybir.ActivationFunctionType
ALU = mybir.AluOpType
AX = mybir.AxisListType

NCORE = 8
D = 4096
KT = 32
NT = 1032
HALO = 8
CW = 344
WL = 2048
DFF = 11008
NFT = 86
NIN = 14336
LN_EPS = 1e-5
ALPHA = 2.0 ** 0.25
CB = 24
NBK = NT // CB
EXP_POS = 1026
S5_EXP = 1023

ENGS = ("pe", "act", "dve", "pool", "sp")
SEG = 12000
NDSEM = 12


class _Rec:
    def __init__(self):
        self.call = None

    def __getattr__(self, name):
        def f(*a, **k):
            assert self.__dict__["call"] is None
            self.__dict__["call"] = (name, a, k)
            return self
        return f


def _freeze(fn):
    if fn is None:
        return None
    rec = _Rec()
    fn(rec)
    name, a, k = rec.call
    return lambda h: getattr(h, name)(*a, **k)


class Sched:
    def __init__(self, nc):
        self.nc = nc
        self.ops = {e: [] for e in ENGS}
        self.lastw = {}
        self.readers = {}
        self.ndma = {e: 0 for e in ENGS}

    def _add(self, eng, fn, reads, writes, dma):
        writes = list(writes)
        for k in reads:
            if isinstance(k, tuple) and k[0] == "ps" and k not in writes:
                writes.append(k)
        deps = []
        rawset = set()
        for k in reads:
            t = self.lastw.get(k)
            if t is not None:
                deps.append(t)
                rawset.add(t)
        for k in writes:
            t = self.lastw.get(k)
            if t is not None:
                deps.append(t)
            deps.extend(self.readers.get(k, ()))
        idx = len(self.ops[eng])
        if dma:
            k = self.ndma[eng]
            self.ndma[eng] += 1
            tok = ("d", eng, k)
            if k >= NDSEM:
                deps.append(("d", eng, k - NDSEM))
        else:
            tok = ("e", eng, idx)
        d2 = []
        for t in set(deps):
            if t[0] == "e" and t[1] == eng:
                if dma:
                    d2.append(t)
                    continue
                if t not in rawset or eng == "pe":
                    continue
            d2.append(t)
        self.ops[eng].append(dict(fn=_freeze(fn), deps=d2, dma=dma, inc=False, tok=tok))
        for k in reads:
            self.readers.setdefault(k, []).append(tok)
        for k in writes:
            self.lastw[k] = tok
            self.readers[k] = []
        return tok

    def op(self, eng, fn, reads=(), writes=()):
        return self._add(eng, fn, reads, writes, False)

    def barrier(self):
        toks = []
        for e in ENGS:
            for i in range(len(self.ops[e]) - 1, -1, -1):
                o = self.ops[e][i]
                if not o["dma"] and o["fn"] is not None:
                    toks.append(("e", e, i))
                    break
            n = self.ndma[e]
            for k in range(max(0, n - NDSEM), n):
                toks.append(("d", e, k))
        for e in ENGS:
            self.ops[e].append(dict(fn=None, deps=[t for t in toks if not (t[0] == "e" and t[1] == e)],
                                    dma=False, inc=False, tok=None))
        self.lastw = {}
        self.readers = {}

    def dma(self, eng, fn, reads=(), writes=()):
        return self._add(eng, fn, reads, writes, True)

    def emit(self, final_wait_eng="sp"):
        nc = self.nc
        for e in ENGS:
            for o in self.ops[e]:
                for t in o["deps"]:
                    if t[0] == "e":
                        self.ops[t[1]][t[2]]["inc"] = True
        for e in ENGS:
            for o in reversed(self.ops[e]):
                if not o["dma"] and o["fn"] is not None:
                    o["inc"] = True
                    break
        cnt = {}
        nseg = {}
        for e in ENGS:
            c = 0
            for i, o in enumerate(self.ops[e]):
                if o["inc"] and not o["dma"]:
                    cnt[(e, i)] = c
                    c += 1
            nseg[e] = max(1, (c + SEG - 1) // SEG)
        esem = {e: [nc.alloc_semaphore(name=nc.make_name(f"s_{e}_{j}", True)) for j in range(nseg[e])]
                for e in ENGS}
        dsem = {e: [nc.alloc_semaphore(name=nc.make_name(f"d_{e}_{j}", True)) for j in range(NDSEM)]
                for e in ENGS if self.ndma[e] > 0}

        def target(t):
            if t[0] == "e":
                c = cnt[(t[1], t[2])]
                return esem[t[1]][c // SEG], (c % SEG) + 1, ("e", t[1], c // SEG)
            k = t[2]
            return dsem[t[1]][k % NDSEM], 16 * (k // NDSEM + 1), ("d", t[1], k % NDSEM)

        def run(e, h):
            waited = {}
            for i, o in enumerate(self.ops[e]):
                for t in o["deps"]:
                    s, v, key = target(t)
                    if waited.get(key, 0) >= v:
                        continue
                    waited[key] = v
                    h.wait_ge(s, v)
                if o["fn"] is None:
                    continue
                ins = o["fn"](h)
                if o["dma"]:
                    s, v, _ = target(o["tok"])
                    ins.then_inc(s, 16)
                elif o["inc"]:
                    c = cnt[(e, i)]
                    ins.then_inc(esem[e][c // SEG], 1)
            if e == final_wait_eng:
                for e2 in ENGS:
                    for i2 in range(len(self.ops[e2]) - 1, -1, -1):
                        if not self.ops[e2][i2]["dma"] and self.ops[e2][i2]["fn"] is not None:
                            s, v, _ = target(("e", e2, i2))
                            h.wait_ge(s, v)
                            break
                    n = self.ndma[e2]
                    for k in range(max(0, n - NDSEM), n):
                        s, v, _ = target(("d", e2, k))
                        h.wait_ge(s, v)

        with nc.Block() as block:
            hmap = {"pe": block.tensor, "act": block.scalar, "dve": block.vector,
                    "pool": block.gpsimd, "sp": block.sync}
            for e in ENGS:
                if not self.ops[e] and e != final_wait_eng:
                    continue
                hmap[e](lambda h, e=e: run(e, h))


def v3(ap2d):
    return ap2d.rearrange("p (b c) -> p b c", c=CW)


class Builder:
    def __init__(self, stage=99, debug=(), ncore=NCORE):
        self.ncore = ncore
        self.stage = stage
        self.debug = set(debug)
        self.nc = nc = bass.Bass("TRN2", target_bir_lowering=False)
        self.dbg_out = {}
        self._ishape = {
            "xh": [NT, D],
            "cT2": [128, KT, 2],
            "flags": [1, 18],
            "w_ada_r": [D, 3072],
            "b_ada_r": [24, 128],
            "w_in": [1, D, NIN],
            "conv_lru_w": [1, 4, WL],
            "conv_lru_b": [1, WL],
            "lru_wr": [1, 16, 128, 128],
            "lru_br": [1, 16, 128],
            "lru_wi": [1, 16, 128, 128],
            "lru_bi": [1, 16, 128],
            "lru_lambda": [1, WL],
            "ssm_lam_re": [1, 128, 64],
            "ssm_lam_im": [1, 128, 64],
            "ssm_b_re": [1, 128, 64, 16],
            "ssm_b_im": [1, 128, 64, 16],
            "ssm_c_re": [1, 128, 16, 64],
            "ssm_c_im": [1, 128, 16, 64],
            "ssm_d": [1, WL],
            "ssm_log_step": [1, 128],
            "w_glu": [1, WL, WL],
            "b_glu": [1, WL],
            "w_proj_lru": [1, WL, D],
            "w_proj_ssm": [1, WL, D],
            "w_out": [1, D, D],
            "ln1_g": [1, D],
            "ln1_b": [1, D],
            "ffn_w_up": [1, D, 2 * DFF],
            "ffn_conv_w": [1, 3, 2 * DFF],
            "ffn_conv_b": [1, 2 * DFF],
            "ffn_w_down": [1, DFF, D],
            "ln2_g": [1, D],
            "ln2_b": [1, D],
        }
        self._idecl = {}
        self.out = nc.dram_tensor("out", [1024, D], F32, kind="ExternalOutput").ap()
        self.ag_in = nc.dram_tensor("ag_in", [128, 288], F32)
        self.ag_out = nc.dram_tensor("ag_out", [ncore * 128, 288], F32)
        self.mg_d = nc.dram_tensor("mg_d", [KT, 128, NT], BF16).ap()
        self.y1_d = nc.dram_tensor("y1_d", [NT, D], F32).ap()
        self.x1_d = nc.dram_tensor("x1_d", [NT, D], F32).ap()
        self.y2_d = nc.dram_tensor("y2_d", [NT, D], F32).ap()
        self.wa_n = 0
        self.ps_n = 0

    def __getattr__(self, name):
        ish = self.__dict__.get("_ishape", {})
        if name in ish:
            d = self.__dict__["_idecl"]
            if name not in d:
                d[name] = self.nc.dram_tensor(name, list(ish[name]), F32, kind="ExternalInput").ap()
            return d[name]
        raise AttributeError(name)

    def T(self, name, shape, dt=F32):
        return self.nc.alloc_sbuf_tensor(self.nc.make_name(name, True), list(shape), dt)

    def dbg(self, S, name, src_ap, shape, dt=F32, reads=()):
        if name not in self.debug:
            return
        t = self.nc.dram_tensor("dbg_" + name, list(shape), dt, kind="ExternalOutput").ap()
        self.dbg_out[name] = "dbg_" + name
        if len(shape) == 3 and shape[1] * shape[2] > 4096:
            for i in range(shape[1]):
                S.dma("sp", lambda h, i=i: h.dma_start(out=t[:, i, :], in_=src_ap[:, i, :]), reads=list(reads))
        else:
            S.dma("sp", lambda h: h.dma_start(out=t, in_=src_ap), reads=list(reads))

    @contextlib.contextmanager
    def scope(self):
        nc = self.nc
        saved = (nc.sbuf_base, nc.sbuf_top)
        yield
        self.S.barrier()
        nc.sbuf_base, nc.sbuf_top = saved

    @contextlib.contextmanager
    def phase(self):
        with self.scope():
            yield self.S

    def wslot(self):
        s = self.wa_n % 2
        self.wa_n += 1
        return s

    def wload(self, S, dst, src, slot):
        S.dma("pool", lambda h: h.dma_start(out=dst, in_=src), writes=[("wa", slot)])

    def load_cols(self, S, dst, src_rows, n, key, tmpname):
        PS, ident = self.PS, self.ident
        done = 0
        i = 0
        while done < n:
            m = min(128, n - done)
            tmp = self.T(f"{tmpname}{i}", [128, 128])
            k1 = (tmpname, i)
            S.dma("sp", lambda h, tmp=tmp, m=m, d=done: h.dma_start(out=tmp[0:m, :], in_=src_rows[d:d + m, :]),
                  writes=[k1])
            S.op("pe", lambda h, tmp=tmp, m=m: h.transpose(out=PS[:, 7, 0:m], in_=tmp[0:m, :], identity=ident[0:m, 0:m]),
                 reads=[k1, "ident"], writes=[("ps", 7)])
            S.op("dve", lambda h, m=m, d=done: h.tensor_copy(out=dst[:, d:d + m], in_=PS[:, 7, 0:m]),
                 reads=[("ps", 7)], writes=[key])
            done += m
            i += 1

    def fm_matmul(self, S, psb, wv, nk, rhs_fn, rkeys, slot):
        PS = self.PS
        for kt in range(nk):
            for b in range(3):
                S.op("pe", lambda h, kt=kt, b=b: h.matmul(PS[:, psb + b, 0:CW], wv[:, kt, :], rhs_fn(kt, b),
                                                          start=(kt == 0), stop=(kt == nk - 1)),
                     reads=[("wa", slot)] + list(rkeys), writes=[("ps", psb + b)])

    def big(self):
        b = 3 * (self.ps_n % 2)
        self.ps_n += 1
        return b

    def ln_rows(self, S, X, M, tag, affine=None):
        J = self.lnJ
        J2 = self.lnJ2
        st = self.lnst
        kx = ("X", tag)
        S.op("dve", lambda h: h.tensor_scalar(out=J[0:M, :], in0=X[0:M, :], scalar1=1.0 / D, scalar2=None,
                                              op0=ALU.mult, op1=ALU.add, accum_out=st[0:M, 0:1]),
             reads=[kx], writes=["lnJ", "st0"])
        S.op("dve", lambda h: h.tensor_scalar(out=X[0:M, :], in0=X[0:M, :], scalar1=st[0:M, 0:1], scalar2=None,
                                              op0=ALU.subtract),
             reads=[kx, "st0"], writes=[kx])
        S.op("act", lambda h: h.activation(out=J2[0:M, :], in_=X[0:M, :], func=AF.Square),
             reads=[kx], writes=["lnJ2"])
        S.op("dve", lambda h: h.tensor_scalar(out=J[0:M, :], in0=J2[0:M, :], scalar1=1.0 / D, scalar2=None,
                                              op0=ALU.mult, op1=ALU.add, accum_out=st[0:M, 1:2]),
             reads=["lnJ2"], writes=["lnJ", "st1"])
        S.op("act", lambda h: h.activation(out=st[0:M, 2:3], in_=st[0:M, 1:2], func=AF.Sqrt, bias=self.epsc[0:M, :]),
             reads=["st1"], writes=["st2"])
        S.op("dve", lambda h: h.reciprocal(out=st[0:M, 3:4], in_=st[0:M, 2:3]), reads=["st2"], writes=["st3"])
        S.op("act", lambda h: h.activation(out=X[0:M, :], in_=X[0:M, :], func=AF.Identity, scale=st[0:M, 3:4]),
             reads=[kx, "st3"], writes=[kx])
        if affine is not None:
            G, Bt = affine
            S.op("dve", lambda h: h.tensor_tensor(out=X[0:M, :], in0=X[0:M, :], in1=G[0:M, :], op=ALU.mult),
                 reads=[kx, "lnG"], writes=[kx])
            S.op("pool", lambda h: h.tensor_tensor(out=X[0:M, :], in0=X[0:M, :], in1=Bt[0:M, :], op=ALU.add),
                 reads=[kx, "lnB"], writes=[kx])

    def rows_to_T(self, S, X, M, tag, c0, sc_off, sh_off, n_evac):
        PS, ident, BIG1, COND = self.PS, self.ident, self.BIG1, self.COND
        kx = ("X", tag)
        for f4 in range(8):
            bank = 6 + (f4 % 2)
            for q in range(4):
                ft = f4 * 4 + q
                S.op("pe", lambda h, ft=ft, q=q, bank=bank: h.transpose(out=PS[:, bank, q * 128:q * 128 + M],
                                                                     in_=X[0:M, ft * 128:(ft + 1) * 128],
                                                                     identity=ident[0:M, 0:M]),
                     reads=[kx, "ident"], writes=[("ps", bank)])
            for q in range(4):
                ft = f4 * 4 + q
                if bank == 7:
                    S.op("act", lambda h, ft=ft, q=q, bank=bank: h.activation(
                        out=BIG1[:, ft, c0:c0 + M], in_=PS[:, bank, q * 128:q * 128 + M], func=AF.Identity,
                        scale=COND[:, sc_off + ft:sc_off + ft + 1], bias=COND[:, sh_off + ft:sh_off + ft + 1]),
                         reads=[("ps", bank), "COND"], writes=[("big1", ft)])
                else:
                    S.op("dve", lambda h, ft=ft, q=q, bank=bank: h.tensor_scalar(
                        out=BIG1[:, ft, c0:c0 + M], in0=PS[:, bank, q * 128:q * 128 + M],
                        scalar1=COND[:, sc_off + ft:sc_off + ft + 1], scalar2=COND[:, sh_off + ft:sh_off + ft + 1],
                        op0=ALU.mult, op1=ALU.add),
                         reads=[("ps", bank), "COND"], writes=[("big1", ft)])

    def bcast_cond(self, S, GB, off):
        PS, ident, ones, COND = self.PS, self.ident, self.ones, self.COND
        Dg = self.T("bc_diag", [128, 2, 128])
        for ft in range(KT):
            s = ft % 2
            S.op("dve", lambda h, ft=ft, s=s: h.tensor_scalar(out=Dg[:, s, :], in0=ident[:, :],
                                                              scalar1=COND[:, off + ft:off + ft + 1], scalar2=None,
                                                              op0=ALU.mult),
                 reads=["ident", "COND"], writes=[("bcd", s)])
            S.op("pe", lambda h, s=s: h.matmul(PS[:, 6 + s, 0:128], ones[:, :], Dg[:, s, :], start=True, stop=True),
                 reads=[("bcd", s), "ones"], writes=[("ps", 6 + s)])
            S.op("act", lambda h, ft=ft, s=s: h.activation(out=GB[:, ft * 128:(ft + 1) * 128], in_=PS[:, 6 + s, 0:128],
                                                           func=AF.Identity),
                 reads=[("ps", 6 + s)], writes=["GB"])

    def build(self):
        nc = self.nc
        stage = self.stage
        self.PS = PS = nc.alloc_psum_tensor("PS", [128, 8, 512], F32)
        self.ident = ident = self.T("ident", [128, 128])
        self.ones = ones = self.T("ones", [128, 128])
        self.FL = FL = self.T("FL", [128, 18])
        self.COND = COND = self.T("COND", [128, 192])
        self.GATH = GATH = self.T("GATH", [128, 288])
        self.epsc = self.T("epsc", [128, 1])
        self.CW4 = self.T("CW4", [128, 64])
        self.CWB = self.T("CWB", [128, 16])
        self.BR = self.T("BR", [128, 16])
        self.BI = self.T("BI", [128, 16])
        self.C1 = self.T("C1", [128, 32])
        self.WR = self.T("WR", [128, 16, 128], BF16)
        self.WI = self.T("WI", [128, 16, 128], BF16)
        self.H0L = self.T("H0L", [128, 16])
        self.BIG1 = BIG1 = self.T("BIG1", [128, KT, NT], BF16)
        self.WA = WA = self.T("WA", [128, 2, 4096], BF16)

        self.S = Sched(nc)
        self.p_consts_ada()
        if stage >= 1:
            self.p_ln1()
        if stage >= 2:
            self.p_lru(1)
        if stage >= 3:
            with self.scope():
                self.U = self.T("U", [128, 16, NT], BF16)
                with self.scope():
                    self.s5_alloc()
                    self.p_s5_prep()
                    self.p_xs()
                    if stage >= 4:
                        self.p_s5(1)
                    if stage >= 5:
                        self.p_gather()
                    if stage >= 6:
                        self.p_s5(2)
                if stage >= 7:
                    self.p_glu()
                    self.ys = self.U
                    self.ya = self.T("ya", [128, 16, NT], BF16)
                    self.p_lru(2)
                if stage >= 8:
                    self.p_merge()
        if stage >= 9:
            self.p_wout()
        if stage >= 10:
            self.p_ln_mid()
        if stage >= 11:
            self.p_ffn()
        if stage >= 12:
            self.p_ln_final()
        self.S.emit()
        return nc

    def p_consts_ada(self):
        nc = self.nc
        PS, ident, ones, FL, COND, WA = self.PS, self.ident, self.ones, self.FL, self.COND, self.WA
        with self.phase() as S:
            S.op("pool", lambda h: h.memset(ident[:, :], 0.0), writes=["ident"])
            S.op("pool", lambda h: h.affine_select(out=ident[:, :], in_=ident[:, :], pattern=[[-1, 128]],
                                                   compare_op=ALU.not_equal, fill=1.0, base=0, channel_multiplier=1),
                 reads=["ident"], writes=["ident"])
            S.op("pool", lambda h: h.memset(ones[:, :], 1.0), writes=["ones"])
            S.op("pool", lambda h: h.memset(self.epsc[:, :], LN_EPS), writes=["epsc"])
            S.dma("sp", lambda h: h.dma_start(out=FL[:, :], in_=self.flags.partition_broadcast(128)), writes=["FL"])
            ct32 = self.T("ct32", [128, KT, 2])
            cact = self.T("cact", [128, KT, 2], BF16)
            S.dma("sp", lambda h: h.dma_start(out=ct32[:, :, :], in_=self.cT2), writes=["ct32"])
            S.op("act", lambda h: h.activation(out=cact[:, :, :], in_=ct32[:, :, :], func=AF.Silu), reads=["ct32"], writes=["cact"])
            bada = self.T("bada", [128, 24])
            self.load_cols(S, bada, self.b_ada_r, 24, "bada", "tb")
            self.load_cols(S, self.CW4, self.conv_lru_w[0].rearrange("k (c p) -> (k c) p", p=128), 64, "CW4", "tcw")
            self.load_cols(S, self.CWB, self.conv_lru_b[0].rearrange("(c p) -> c p", p=128), 16, "CWB", "tcb")
            self.load_cols(S, self.BR, self.lru_br[0], 16, "BR", "tbr")
            self.load_cols(S, self.BI, self.lru_bi[0], 16, "BI", "tbi")
            lam = self.T("lamc", [128, 16])
            self.load_cols(S, lam, self.lru_lambda[0].rearrange("(c p) -> c p", p=128), 16, "lamc", "tlm")
            e1 = self.T("e1", [128, 16])
            S.op("act", lambda h: h.activation(out=e1[:, :], in_=lam[:, :], func=AF.Exp, scale=-1.0), reads=["lamc"], writes=["e1"])
            S.op("act", lambda h: h.activation(out=e1[:, :], in_=e1[:, :], func=AF.Ln, bias=ones[:, 0:1]), reads=["e1", "ones"], writes=["e1"])
            S.op("dve", lambda h: h.tensor_scalar(out=self.C1[:, 0:16], in0=e1[:, :], scalar1=-8.0, scalar2=None, op0=ALU.mult),
                 reads=["e1"], writes=["C1"])
            S.op("dve", lambda h: h.tensor_scalar(out=self.C1[:, 16:32], in0=e1[:, :], scalar1=-16.0, scalar2=None, op0=ALU.mult),
                 reads=["e1"], writes=["C1"])
            S.dma("pool", lambda h: h.dma_start(out=self.WR[:, :, :], in_=self.lru_wr[0].rearrange("h i j -> i h j")), writes=["WR"])
            S.dma("pool", lambda h: h.dma_start(out=self.WI[:, :, :], in_=self.lru_wi[0].rearrange("h i j -> i h j")), writes=["WI"])
            for i in range(24):
                slot = self.wslot()
                wv = WA[:, slot, :].rearrange("p (k c) -> p k c", c=128)
                self.wload(S, wv, self.w_ada_r[:, i * 128:(i + 1) * 128].rearrange("(k p) c -> p k c", p=128), slot)
                for kt in range(KT):
                    S.op("pe", lambda h, wv=wv, kt=kt, i=i: h.matmul(PS[:, 5, 2 * i:2 * i + 2], wv[:, kt, :], cact[:, kt, :],
                                                                      start=(kt == 0), stop=(kt == KT - 1)),
                         reads=[("wa", slot), "cact"], writes=[("ps", 5)])
            cl = self.T("cl", [128, 24, 2])
            S.op("dve", lambda h: h.tensor_tensor(out=cl[:, :, :], in0=PS[:, 5, 0:48].rearrange("p (i b) -> p i b", b=2),
                                                  in1=bada[:, :, None].broadcast_to([128, 24, 2]), op=ALU.add),
                 reads=[("ps", 5), "bada"], writes=["cl"])
            NCR = self.ncore
            cg_in = self.nc.dram_tensor("cg_in", [128, 48], F32)
            cg_out = self.nc.dram_tensor("cg_out", [NCR * 128, 48], F32)
            S.dma("pool", lambda h: h.dma_start(out=cg_in.ap(), in_=cl[:, :, :].rearrange("p i b -> p (i b)")), reads=["cl"], writes=["cg_in"])
            S.op("pool", lambda h: h.collective_compute("AllGather", ALU.bypass, replica_groups=[list(range(NCR))],
                                                        ins=[cg_in.ap().opt()], outs=[cg_out.ap().opt()]),
                 reads=["cg_in"], writes=["cg_out"])
            CG = self.T("CG", [128, 8, 48])
            if NCR < 8:
                S.op("dve", lambda h: h.memset(CG[:, :, :], 0.0), writes=["CG"])
            S.dma("pool", lambda h: h.dma_start(out=CG[:, 0:NCR, :], in_=cg_out.ap().rearrange("(r p) c -> p r c", p=128)),
                  reads=["cg_out"], writes=["CG"])
            CGv = CG[:, :, :].rearrange("p r (i b) -> p (r i) b", b=2)
            dlt = self.T("dlt", [128, 192])
            S.op("dve", lambda h: h.tensor_tensor(out=dlt[:, :], in0=CGv[:, :, 1], in1=CGv[:, :, 0], op=ALU.subtract),
                 reads=["CG"], writes=["dlt"])
            S.op("dve", lambda h: h.scalar_tensor_tensor(out=COND[:, :], in0=dlt[:, :], scalar=FL[:, 17:18], in1=CGv[:, :, 0],
                                                         op0=ALU.mult, op1=ALU.add),
                 reads=["dlt", "CG", "FL"], writes=["COND"])
            for sec in (1, 2, 4, 5):
                S.op("dve", lambda h, sec=sec: h.tensor_scalar(out=COND[:, sec * 32:(sec + 1) * 32], in0=COND[:, sec * 32:(sec + 1) * 32],
                                                               scalar1=1.0, scalar2=None, op0=ALU.add),
                     reads=["COND"], writes=["COND"])
            self.dbg(S, "cond", COND[:, :], [128, 192], reads=["COND"])

    def tok_tiles(self):
        tiles = [(0, HALO)]
        for i in range(8):
            tiles.append((HALO + 128 * i, 128))
        return tiles

    def p_ln1(self):
        BIG1, FL = self.BIG1, self.FL
        with self.phase() as S:
            self.lnJ = self.T("lnJ", [128, D], BF16)
            self.lnJ2 = self.T("lnJ2", [128, D])
            self.lnst = self.T("lnst", [128, 4])
            XT = [self.T(f"XT{i}", [128, D]) for i in range(2)]
            nev = [0]
            for ti, (r0, M) in enumerate(self.tok_tiles()):
                X = XT[ti % 2]
                tag = ti % 2
                S.dma("act", lambda h, X=X, r0=r0, M=M: h.dma_start(out=X[0:M, :], in_=self.xh[r0:r0 + M, :]),
                      writes=[("X", tag)])
                import os
                cut = int(os.environ.get("LN1_CUT", "9"))
                if cut >= 1:
                    self.ln_rows(S, X, M, tag)
                if cut >= 2:
                    self.rows_to_T(S, X, M, tag, r0, 32, 0, nev)
            if cut >= 3:
              S.op("dve", lambda h: h.tensor_scalar(out=BIG1[:, :, 0:HALO], in0=BIG1[:, :, 0:HALO], scalar1=FL[:, 0:1],
                                                  scalar2=None, op0=ALU.mult),
                 reads=[("big1", ft) for ft in range(KT)] + ["FL"], writes=[("big1", ft) for ft in range(KT)])
            self.dbg(S, "hT", BIG1[:, :, :], [128, KT, NT], BF16, reads=[("big1", ft) for ft in range(KT)])

    def p_lru(self, pas):
        nc = self.nc
        PS, BIG1, WA, FL, GATH = self.PS, self.BIG1, self.WA, self.FL, self.GATH
        w_in = self.w_in[0]
        with self.phase() as S:
            xap = self.T("xap", [128, NT + 3])
            xc = self.T("xc", [128, NT])
            xcb = self.T("xcb", [128, NT], BF16)
            rr = self.T("rr", [128, NT])
            ii = self.T("ii", [128, NT])
            aa = self.T("aa", [128, NT])
            bb = self.T("bb", [128, NT])
            sm = self.T("sm", [128, 2])
            S.op("pool", lambda h: h.memset(xap[:, 0:3], 0.0), writes=["xap_pad"])
            hk = [("big1", ft) for ft in range(KT)]
            hh = xap[:, 3:NT + 3]
            gg = rr
            for ct in range(16):
                slot = self.wslot()
                wv = WA[:, slot, :].rearrange("p (k c) -> p k c", c=128)
                self.wload(S, wv, w_in[:, ct * 128:(ct + 1) * 128].rearrange("(k p) c -> p k c", p=128), slot)
                pb = self.big()
                self.fm_matmul(S, pb, wv, KT, lambda kt, b: BIG1[:, kt, b * CW:(b + 1) * CW], hk, slot)
                pk = [("ps", pb + b) for b in range(3)]
                S.op("act", lambda h, pb=pb: h.activation(out=v3(xap[:, 3:NT + 3]), in_=PS[:, pb:pb + 3, 0:CW], func=AF.Identity),
                     reads=pk, writes=["xap"])
                cw = lambda k, ct=ct: self.CW4[:, k * 16 + ct:k * 16 + ct + 1]
                S.op("dve", lambda h, ct=ct, cw=cw: h.tensor_scalar(out=xc[:, :], in0=xap[:, 0:NT], scalar1=cw(0),
                                                                    scalar2=self.CWB[:, ct:ct + 1], op0=ALU.mult, op1=ALU.add),
                     reads=["xap", "xap_pad", "CW4", "CWB"], writes=["xc"])
                for k in (1, 2, 3):
                    S.op("dve", lambda h, k=k, cw=cw: h.scalar_tensor_tensor(out=xc[:, :], in0=xap[:, k:k + NT], scalar=cw(k),
                                                                             in1=xc[:, :], op0=ALU.mult, op1=ALU.add),
                         reads=["xap", "xap_pad", "xc", "CW4"], writes=["xc"])
                S.op("act", lambda h: h.activation(out=xcb[:, :], in_=xc[:, :], func=AF.Identity), reads=["xc"], writes=["xcb"])
                pr = self.big()
                for b in range(3):
                    S.op("pe", lambda h, b=b, ct=ct, pr=pr: h.matmul(PS[:, pr + b, 0:CW], self.WR[:, ct, :], xcb[:, b * CW:(b + 1) * CW],
                                                                     start=True, stop=True),
                         reads=["xcb", "WR"], writes=[("ps", pr + b)])
                S.op("act", lambda h, ct=ct, pr=pr: h.activation(out=v3(rr[:, :]), in_=PS[:, pr:pr + 3, 0:CW], func=AF.Sigmoid,
                                                                 bias=self.BR[:, ct:ct + 1]),
                     reads=[("ps", pr + b) for b in range(3)] + ["BR"], writes=["rr"])
                pi = self.big()
                for b in range(3):
                    S.op("pe", lambda h, b=b, ct=ct, pi=pi: h.matmul(PS[:, pi + b, 0:CW], self.WI[:, ct, :], xcb[:, b * CW:(b + 1) * CW],
                                                                     start=True, stop=True),
                         reads=["xcb", "WI"], writes=[("ps", pi + b)])
                S.op("act", lambda h, ct=ct, pi=pi: h.activation(out=v3(ii[:, :]), in_=PS[:, pi:pi + 3, 0:CW], func=AF.Sigmoid,
                                                                 bias=self.BI[:, ct:ct + 1]),
                     reads=[("ps", pi + b) for b in range(3)] + ["BI"], writes=["ii"])
                S.op("act", lambda h, ct=ct: h.activation(out=aa[:, :], in_=rr[:, :], func=AF.Exp, scale=self.C1[:, ct:ct + 1]),
                     reads=["rr", "C1"], writes=["aa"])
                S.op("act", lambda h, ct=ct: h.activation(out=bb[:, :], in_=rr[:, :], func=AF.Exp, scale=self.C1[:, 16 + ct:17 + ct]),
                     reads=["rr", "C1"], writes=["bb"])
                S.op("act", lambda h: h.activation(out=bb[:, :], in_=bb[:, :], func=AF.Sqrt, scale=-1.0, bias=self.ones[:, 0:1]),
                     reads=["bb", "ones"], writes=["bb"])
                S.op("dve", lambda h: h.tensor_tensor(out=ii[:, :], in0=ii[:, :], in1=xc[:, :], op=ALU.mult),
                     reads=["ii", "xc"], writes=["ii"])
                S.op("dve", lambda h: h.tensor_tensor(out=bb[:, :], in0=bb[:, :], in1=ii[:, :], op=ALU.mult),
                     reads=["bb", "ii"], writes=["bb"])
                S.op("dve", lambda h: h.tensor_tensor(out=bb[:, 0:8], in0=bb[:, 0:8], in1=FL[:, 1:9], op=ALU.mult),
                     reads=["bb", "FL"], writes=["bb"])
                if pas == 2:
                    S.op("dve", lambda h, ct=ct: h.tensor_copy(out=bb[:, 2:3], in_=self.H0L[:, ct:ct + 1]),
                         reads=["bb", "H0L"], writes=["bb"])
                S.op("dve", lambda h: h.tensor_tensor_scan(out=hh, data0=aa[:, :], data1=bb[:, :], initial=0.0,
                                                           op0=ALU.mult, op1=ALU.add),
                     reads=["aa", "bb", "xc"], writes=["xap"])
                if pas == 1:
                    S.op("dve", lambda h, ct=ct: h.tensor_copy(out=GATH[:, 16 + ct:17 + ct], in_=xap[:, 3 + EXP_POS:4 + EXP_POS]),
                         reads=["xap"], writes=["GATH"])
                    S.op("dve", lambda h: h.tensor_reduce(out=sm[:, 0:1], in_=rr[:, 3:EXP_POS + 1], axis=AX.X, op=ALU.add),
                         reads=["rr"], writes=["sm"])
                    S.op("act", lambda h, ct=ct: h.activation(out=GATH[:, ct:ct + 1], in_=sm[:, 0:1], func=AF.Exp,
                                                              scale=self.C1[:, ct:ct + 1]),
                         reads=["sm", "C1"], writes=["GATH"])
                    if ct == 0:
                        self.dbg(S, "lru_h0", hh, [128, NT], reads=["xap"])
                        self.dbg(S, "lru_a0", aa[:, :], [128, NT], reads=["aa"])
                        self.dbg(S, "lru_b0", bb[:, :], [128, NT], reads=["bb"])
                        self.dbg(S, "lru_xc0", xc[:, :], [128, NT], reads=["xc"])
                else:
                    slot = self.wslot()
                    wv2 = WA[:, slot, :].rearrange("p (k c) -> p k c", c=128)
                    self.wload(S, wv2, w_in[:, 2048 + ct * 128:2048 + (ct + 1) * 128].rearrange("(k p) c -> p k c", p=128), slot)
                    pg = self.big()
                    self.fm_matmul(S, pg, wv2, KT, lambda kt, b: BIG1[:, kt, b * CW:(b + 1) * CW], hk, slot)
                    S.op("act", lambda h, pg=pg: h.activation(out=v3(gg[:, :]), in_=PS[:, pg:pg + 3, 0:CW], func=AF.Gelu),
                         reads=[("ps", pg + b) for b in range(3)], writes=["rr"])
                    S.op("dve", lambda h, ct=ct: h.tensor_tensor(out=self.ya[:, ct, :], in0=hh, in1=gg[:, :], op=ALU.mult),
                         reads=["xap", "rr"], writes=[("ya", ct)])
            if pas == 1:
                self.dbg(S, "gath_lru", GATH[:, 0:32], [128, 32], reads=["GATH"])
            else:
                self.dbg(S, "ya", self.ya[:, :, :], [128, 16, NT], BF16, reads=[("ya", ct) for ct in range(16)])

    def s5_alloc(self):
        self.CT = self.T("CT", [64, 2, 128, 16])
        self.BTb = self.T("BTb", [128, 16, 128], BF16)
        self.LD1 = self.T("LD1", [64, 2, 128])
        self.LD2 = self.T("LD2", [64, 2, 128])
        self.PM = self.T("PM", [128, 8])
        self.DCOL = self.T("DCOL", [128, 16])
        self.QRE = self.T("QRE", [64, 128])
        self.QIM = self.T("QIM", [64, 128])
        self.H0S = self.T("H0S", [64, 2, 128])
        self.LBAR = self.T("LBAR", [64, 2, 128])

    def p_s5_prep(self):
        PS, ident = self.PS, self.ident
        with self.phase() as S:
            T = self.T
            lr_n = T("lr_n", [128, 64]); li_n = T("li_n", [128, 64])
            S.dma("sp", lambda h: h.dma_start(out=lr_n[:, :], in_=self.ssm_lam_re[0]), writes=["lr_n"])
            S.dma("sp", lambda h: h.dma_start(out=li_n[:, :], in_=self.ssm_lam_im[0]), writes=["li_n"])
            LRE = T("LRE", [64, 128]); LIM = T("LIM", [64, 128])
            for src, dst, k in ((lr_n, LRE, "LRE"), (li_n, LIM, "LIM")):
                S.op("pe", lambda h, src=src: h.transpose(out=PS[0:64, 7, 0:128], in_=src[:, :], identity=ident[:, :]),
                     reads=[src is lr_n and "lr_n" or "li_n", "ident"], writes=[("ps", 7)])
                S.op("dve", lambda h, dst=dst: h.tensor_copy(out=dst[:, :], in_=PS[0:64, 7, 0:128]), reads=[("ps", 7)], writes=[k])
            STEP = T("STEP", [64, 128])
            S.dma("sp", lambda h: h.dma_start(out=STEP[:, :], in_=self.ssm_log_step.partition_broadcast(64)), writes=["STEP"])
            S.op("act", lambda h: h.activation(out=STEP[:, :], in_=STEP[:, :], func=AF.Exp), reads=["STEP"], writes=["STEP"])
            A_ = T("A_", [64, 128]); TH = T("TH", [64, 128]); MAG = T("MAG", [64, 128])
            S.op("dve", lambda h: h.tensor_tensor(out=A_[:, :], in0=LRE[:, :], in1=STEP[:, :], op=ALU.mult), reads=["LRE", "STEP"], writes=["A_"])
            S.op("dve", lambda h: h.tensor_tensor(out=TH[:, :], in0=LIM[:, :], in1=STEP[:, :], op=ALU.mult), reads=["LIM", "STEP"], writes=["TH"])
            S.op("act", lambda h: h.activation(out=MAG[:, :], in_=A_[:, :], func=AF.Exp), reads=["A_"], writes=["MAG"])
            hp = T("hpi", [64, 1])
            S.op("pool", lambda h: h.memset(hp[:, :], math.pi / 2), writes=["hpi"])
            sn = [T("sn0", [64, 128]), T("sn1", [64, 128])]
            cs = [T("cs0", [64, 128]), T("cs1", [64, 128])]
            tq = T("tq", [64, 128])
            S.op("act", lambda h: h.activation(out=sn[0][:, :], in_=TH[:, :], func=AF.Sin, scale=1.0 / 32), reads=["TH"], writes=["sn0"])
            S.op("act", lambda h: h.activation(out=cs[0][:, :], in_=TH[:, :], func=AF.Sin, scale=1.0 / 32, bias=hp[:, :]),
                 reads=["TH", "hpi"], writes=["cs0"])
            cur = 0
            for it in range(5):
                nx = 1 - cur
                S.op("dve", lambda h, cur=cur: h.tensor_tensor(out=tq[:, :], in0=sn[cur][:, :], in1=sn[cur][:, :], op=ALU.mult),
                     reads=[f"sn{cur}"], writes=["tq"])
                S.op("dve", lambda h, nx=nx: h.tensor_scalar(out=cs[nx][:, :], in0=tq[:, :], scalar1=-2.0, scalar2=1.0,
                                                             op0=ALU.mult, op1=ALU.add),
                     reads=["tq"], writes=[f"cs{nx}"])
                S.op("dve", lambda h, cur=cur, nx=nx: h.scalar_tensor_tensor(out=sn[nx][:, :], in0=sn[cur][:, :], scalar=2.0,
                                                                             in1=cs[cur][:, :], op0=ALU.mult, op1=ALU.mult),
                     reads=[f"sn{cur}", f"cs{cur}"], writes=[f"sn{nx}"])
                cur = nx
            LB = self.LBAR
            S.op("dve", lambda h, cur=cur: h.tensor_tensor(out=LB[:, 0, :], in0=MAG[:, :], in1=cs[cur][:, :], op=ALU.mult),
                 reads=["MAG", f"cs{cur}"], writes=["LB"])
            S.op("dve", lambda h, cur=cur: h.tensor_tensor(out=LB[:, 1, :], in0=MAG[:, :], in1=sn[cur][:, :], op=ALU.mult),
                 reads=["MAG", f"sn{cur}"], writes=["LB"])
            LD1, LD2 = self.LD1, self.LD2
            S.op("dve", lambda h: h.tensor_copy(out=LD1[:, 0, :], in_=LB[:, 0, :]), reads=["LB"], writes=["LD1"])
            S.op("dve", lambda h: h.tensor_copy(out=LD1[:, 1, :], in_=LB[:, 0, :]), reads=["LB"], writes=["LD1"])
            S.op("dve", lambda h: h.tensor_scalar(out=LD2[:, 0, :], in0=LB[:, 1, :], scalar1=-1.0, scalar2=None, op0=ALU.mult),
                 reads=["LB"], writes=["LD2"])
            S.op("dve", lambda h: h.tensor_copy(out=LD2[:, 1, :], in_=LB[:, 1, :]), reads=["LB"], writes=["LD2"])
            den = T("den", [64, 128]); t2 = T("t2", [64, 128]); xr = T("xr", [64, 128])
            qre, qim = self.QRE, self.QIM
            S.op("dve", lambda h: h.tensor_tensor(out=den[:, :], in0=LRE[:, :], in1=LRE[:, :], op=ALU.mult), reads=["LRE"], writes=["den"])
            S.op("dve", lambda h: h.tensor_tensor(out=t2[:, :], in0=LIM[:, :], in1=LIM[:, :], op=ALU.mult), reads=["LIM"], writes=["t2"])
            S.op("dve", lambda h: h.tensor_tensor(out=den[:, :], in0=den[:, :], in1=t2[:, :], op=ALU.add), reads=["den", "t2"], writes=["den"])
            S.op("dve", lambda h: h.reciprocal(out=den[:, :], in_=den[:, :]), reads=["den"], writes=["den"])
            S.op("dve", lambda h: h.tensor_scalar(out=xr[:, :], in0=LB[:, 0, :], scalar1=-1.0, scalar2=None, op0=ALU.add), reads=["LB"], writes=["xr"])
            S.op("dve", lambda h: h.tensor_tensor(out=qre[:, :], in0=xr[:, :], in1=LRE[:, :], op=ALU.mult), reads=["xr", "LRE"], writes=["qre"])
            S.op("dve", lambda h: h.tensor_tensor(out=t2[:, :], in0=LB[:, 1, :], in1=LIM[:, :], op=ALU.mult), reads=["LB", "LIM", "den"], writes=["t2"])
            S.op("dve", lambda h: h.tensor_tensor(out=qre[:, :], in0=qre[:, :], in1=t2[:, :], op=ALU.add), reads=["qre", "t2"], writes=["qre"])
            S.op("dve", lambda h: h.tensor_tensor(out=qre[:, :], in0=qre[:, :], in1=den[:, :], op=ALU.mult), reads=["qre", "den"], writes=["qre"])
            S.op("dve", lambda h: h.tensor_tensor(out=qim[:, :], in0=LB[:, 1, :], in1=LRE[:, :], op=ALU.mult), reads=["LB", "LRE"], writes=["qim"])
            S.op("dve", lambda h: h.tensor_tensor(out=t2[:, :], in0=xr[:, :], in1=LIM[:, :], op=ALU.mult), reads=["xr", "LIM", "qre"], writes=["t2"])
            S.op("dve", lambda h: h.tensor_tensor(out=qim[:, :], in0=qim[:, :], in1=t2[:, :], op=ALU.subtract), reads=["qim", "t2"], writes=["qim"])
            S.op("dve", lambda h: h.tensor_tensor(out=qim[:, :], in0=qim[:, :], in1=den[:, :], op=ALU.mult), reads=["qim", "den"], writes=["qim"])
            self.dbg(S, "lbar", LB[:, :, :], [64, 2, 128], reads=["LB"])
            self.dbg(S, "qre", qre[:, :], [64, 128], reads=["qre"])
            self.dbg(S, "qim", qim[:, :], [64, 128], reads=["qim"])
        for hb in range(2):
          with self.phase() as S:
            T = self.T
            qre, qim = self.QRE, self.QIM
            g0 = hb * 64
            bre = T("bre", [64, 64, 16]); bim = T("bim", [64, 64, 16])
            bnat = [T("bnat0", [128, 1024]), T("bnat1", [128, 1024])]
            for a, (srcd, dst, k) in enumerate(((self.ssm_b_re, bre, "bre"), (self.ssm_b_im, bim, "bim"))):
                S.dma("sp" if a == 0 else "act", lambda h, a=a, srcd=srcd: h.dma_start(out=bnat[a][:, :], in_=srcd[0].rearrange("g p n -> g (p n)")),
                      writes=[f"bnat{a}"])
                for n in range(16):
                    bank = 6 + (n % 2)
                    S.op("pe", lambda h, a=a, n=n, bank=bank: h.transpose(
                        out=PS[0:64, bank, 0:64], in_=bnat[a][g0:g0 + 64, :].rearrange("g (p n) -> g p n", n=16)[:, :, n],
                        identity=ident[g0:g0 + 64, g0:g0 + 64]),
                         reads=[f"bnat{a}", "ident"], writes=[("ps", bank)])
                    S.op("act", lambda h, dst=dst, n=n, bank=bank: h.activation(out=dst[:, :, n], in_=PS[0:64, bank, 0:64], func=AF.Identity),
                         reads=[("ps", bank)], writes=[k])
            BB = T("BB", [64, 2, 64, 16]); tb = T("tbb", [64, 64, 16])
            qb = lambda q: q[:, g0:g0 + 64, None].broadcast_to([64, 64, 16])
            S.op("dve", lambda h: h.tensor_tensor(out=BB[:, 0, :, :], in0=bre[:, :, :], in1=qb(qre), op=ALU.mult), reads=["bre"], writes=["BB0"])
            S.op("dve", lambda h: h.tensor_tensor(out=tb[:, :, :], in0=bim[:, :, :], in1=qb(qim), op=ALU.mult), reads=["bim"], writes=["tbb"])
            S.op("dve", lambda h: h.tensor_tensor(out=BB[:, 0, :, :], in0=BB[:, 0, :, :], in1=tb[:, :, :], op=ALU.subtract), reads=["BB0", "tbb"], writes=["BB0"])
            S.op("dve", lambda h: h.tensor_tensor(out=BB[:, 1, :, :], in0=bim[:, :, :], in1=qb(qre), op=ALU.mult), reads=["bim"], writes=["BB1"])
            S.op("dve", lambda h: h.tensor_tensor(out=tb[:, :, :], in0=bre[:, :, :], in1=qb(qim), op=ALU.mult), reads=["bre", "BB0"], writes=["tbb"])
            S.op("dve", lambda h: h.tensor_tensor(out=BB[:, 1, :, :], in0=BB[:, 1, :, :], in1=tb[:, :, :], op=ALU.add), reads=["BB1", "tbb"], writes=["BB1"])
            for gbl in range(8):
                gb = hb * 8 + gbl
                for ri in range(2):
                    bank = 6 + (ri % 2)
                    S.op("pe", lambda h, gbl=gbl, ri=ri, bank=bank: h.transpose(
                        out=PS[:, bank, 0:64], in_=BB[:, ri, gbl * 8:(gbl + 1) * 8, :].rearrange("p g n -> p (g n)"),
                        identity=ident[0:64, 0:64]),
                         reads=[f"BB{ri}", "ident"], writes=[("ps", bank)])
                    S.op("act", lambda h, gb=gb, ri=ri, bank=bank: h.activation(out=self.BTb[:, gb, ri * 64:(ri + 1) * 64],
                                                                                 in_=PS[:, bank, 0:64], func=AF.Identity),
                         reads=[("ps", bank)], writes=["BTb"])
            self.dbg(S, f"BB{hb}", BB[:, :, :, :], [64, 2, 64, 16], reads=["BB0", "BB1"])
            self.dbg(S, f"bre{hb}", bre[:, :, :], [64, 64, 16], reads=["bre"])
        with self.phase() as S:
            T = self.T
            cnat = [T("cnat0", [128, 1024]), T("cnat1", [128, 1024])]
            for ri, srcd in enumerate((self.ssm_c_re, self.ssm_c_im)):
                S.dma("sp" if ri == 0 else "act", lambda h, ri=ri, srcd=srcd: h.dma_start(out=cnat[ri][:, :], in_=srcd[0].rearrange("g n p -> g (n p)")),
                      writes=[f"cnat{ri}"])
                for n in range(16):
                    bank = 6 + (n % 2)
                    S.op("pe", lambda h, ri=ri, n=n, bank=bank: h.transpose(out=PS[0:64, bank, 0:128], in_=cnat[ri][:, n * 64:(n + 1) * 64],
                                                                            identity=ident[:, :]),
                         reads=[f"cnat{ri}", "ident"], writes=[("ps", bank)])
                    S.op("act", lambda h, ri=ri, n=n, bank=bank: h.activation(out=self.CT[:, ri, :, n], in_=PS[0:64, bank, 0:128],
                                                                              func=AF.Identity, scale=(1.0 if ri == 0 else -1.0)),
                         reads=[("ps", bank)], writes=["CT"])
            S.op("dve", lambda h: h.tensor_reduce(out=self.PM[:, :], in_=ident[:, :].rearrange("p (a b) -> p a b", b=16), axis=AX.X, op=ALU.add),
                 reads=["ident"], writes=["PM"])
            self.load_cols(S, self.DCOL, self.ssm_d[0].rearrange("(c p) -> c p", p=128), 16, "DCOL", "tsd")
            self.dbg(S, "CT", self.CT[:, :, :, :], [64, 2, 128, 16], reads=["CT"])

    def p_xs(self):
        PS, BIG1, WA, U = self.PS, self.BIG1, self.WA, self.U
        w_in = self.w_in[0]
        hk = [("big1", ft) for ft in range(KT)]
        with self.phase() as S:
            for ct in range(16):
                slot = self.wslot()
                wv = WA[:, slot, :].rearrange("p (k c) -> p k c", c=128)
                self.wload(S, wv, w_in[:, 4096 + ct * 128:4096 + (ct + 1) * 128].rearrange("(k p) c -> p k c", p=128), slot)
                pb = self.big()
                self.fm_matmul(S, pb, wv, KT, lambda kt, b: BIG1[:, kt, b * CW:(b + 1) * CW], hk, slot)
                pk = [("ps", pb + b) for b in range(3)]
                if ct % 2 == 0:
                    S.op("act", lambda h, pb=pb, ct=ct: h.activation(out=v3(U[:, ct, :]), in_=PS[:, pb:pb + 3, 0:CW], func=AF.Identity),
                         reads=pk, writes=[("U", ct)])
                else:
                    S.op("dve", lambda h, pb=pb, ct=ct: h.tensor_copy(out=v3(U[:, ct, :]), in_=PS[:, pb:pb + 3, 0:CW]),
                         reads=pk, writes=[("U", ct)])
            self.dbg(S, "U", U[:, :, :], [128, 16, NT], BF16, reads=[("U", ct) for ct in range(16)])

    def p_s5(self, pas):
        PS, ident, U, CT, BTb, PM = self.PS, self.ident, self.U, self.CT, self.BTb, self.PM
        LD1, LD2 = self.LD1, self.LD2
        with self.phase() as S:
            hist = self.T("hist", [64, 2, 128, CB + 1])
            T1 = self.T("T1", [64, 2, 128]); T2 = self.T("T2", [64, 2, 128])
            Um = [self.T(f"Um{i}", [128, 8, CB], BF16) for i in range(4)]
            ytm = self.T("ytm", [CB, WL])
            ysb = self.T("ysb", [128, 16, CB])
            yv = self.T("yv", [128, 16, CB])
            uk = [("U", ct) for ct in range(16)]
            if pas == 1:
                S.op("dve", lambda h: h.memset(hist[:, :, :, 0], 0.0), writes=[("hs", 0)])
            else:
                S.op("dve", lambda h: h.tensor_copy(out=hist[:, :, :, 0], in_=self.H0S[:, :, :]), reads=["H0S"], writes=[("hs", 0)])
            for blk in range(NBK):
                t0 = blk * CB
                slots = [("hs", t + 1) for t in range(CB)]
                for gb in range(16):
                    um = Um[gb % 4]
                    uk1 = ("Um", gb % 4)
                    S.op("pool", lambda h, um=um, gb=gb, t0=t0: h.tensor_tensor(
                        out=um[:, :, :], in0=U[:, gb, None, t0:t0 + CB].broadcast_to([128, 8, CB]),
                        in1=PM[:, :, None].broadcast_to([128, 8, CB]), op=ALU.mult),
                         reads=[("U", gb), "PM"], writes=[uk1])
                    bank = gb % 6
                    for ri in range(2):
                        for gl in range(8):
                            col = (ri * 8 + gl) * CB
                            S.op("pe", lambda h, um=um, gb=gb, ri=ri, gl=gl, col=col, bank=bank: h.matmul(
                                PS[0:64, bank, col:col + CB], BTb[:, gb, ri * 64:(ri + 1) * 64], um[:, gl, :], start=True, stop=True),
                                 reads=[uk1, "BTb"], writes=[("ps", bank)])
                    S.op("act", lambda h, gb=gb, bank=bank: h.activation(
                        out=hist[:, :, gb * 8:(gb + 1) * 8, 1:CB + 1],
                        in_=PS[0:64, bank, 0:16 * CB].rearrange("p (r g t) -> p r g t", r=2, g=8), func=AF.Identity),
                         reads=[("ps", bank)], writes=slots)
                for t in range(CB):
                    kp, kc = ("hs", t), ("hs", t + 1)
                    prev = hist[:, :, :, t]
                    curv = hist[:, :, :, t + 1]
                    S.op("dve", lambda h, prev=prev: h.tensor_tensor(out=T1[:, :, :], in0=prev, in1=LD1[:, :, :], op=ALU.mult),
                         reads=[kp, "LD1"], writes=["T1"])
                    S.op("dve", lambda h, t=t: h.tensor_tensor(out=T2[:, 0, :], in0=hist[:, 1, :, t], in1=LD2[:, 0, :], op=ALU.mult),
                         reads=[kp, "LD2"], writes=["T2a"])
                    S.op("dve", lambda h, t=t: h.tensor_tensor(out=T2[:, 1, :], in0=hist[:, 0, :, t], in1=LD2[:, 1, :], op=ALU.mult),
                         reads=[kp, "LD2"], writes=["T2b"])
                    S.op("dve", lambda h: h.tensor_tensor(out=T1[:, :, :], in0=T1[:, :, :], in1=T2[:, :, :], op=ALU.add),
                         reads=["T1", "T2a", "T2b"], writes=["T1"])
                    S.op("dve", lambda h, curv=curv: h.tensor_tensor(out=curv, in0=curv, in1=T1[:, :, :], op=ALU.add),
                         reads=[kc, "T1"], writes=[kc])
                if pas == 1:
                    if blk == NBK - 1:
                        sl = S5_EXP - t0 + 1
                        S.op("dve", lambda h, sl=sl: h.tensor_copy(out=self.GATH[0:64, 32:288].rearrange("p (r g) -> p r g", r=2),
                                                                   in_=hist[:, :, :, sl]),
                             reads=[("hs", sl)], writes=["GATH"])
                else:
                    for g in range(128):
                        bank = g // 32
                        for ri in range(2):
                            S.op("pe", lambda h, g=g, ri=ri, bank=bank: h.matmul(
                                PS[0:CB, bank, (g % 32) * 16:(g % 32) * 16 + 16], hist[:, ri, g, 1:CB + 1], CT[:, ri, g, :],
                                start=(ri == 0), stop=(ri == 1)),
                                 reads=slots + ["CT"], writes=[("ps", bank)])
                    S.op("act", lambda h: h.activation(out=ytm[:, :].rearrange("p (b c) -> p b c", c=512), in_=PS[0:CB, 0:4, :],
                                                       func=AF.Identity),
                         reads=[("ps", b) for b in range(4)], writes=["ytm"])
                    for ct in range(16):
                        S.op("pe", lambda h, ct=ct: h.transpose(out=PS[:, 4, ct * CB:(ct + 1) * CB], in_=ytm[0:CB, ct * 128:(ct + 1) * 128],
                                                                identity=ident[0:CB, 0:CB]),
                             reads=["ytm", "ident"], writes=[("ps", 4)])
                    S.op("act", lambda h: h.activation(out=ysb[:, :, :], in_=PS[:, 4, 0:16 * CB].rearrange("p (c t) -> p c t", t=CB),
                                                       func=AF.Identity),
                         reads=[("ps", 4)], writes=["ysb"])
                    S.op("pool", lambda h, t0=t0: h.tensor_tensor(out=yv[:, :, :], in0=U[:, :, t0:t0 + CB],
                                                                   in1=self.DCOL[:, :, None].broadcast_to([128, 16, CB]), op=ALU.mult),
                         reads=uk + ["DCOL"], writes=["yv"])
                    S.op("pool", lambda h: h.tensor_tensor(out=yv[:, :, :], in0=yv[:, :, :], in1=ysb[:, :, :], op=ALU.add),
                         reads=["yv", "ysb"], writes=["yv"])
                    S.op("act", lambda h, t0=t0: h.activation(out=U[:, :, t0:t0 + CB], in_=yv[:, :, :], func=AF.Gelu),
                         reads=["yv"], writes=uk)
                if blk < NBK - 1:
                    S.op("dve", lambda h: h.tensor_copy(out=hist[:, :, :, 0], in_=hist[:, :, :, CB]),
                         reads=[("hs", CB)], writes=[("hs", 0)])
            if pas == 1:
                self.dbg(S, "gath_s5", self.GATH[0:64, 32:288], [64, 256], reads=["GATH"])
            else:
                self.dbg(S, "yg", U[:, :, :], [128, 16, NT], BF16, reads=uk)

    def p_gather(self):
        nc = self.nc
        GATH, FL = self.GATH, self.FL
        with self.phase() as S:
            NCORE = self.ncore
            G8 = self.T("G8", [128, NCORE, 288])
            S.dma("pool", lambda h: h.dma_start(out=self.ag_in.ap(), in_=GATH[:, :]), writes=["ag_in"])
            S.op("pool", lambda h: h.collective_compute("AllGather", ALU.bypass, replica_groups=[list(range(NCORE))],
                                                        ins=[self.ag_in.ap().opt()], outs=[self.ag_out.ap().opt()]),
                 reads=["ag_in"], writes=["ag_out"])
            S.dma("pool", lambda h: h.dma_start(out=G8[:, :, :], in_=self.ag_out.ap().rearrange("(r p) c -> p r c", p=128)),
                  reads=["ag_out"], writes=["G8"])
            H0L = self.H0L
            tl = self.T("tl", [128, 16])
            S.op("dve", lambda h: h.memset(H0L[:, :], 0.0), writes=["H0L"])
            for r in range(NCORE):
                S.op("dve", lambda h, r=r: h.tensor_tensor(out=tl[:, :], in0=G8[:, r, 0:16], in1=H0L[:, :], op=ALU.mult),
                     reads=["G8", "H0L"], writes=["tl"])
                S.op("dve", lambda h, r=r: h.tensor_tensor(out=tl[:, :], in0=tl[:, :], in1=G8[:, r, 16:32], op=ALU.add),
                     reads=["tl", "G8"], writes=["tl"])
                S.op("dve", lambda h: h.tensor_tensor(out=tl[:, :], in0=tl[:, :], in1=H0L[:, :], op=ALU.subtract),
                     reads=["tl", "H0L"], writes=["tl"])
                S.op("dve", lambda h, r=r: h.scalar_tensor_tensor(out=H0L[:, :], in0=tl[:, :], scalar=FL[:, 9 + r:10 + r], in1=H0L[:, :],
                                                                  op0=ALU.mult, op1=ALU.add),
                     reads=["tl", "H0L", "FL"], writes=["H0L"])
            AP_ = [self.T("AP0", [64, 2, 128]), self.T("AP1", [64, 2, 128])]
            tq = self.T("tqs", [64, 128]); tq2 = self.T("tqs2", [64, 128])
            S.op("dve", lambda h: h.tensor_copy(out=AP_[0][:, :, :], in_=self.LBAR[:, :, :]), writes=["AP0"])
            cur = 0
            for it in range(10):
                nx = 1 - cur
                a, b = AP_[cur], AP_[nx]
                S.op("dve", lambda h, a=a: h.tensor_tensor(out=tq[:, :], in0=a[:, 0, :], in1=a[:, 0, :], op=ALU.mult), reads=[f"AP{cur}"], writes=["tq"])
                S.op("dve", lambda h, a=a: h.tensor_tensor(out=tq2[:, :], in0=a[:, 1, :], in1=a[:, 1, :], op=ALU.mult), reads=[f"AP{cur}"], writes=["tq2"])
                S.op("dve", lambda h, b=b: h.tensor_tensor(out=b[:, 0, :], in0=tq[:, :], in1=tq2[:, :], op=ALU.subtract), reads=["tq", "tq2"], writes=[f"AP{nx}"])
                S.op("dve", lambda h, a=a, b=b: h.scalar_tensor_tensor(out=b[:, 1, :], in0=a[:, 0, :], scalar=2.0, in1=a[:, 1, :],
                                                                       op0=ALU.mult, op1=ALU.mult), reads=[f"AP{cur}"], writes=[f"AP{nx}"])
                cur = nx
            A = AP_[cur]
            ak = f"AP{cur}"
            H0S = self.H0S
            ts = self.T("ts5", [64, 2, 128]); tu = self.T("tu5", [64, 128])
            S.op("dve", lambda h: h.memset(H0S[:, :, :], 0.0), writes=["H0S"])
            for r in range(NCORE):
                E = G8[0:64, r, 32:288].rearrange("p (r g) -> p r g", r=2)
                S.op("dve", lambda h: h.tensor_tensor(out=ts[:, 0, :], in0=A[:, 0, :], in1=H0S[:, 0, :], op=ALU.mult), reads=[ak, "H0S"], writes=["ts0"])
                S.op("dve", lambda h: h.tensor_tensor(out=tu[:, :], in0=A[:, 1, :], in1=H0S[:, 1, :], op=ALU.mult), reads=[ak, "H0S"], writes=["tu"])
                S.op("dve", lambda h: h.tensor_tensor(out=ts[:, 0, :], in0=ts[:, 0, :], in1=tu[:, :], op=ALU.subtract), reads=["ts0", "tu"], writes=["ts0"])
                S.op("dve", lambda h: h.tensor_tensor(out=ts[:, 1, :], in0=A[:, 0, :], in1=H0S[:, 1, :], op=ALU.mult), reads=[ak, "H0S"], writes=["ts1"])
                S.op("dve", lambda h: h.tensor_tensor(out=tu[:, :], in0=A[:, 1, :], in1=H0S[:, 0, :], op=ALU.mult), reads=[ak, "H0S", "ts0"], writes=["tu"])
                S.op("dve", lambda h: h.tensor_tensor(out=ts[:, 1, :], in0=ts[:, 1, :], in1=tu[:, :], op=ALU.add), reads=["ts1", "tu"], writes=["ts1"])
                S.op("dve", lambda h, E=E: h.tensor_tensor(out=ts[:, :, :], in0=ts[:, :, :], in1=E, op=ALU.add), reads=["ts0", "ts1", "G8"], writes=["ts0", "ts1"])
                S.op("dve", lambda h: h.tensor_tensor(out=ts[:, :, :], in0=ts[:, :, :], in1=H0S[:, :, :], op=ALU.subtract), reads=["ts0", "ts1", "H0S"], writes=["ts0", "ts1"])
                S.op("dve", lambda h, r=r: h.scalar_tensor_tensor(out=H0S[:, :, :], in0=ts[:, :, :], scalar=FL[0:64, 9 + r:10 + r], in1=H0S[:, :, :],
                                                                  op0=ALU.mult, op1=ALU.add), reads=["ts0", "ts1", "H0S", "FL"], writes=["H0S"])
            self.dbg(S, "h0l", H0L[:, :], [128, 16], reads=["H0L"])
            self.dbg(S, "h0s", H0S[:, :, :], [64, 2, 128], reads=["H0S"])

    def p_glu(self):
        PS, WA, U = self.PS, self.WA, self.U
        ys_d = self.nc.dram_tensor("ys_d", [16, 128, NT], BF16).ap()
        with self.phase() as S:
            yst = [self.T("yst0", [128, NT], BF16), self.T("yst1", [128, NT], BF16)]
            BG = self.T("BG", [128, 16])
            self.load_cols(S, BG, self.b_glu[0].rearrange("(c p) -> c p", p=128), 16, "BG", "tbg")
            sg = [self.T("sg0", [128, NT]), self.T("sg1", [128, NT])]
            uk = [("U", ct) for ct in range(16)]
            for ot in range(16):
                slot = self.wslot()
                wv = WA[:, slot, :].rearrange("p (k c) -> p k c", c=128)
                self.wload(S, wv[:, 0:16, :], self.w_glu[0][:, ot * 128:(ot + 1) * 128].rearrange("(k p) c -> p k c", p=128), slot)
                pb = self.big()
                self.fm_matmul(S, pb, wv, 16, lambda kt, b: U[:, kt, b * CW:(b + 1) * CW], uk, slot)
                s = sg[ot % 2]
                S.op("act", lambda h, pb=pb, ot=ot, s=s: h.activation(out=v3(s[:, :]), in_=PS[:, pb:pb + 3, 0:CW], func=AF.Sigmoid,
                                                                      bias=BG[:, ot:ot + 1]),
                     reads=[("ps", pb + b) for b in range(3)] + ["BG"], writes=[("sg", ot % 2)])
                yt = yst[ot % 2]
                S.op("dve", lambda h, ot=ot, s=s, yt=yt: h.tensor_tensor(out=yt[:, :], in0=s[:, :], in1=U[:, ot, :], op=ALU.mult),
                     reads=[("sg", ot % 2), ("U", ot)], writes=[("yst", ot % 2)])
                S.dma("sp", lambda h, ot=ot, yt=yt: h.dma_start(out=ys_d[ot], in_=yt[:, :]), reads=[("yst", ot % 2)], writes=["ys_d"])
        with self.phase() as S:
            S.dma("sp", lambda h: h.dma_start(out=U[:, :, :], in_=ys_d.rearrange("k p t -> p k t")), writes=[("ys", ct) for ct in range(16)])
            self.dbg(S, "ys", U[:, :, :], [128, 16, NT], BF16, reads=[("ys", ct) for ct in range(16)])

    def p_merge(self):
        PS, WA, BIG1, ya, ys = self.PS, self.WA, self.BIG1, self.ya, self.ys
        w_in = self.w_in[0]
        hk = [("big1", ft) for ft in range(KT)]
        with self.phase() as S:
            sa = self.T("sa", [128, NT]); sb_ = self.T("sb", [128, NT])
            m1 = self.T("m1", [128, NT]); m2 = self.T("m2", [128, NT])
            mg = [self.T("mg0", [128, NT], BF16), self.T("mg1", [128, NT], BF16)]
            yak = [("ya", ct) for ct in range(16)]
            ysk = [("ys", ct) for ct in range(16)]
            for mt in range(KT):
                def gate(col0, dst, key):
                    slot = self.wslot()
                    wv = WA[:, slot, :].rearrange("p (k c) -> p k c", c=128)
                    self.wload(S, wv, w_in[:, col0 + mt * 128:col0 + (mt + 1) * 128].rearrange("(k p) c -> p k c", p=128), slot)
                    pb = self.big()
                    self.fm_matmul(S, pb, wv, KT, lambda kt, b: BIG1[:, kt, b * CW:(b + 1) * CW], hk, slot)
                    S.op("act", lambda h, pb=pb: h.activation(out=v3(dst[:, :]), in_=PS[:, pb:pb + 3, 0:CW], func=AF.Sigmoid),
                         reads=[("ps", pb + b) for b in range(3)], writes=[key])

                def proj(wd, src, skeys, gt, gkey, dst, dkey):
                    slot = self.wslot()
                    wv = WA[:, slot, :].rearrange("p (k c) -> p k c", c=128)
                    self.wload(S, wv[:, 0:16, :], wd[:, mt * 128:(mt + 1) * 128].rearrange("(k p) c -> p k c", p=128), slot)
                    pb = self.big()
                    self.fm_matmul(S, pb, wv, 16, lambda kt, b: src[:, kt, b * CW:(b + 1) * CW], skeys, slot)
                    S.op("dve", lambda h, pb=pb: h.tensor_tensor(out=v3(dst[:, :]), in0=PS[:, pb:pb + 3, 0:CW], in1=v3(gt[:, :]), op=ALU.mult),
                         reads=[("ps", pb + b) for b in range(3)] + [gkey], writes=[dkey])

                gate(6144, sa, "sa")
                proj(self.w_proj_lru[0], ya, yak, sa, "sa", m1, "m1")
                gate(10240, sb_, "sb")
                proj(self.w_proj_ssm[0], ys, ysk, sb_, "sb", m2, "m2")
                mo = mg[mt % 2]
                S.op("dve", lambda h, mo=mo: h.tensor_tensor(out=mo[:, :], in0=m1[:, :], in1=m2[:, :], op=ALU.add),
                     reads=["m1", "m2"], writes=[("mg", mt % 2)])
                S.dma("sp", lambda h, mo=mo, mt=mt: h.dma_start(out=self.mg_d[mt], in_=mo[:, :]), reads=[("mg", mt % 2)], writes=["mg_d"])

    def p_wout(self):
        PS, WA, BIG1 = self.PS, self.WA, self.BIG1
        w_out = self.w_out[0]
        with self.phase() as S:
            S.dma("sp", lambda h: h.dma_start(out=BIG1[:, :, :], in_=self.mg_d.rearrange("k p t -> p k t")), writes=["mgall"])
            GB = self.T("GB", [128, D])
            self.bcast_cond(S, GB, 64)
            xr = [self.T(f"xres{i}", [128, 512]) for i in range(3)]
            yo = [self.T(f"yo{i}", [128, 512]) for i in range(3)]
            tiles = [(6, 2)] + [(HALO + 128 * i, 128) for i in range(8)]
            cnt = 0
            for batch in (tiles[0:5], tiles[5:9]):
                for nch in range(8):
                    for kg in range(4):
                        slot = self.wslot()
                        wv = WA[:, slot, :].rearrange("p (k c) -> p k c", c=512)
                        self.wload(S, wv, w_out[kg * 1024:(kg + 1) * 1024, nch * 512:(nch + 1) * 512].rearrange("(k p) c -> p k c", p=128), slot)
                        for ti, (r0, M) in enumerate(batch):
                            for kt in range(8):
                                k = kg * 8 + kt
                                S.op("pe", lambda h, ti=ti, r0=r0, M=M, k=k, kt=kt, wv=wv, kg=kg: h.matmul(
                                    PS[0:M, ti, :], BIG1[:, k, r0:r0 + M], wv[:, kt, :], start=(k == 0), stop=(k == KT - 1)),
                                     reads=["mgall", ("wa", slot)], writes=[("ps", ti)])
                    for ti, (r0, M) in enumerate(batch):
                        i3 = cnt % 3
                        cnt += 1
                        xt, yt = xr[i3], yo[i3]
                        S.dma("act", lambda h, xt=xt, r0=r0, M=M, nch=nch: h.dma_start(out=xt[0:M, :], in_=self.xh[r0:r0 + M, nch * 512:(nch + 1) * 512]),
                              writes=[("xres", i3)])
                        S.op("dve", lambda h, yt=yt, ti=ti, M=M, nch=nch: h.tensor_tensor(out=yt[0:M, :], in0=PS[0:M, ti, :],
                                                                                          in1=GB[0:M, nch * 512:(nch + 1) * 512], op=ALU.mult),
                             reads=[("ps", ti), "GB"], writes=[("yo", i3)])
                        S.op("dve", lambda h, yt=yt, xt=xt, M=M: h.scalar_tensor_tensor(out=yt[0:M, :], in0=xt[0:M, :], scalar=ALPHA, in1=yt[0:M, :],
                                                                                          op0=ALU.mult, op1=ALU.add),
                             reads=[("yo", i3), ("xres", i3)], writes=[("yo", i3)])
                        S.dma("sp", lambda h, yt=yt, r0=r0, M=M, nch=nch: h.dma_start(out=self.y1_d[r0:r0 + M, nch * 512:(nch + 1) * 512], in_=yt[0:M, :]),
                              reads=[("yo", i3)], writes=["y1_d"])

    def p_ln_mid(self):
        with self.phase() as S:
            self.lnJ = self.T("lnJ", [128, D], BF16)
            self.lnJ2 = self.T("lnJ2", [128, D])
            self.lnst = self.T("lnst", [128, 4])
            G = self.T("lnG", [128, D]); Bt = self.T("lnB", [128, D])
            S.dma("sp", lambda h: h.dma_start(out=G[:, :], in_=self.ln1_g.partition_broadcast(128)), writes=["lnG"])
            S.dma("sp", lambda h: h.dma_start(out=Bt[:, :], in_=self.ln1_b.partition_broadcast(128)), writes=["lnB"])
            XT = [self.T(f"XT{i}", [128, D]) for i in range(2)]
            nev = [0]
            tiles = [(6, 2)] + [(HALO + 128 * i, 128) for i in range(8)]
            for ti, (r0, M) in enumerate(tiles):
                X = XT[ti % 2]
                tag = ti % 2
                S.dma("act", lambda h, X=X, r0=r0, M=M: h.dma_start(out=X[0:M, :], in_=self.y1_d[r0:r0 + M, :]), writes=[("X", tag)])
                self.ln_rows(S, X, M, tag, affine=(G, Bt))
                S.dma("sp", lambda h, X=X, r0=r0, M=M: h.dma_start(out=self.x1_d[r0:r0 + M, :], in_=X[0:M, :]), reads=[("X", tag)], writes=["x1_d"])
                self.ln_rows(S, X, M, tag)
                self.rows_to_T(S, X, M, tag, r0, 128, 96, nev)
            self.dbg(S, "h2T", self.BIG1[:, :, :], [128, KT, NT], BF16, reads=[("big1", ft) for ft in range(KT)])

    def p_ffn(self):
        PS, WA, BIG1, FL = self.PS, self.WA, self.BIG1, self.FL
        w_up = self.ffn_w_up[0]
        w_dn = self.ffn_w_down[0]
        NH = 514
        HC = 257
        hk = [("big1", ft) for ft in range(KT)]
        with self.phase() as S:
            FCW = self.T("FCW", [128, 516])
            FCB = self.T("FCB", [128, 172])
            self.load_cols(S, FCW, self.ffn_conv_w[0].rearrange("k (c p) -> (k c) p", p=128), 516, "FCW", "tfw")
            self.load_cols(S, FCB, self.ffn_conv_b[0].rearrange("(c p) -> c p", p=128), 172, "FCB", "tfb")
            GBc = self.T("GBc", [128, 512])
            Dg = self.T("ffn_diag", [128, 2, 128])
            aT = self.T("aT", [128, NFT, 512], BF16)
            us = [self.T(f"us{i}", [128, NH]) for i in range(2)]
            cg = self.T("cg", [128, 512]); cv = self.T("cv", [128, 512])
            xr = [self.T(f"xres{i}", [128, 512]) for i in range(2)]
            yo = [self.T(f"yo{i}", [128, 512]) for i in range(2)]
            cnt = 0
            for hf in range(2):
                c0 = 6 + 512 * hf
                for j in range(NFT):
                    for which in range(2):
                        tile = which * NFT + j
                        slot = self.wslot()
                        wv = WA[:, slot, :].rearrange("p (k c) -> p k c", c=128)
                        self.wload(S, wv, w_up[:, tile * 128:(tile + 1) * 128].rearrange("(k p) c -> p k c", p=128), slot)
                        pb = 2 * (self.ps_n % 3)
                        self.ps_n += 1
                        for kt in range(KT):
                            for b in range(2):
                                S.op("pe", lambda h, kt=kt, b=b, pb=pb, wv=wv: h.matmul(
                                    PS[:, pb + b, 0:HC], wv[:, kt, :], BIG1[:, kt, c0 + b * HC:c0 + (b + 1) * HC],
                                    start=(kt == 0), stop=(kt == KT - 1)),
                                     reads=[("wa", slot)] + hk, writes=[("ps", pb + b)])
                        u = us[which]
                        uk = ("us", which)
                        S.op("act", lambda h, u=u, pb=pb: h.activation(out=u[:, :].rearrange("p (b c) -> p b c", c=HC), in_=PS[:, pb:pb + 2, 0:HC],
                                                                       func=AF.Identity),
                             reads=[("ps", pb), ("ps", pb + 1)], writes=[uk])
                        if hf == 0:
                            S.op("dve", lambda h, u=u: h.tensor_scalar(out=u[:, 0:2], in0=u[:, 0:2], scalar1=FL[:, 0:1], scalar2=None, op0=ALU.mult),
                                 reads=[uk, "FL"], writes=[uk])
                        dst = cg if which == 0 else cv
                        dk = "cg" if which == 0 else "cv"
                        fw = lambda k, tile=tile: FCW[:, k * 172 + tile:k * 172 + tile + 1]
                        S.op("dve", lambda h, u=u, dst=dst, fw=fw, tile=tile: h.tensor_scalar(out=dst[:, :], in0=u[:, 0:512], scalar1=fw(0),
                                                                                              scalar2=FCB[:, tile:tile + 1], op0=ALU.mult, op1=ALU.add),
                             reads=[uk, "FCW", "FCB"], writes=[dk])
                        for k in (1, 2):
                            S.op("dve", lambda h, u=u, dst=dst, fw=fw, k=k: h.scalar_tensor_tensor(out=dst[:, :], in0=u[:, k:k + 512], scalar=fw(k),
                                                                                                   in1=dst[:, :], op0=ALU.mult, op1=ALU.add),
                                 reads=[uk, dk, "FCW"], writes=[dk])
                    S.op("act", lambda h: h.activation(out=cg[:, :], in_=cg[:, :], func=AF.Gelu), reads=["cg"], writes=["cg"])
                    S.op("dve", lambda h, j=j: h.tensor_tensor(out=aT[:, j, :], in0=cg[:, :], in1=cv[:, :], op=ALU.mult),
                         reads=["cg", "cv"], writes=[("aT", j)])
                ak = [("aT", j) for j in range(NFT)]
                for nch in range(8):
                    ngr = (NFT + 7) // 8
                    for kg in range(ngr):
                        nk = min(8, NFT - kg * 8)
                        slot = self.wslot()
                        wv = WA[:, slot, :].rearrange("p (k c) -> p k c", c=512)
                        self.wload(S, wv[:, 0:nk, :], w_dn[kg * 1024:kg * 1024 + nk * 128, nch * 512:(nch + 1) * 512].rearrange("(k p) c -> p k c", p=128), slot)
                        for ti in range(4):
                            for kt in range(nk):
                                k = kg * 8 + kt
                                S.op("pe", lambda h, ti=ti, k=k, kt=kt, wv=wv: h.matmul(
                                    PS[:, 4 + ti, :], aT[:, k, ti * 128:(ti + 1) * 128], wv[:, kt, :], start=(k == 0), stop=(k == NFT - 1)),
                                     reads=ak + [("wa", slot)], writes=[("ps", 4 + ti)])
                    for q4 in range(4):
                        ft = nch * 4 + q4
                        s2 = q4 % 2
                        S.op("dve", lambda h, ft=ft, s2=s2: h.tensor_scalar(out=Dg[:, s2, :], in0=self.ident[:, :],
                                                                             scalar1=self.COND[:, 160 + ft:161 + ft], scalar2=None, op0=ALU.mult),
                             reads=["ident", "COND"], writes=[("fdg", s2)])
                        S.op("pe", lambda h, s2=s2: h.matmul(PS[:, 2 + s2, 0:128], self.ones[:, :], Dg[:, s2, :], start=True, stop=True),
                             reads=[("fdg", s2), "ones"], writes=[("ps", 2 + s2)])
                        S.op("act", lambda h, q4=q4, s2=s2: h.activation(out=GBc[:, q4 * 128:(q4 + 1) * 128], in_=PS[:, 2 + s2, 0:128], func=AF.Identity),
                             reads=[("ps", 2 + s2)], writes=["GBc"])
                    for ti in range(4):
                        r0 = HALO + 512 * hf + 128 * ti
                        i3 = cnt % 2
                        cnt += 1
                        xt, yt = xr[i3], yo[i3]
                        S.dma("act", lambda h, xt=xt, r0=r0, nch=nch: h.dma_start(out=xt[:, :], in_=self.x1_d[r0:r0 + 128, nch * 512:(nch + 1) * 512]),
                              writes=[("xres", i3)])
                        S.op("dve", lambda h, yt=yt, ti=ti, nch=nch: h.tensor_tensor(out=yt[:, :], in0=PS[:, 4 + ti, :],
                                                                                     in1=GBc[:, :], op=ALU.mult),
                             reads=[("ps", 4 + ti), "GBc"], writes=[("yo", i3)])
                        S.op("dve", lambda h, yt=yt, xt=xt: h.scalar_tensor_tensor(out=yt[:, :], in0=xt[:, :], scalar=ALPHA, in1=yt[:, :],
                                                                                     op0=ALU.mult, op1=ALU.add),
                             reads=[("yo", i3), ("xres", i3)], writes=[("yo", i3)])
                        S.dma("sp", lambda h, yt=yt, r0=r0, nch=nch: h.dma_start(out=self.y2_d[r0:r0 + 128, nch * 512:(nch + 1) * 512], in_=yt[:, :]),
                              reads=[("yo", i3)], writes=["y2_d"])

    def p_ln_final(self):
        with self.phase() as S:
            self.lnJ = self.T("lnJ", [128, D], BF16)
            self.lnJ2 = self.T("lnJ2", [128, D])
            self.lnst = self.T("lnst", [128, 4])
            G = self.T("lnG", [128, D]); Bt = self.T("lnB", [128, D])
            S.dma("sp", lambda h: h.dma_start(out=G[:, :], in_=self.ln2_g.partition_broadcast(128)), writes=["lnG"])
            S.dma("sp", lambda h: h.dma_start(out=Bt[:, :], in_=self.ln2_b.partition_broadcast(128)), writes=["lnB"])
            XT = [self.T(f"XT{i}", [128, D]) for i in range(2)]
            for ti in range(8):
                r0 = HALO + 128 * ti
                X = XT[ti % 2]
                tag = ti % 2
                S.dma("act", lambda h, X=X, r0=r0: h.dma_start(out=X[:, :], in_=self.y2_d[r0:r0 + 128, :]), writes=[("X", tag)])
                self.ln_rows(S, X, 128, tag, affine=(G, Bt))
                S.dma("sp", lambda h, X=X, ti=ti: h.dma_start(out=self.out[ti * 128:(ti + 1) * 128, :], in_=X[:, :]), reads=[("X", tag)], writes=["out"])


WEIGHT_NAMES = ["w_in", "conv_lru_w", "conv_lru_b", "lru_wr", "lru_br", "lru_wi", "lru_bi",
                "lru_lambda", "ssm_lam_re", "ssm_lam_im", "ssm_b_re", "ssm_b_im", "ssm_c_re", "ssm_c_im", "ssm_d",
                "ssm_log_step", "w_glu", "b_glu", "w_proj_lru", "w_proj_ssm", "w_out", "ln1_g", "ln1_b",
                "ffn_w_up", "ffn_conv_w", "ffn_conv_b", "ffn_w_down", "ln2_g", "ln2_b"]


def make_in_maps(inputs, used=None, ncore=NCORE):
    x = np.asarray(inputs["x"], dtype=np.float32)
    c = np.asarray(inputs["c"], dtype=np.float32)
    used = set(WEIGHT_NAMES + ["xh", "cT2", "flags", "w_ada_r", "b_ada_r"]) if used is None else set(used)
    w_ada = np.asarray(inputs["w_ada"], dtype=np.float32)[0]
    b_ada = np.asarray(inputs["b_ada"], dtype=np.float32)[0]
    shared = {k: np.ascontiguousarray(np.asarray(inputs[k], dtype=np.float32)) for k in WEIGHT_NAMES if k in used}
    in_maps = []
    for k in range(ncore):
        b, j = k // 4, k % 4
        s = 1024 * j
        xh = np.zeros((NT, D), np.float32)
        if j == 0:
            xh[HALO:] = x[b, 0:1024]
        else:
            xh[:] = x[b, s - HALO:s + 1024]
        cT2 = np.ascontiguousarray(np.stack([c[0].reshape(KT, 128).T, c[1].reshape(KT, 128).T], axis=-1))
        r8 = k if ncore == NCORE else k
        w_ada_r = np.ascontiguousarray(w_ada[:, 3072 * r8:3072 * (r8 + 1)])
        b_ada_r = np.ascontiguousarray(b_ada[3072 * r8:3072 * (r8 + 1)].reshape(24, 128))
        flags = np.zeros((1, 18), np.float32)
        flags[0, 17] = float(b)
        flags[0, 0] = 0.0 if j == 0 else 1.0
        nmask = 8 if j == 0 else 3
        flags[0, 1:9] = 1.0
        flags[0, 1:1 + nmask] = 0.0
        for r in range(NCORE):
            if r // 4 == b and r < k:
                flags[0, 9 + r] = 1.0
        m = dict(shared)
        for nm, arr in (("xh", xh), ("cT2", cT2), ("flags", flags), ("w_ada_r", w_ada_r), ("b_ada_r", b_ada_r)):
            if nm in used:
                m[nm] = arr
        in_maps.append(m)
    return in_maps


def kernel(**inputs):
    bld = Builder()
    nc = bld.build()
    in_maps = make_in_maps(inputs)
    res = run_bass_kernel_spmd(nc, in_maps, core_ids=list(range(NCORE)))
    out = np.zeros((2, 4096, D), np.float32)
    for k in range(NCORE):
        b, j = k // 4, k % 4
        out[b, 1024 * j:1024 * (j + 1)] = np.asarray(res.results[k]["out"], dtype=np.float32)
    return out
```

```python
import contextlib
import math
import numpy as np
import concourse.bass as bass
import concourse.mybir as mybir
from concourse.bass_utils import run_bass_kernel_spmd

F32 = mybir.dt.float32
BF16 = mybir.dt.bfloat16
AF = mybir.ActivationFunctionType
ALU = mybir.AluOpType
AX = mybir.AxisListType

NCORE = 8
D = 4096
KT = 32
NT = 1032
HALO = 8
CW = 344
WL = 2048
DFF = 11008
NFT = 86
NIN = 14336
LN_EPS = 1e-5
ALPHA = 2.0 ** 0.25
CB = 24
NBK = NT // CB
EXP_POS = 1026
S5_EXP = 1023

ENGS = ("pe", "act", "dve", "pool", "sp")
SEG = 12000
NDSEM = 12


class _Rec:
    def __init__(self):
        self.call = None

    def __getattr__(self, name):
        def f(*a, **k):
            assert self.__dict__["call"] is None
            self.__dict__["call"] = (name, a, k)
            return self
        return f


def _freeze(fn):
    if fn is None:
        return None
    rec = _Rec()
    fn(rec)
    name, a, k = rec.call
    return lambda h: getattr(h, name)(*a, **k)


class Sched:
    def __init__(self, nc):
        self.nc = nc
        self.ops = {e: [] for e in ENGS}
        self.lastw = {}
        self.readers = {}
        self.ndma = {e: 0 for e in ENGS}

    def _add(self, eng, fn, reads, writes, dma):
        writes = list(writes)
        for k in reads:
            if isinstance(k, tuple) and k[0] == "ps" and k not in writes:
                writes.append(k)
        deps = []
        rawset = set()
        for k in reads:
            t = self.lastw.get(k)
            if t is not None:
                deps.append(t)
                rawset.add(t)
        for k in writes:
            t = self.lastw.get(k)
            if t is not None:
                deps.append(t)
            deps.extend(self.readers.get(k, ()))
        idx = len(self.ops[eng])
        if dma:
            k = self.ndma[eng]
            self.ndma[eng] += 1
            tok = ("d", eng, k)
            if k >= NDSEM:
                deps.append(("d", eng, k - NDSEM))
        else:
            tok = ("e", eng, idx)
        d2 = []
        for t in set(deps):
            if t[0] == "e" and t[1] == eng:
                if dma:
                    d2.append(t)
                    continue
                if t not in rawset or eng == "pe":
                    continue
            d2.append(t)
        self.ops[eng].append(dict(fn=_freeze(fn), deps=d2, dma=dma, inc=False, tok=tok))
        for k in reads:
            self.readers.setdefault(k, []).append(tok)
        for k in writes:
            self.lastw[k] = tok
            self.readers[k] = []
        return tok

    def op(self, eng, fn, reads=(), writes=()):
        return self._add(eng, fn, reads, writes, False)

    def barrier(self):
        toks = []
        for e in ENGS:
            for i in range(len(self.ops[e]) - 1, -1, -1):
                o = self.ops[e][i]
                if not o["dma"] and o["fn"] is not None:
                    toks.append(("e", e, i))
                    break
            n = self.ndma[e]
            for k in range(max(0, n - NDSEM), n):
                toks.append(("d", e, k))
        for e in ENGS:
            self.ops[e].append(dict(fn=None, deps=[t for t in toks if not (t[0] == "e" and t[1] == e)],
                                    dma=False, inc=False, tok=None))
        self.lastw = {}
        self.readers = {}

    def dma(self, eng, fn, reads=(), writes=()):
        return self._add(eng, fn, reads, writes, True)

    def emit(self, final_wait_eng="sp"):
        nc = self.nc
        for e in ENGS:
            for o in self.ops[e]:
                for t in o["deps"]:
                    if t[0] == "e":
                        self.ops[t[1]][t[2]]["inc"] = True
        for e in ENGS:
            for o in reversed(self.ops[e]):
                if not o["dma"] and o["fn"] is not None:
                    o["inc"] = True
                    break
        cnt = {}
        nseg = {}
        for e in ENGS:
            c = 0
            for i, o in enumerate(self.ops[e]):
                if o["inc"] and not o["dma"]:
                    cnt[(e, i)] = c
                    c += 1
            nseg[e] = max(1, (c + SEG - 1) // SEG)
        esem = {e: [nc.alloc_semaphore(name=nc.make_name(f"s_{e}_{j}", True)) for j in range(nseg[e])]
                for e in ENGS}
        dsem = {e: [nc.alloc_semaphore(name=nc.make_name(f"d_{e}_{j}", True)) for j in range(NDSEM)]
                for e in ENGS if self.ndma[e] > 0}

        def target(t):
            if t[0] == "e":
                c = cnt[(t[1], t[2])]
                return esem[t[1]][c // SEG], (c % SEG) + 1, ("e", t[1], c // SEG)
            k = t[2]
            return dsem[t[1]][k % NDSEM], 16 * (k // NDSEM + 1), ("d", t[1], k % NDSEM)

        def run(e, h):
            waited = {}
            for i, o in enumerate(self.ops[e]):
                for t in o["deps"]:
                    s, v, key = target(t)
                    if waited.get(key, 0) >= v:
                        continue
                    waited[key] = v
                    h.wait_ge(s, v)
                if o["fn"] is None:
                    continue
                ins = o["fn"](h)
                if o["dma"]:
                    s, v, _ = target(o["tok"])
                    ins.then_inc(s, 16)
                elif o["inc"]:
                    c = cnt[(e, i)]
                    ins.then_inc(esem[e][c // SEG], 1)
            if e == final_wait_eng:
                for e2 in ENGS:
                    for i2 in range(len(self.ops[e2]) - 1, -1, -1):
                        if not self.ops[e2][i2]["dma"] and self.ops[e2][i2]["fn"] is not None:
                            s, v, _ = target(("e", e2, i2))
                            h.wait_ge(s, v)
                            break
                    n = self.ndma[e2]
                    for k in range(max(0, n - NDSEM), n):
                        s, v, _ = target(("d", e2, k))
                        h.wait_ge(s, v)

        with nc.Block() as block:
            hmap = {"pe": block.tensor, "act": block.scalar, "dve": block.vector,
                    "pool": block.gpsimd, "sp": block.sync}
            for e in ENGS:
                if not self.ops[e] and e != final_wait_eng:
                    continue
                hmap[e](lambda h, e=e: run(e, h))


def v3(ap2d):
    return ap2d.rearrange("p (b c) -> p b c", c=CW)


class Builder:
    def __init__(self, stage=99, debug=(), ncore=NCORE):
        self.ncore = ncore
        self.stage = stage
        self.debug = set(debug)
        self.nc = nc = bass.Bass("TRN2", target_bir_lowering=False)
        self.dbg_out = {}
        self._ishape = {
            "xh": [NT, D],
            "cT": [128, KT],
            "flags": [1, 18],
            "w_ada": [1, D, 6 * D],
            "b_ada": [1, 6 * D],
            "w_in": [1, D, NIN],
            "conv_lru_w": [1, 4, WL],
            "conv_lru_b": [1, WL],
            "lru_wr": [1, 16, 128, 128],
            "lru_br": [1, 16, 128],
            "lru_wi": [1, 16, 128, 128],
            "lru_bi": [1, 16, 128],
            "lru_lambda": [1, WL],
            "ssm_lam_re": [1, 128, 64],
            "ssm_lam_im": [1, 128, 64],
            "ssm_b_re": [1, 128, 64, 16],
            "ssm_b_im": [1, 128, 64, 16],
            "ssm_c_re": [1, 128, 16, 64],
            "ssm_c_im": [1, 128, 16, 64],
            "ssm_d": [1, WL],
            "ssm_log_step": [1, 128],
            "w_glu": [1, WL, WL],
            "b_glu": [1, WL],
            "w_proj_lru": [1, WL, D],
            "w_proj_ssm": [1, WL, D],
            "w_out": [1, D, D],
            "ln1_g": [1, D],
            "ln1_b": [1, D],
            "ffn_w_up": [1, D, 2 * DFF],
            "ffn_conv_w": [1, 3, 2 * DFF],
            "ffn_conv_b": [1, 2 * DFF],
            "ffn_w_down": [1, DFF, D],
            "ln2_g": [1, D],
            "ln2_b": [1, D],
        }
        self._idecl = {}
        self.out = nc.dram_tensor("out", [1024, D], F32, kind="ExternalOutput").ap()
        self.ag_in = nc.dram_tensor("ag_in", [128, 288], F32)
        self.ag_out = nc.dram_tensor("ag_out", [ncore * 128, 288], F32)
        self.mg_d = nc.dram_tensor("mg_d", [KT, 128, NT], BF16).ap()
        self.y1_d = nc.dram_tensor("y1_d", [NT, D], F32).ap()
        self.x1_d = nc.dram_tensor("x1_d", [NT, D], F32).ap()
        self.y2_d = nc.dram_tensor("y2_d", [NT, D], F32).ap()
        self.wa_n = 0
        self.ps_n = 0

    def __getattr__(self, name):
        ish = self.__dict__.get("_ishape", {})
        if name in ish:
            d = self.__dict__["_idecl"]
            if name not in d:
                d[name] = self.nc.dram_tensor(name, list(ish[name]), F32, kind="ExternalInput").ap()
            return d[name]
        raise AttributeError(name)

    def T(self, name, shape, dt=F32):
        return self.nc.alloc_sbuf_tensor(self.nc.make_name(name, True), list(shape), dt)

    def dbg(self, S, name, src_ap, shape, dt=F32, reads=()):
        if name not in self.debug:
            return
        t = self.nc.dram_tensor("dbg_" + name, list(shape), dt, kind="ExternalOutput").ap()
        self.dbg_out[name] = "dbg_" + name
        if len(shape) == 3 and shape[1] * shape[2] > 4096:
            for i in range(shape[1]):
                S.dma("sp", lambda h, i=i: h.dma_start(out=t[:, i, :], in_=src_ap[:, i, :]), reads=list(reads))
        else:
            S.dma("sp", lambda h: h.dma_start(out=t, in_=src_ap), reads=list(reads))

    @contextlib.contextmanager
    def scope(self):
        nc = self.nc
        saved = (nc.sbuf_base, nc.sbuf_top)
        yield
        self.S.barrier()
        nc.sbuf_base, nc.sbuf_top = saved

    @contextlib.contextmanager
    def phase(self):
        with self.scope():
            yield self.S

    def wslot(self):
        s = self.wa_n % 2
        self.wa_n += 1
        return s

    def wload(self, S, dst, src, slot):
        S.dma("pool", lambda h: h.dma_start(out=dst, in_=src), writes=[("wa", slot)])

    def load_cols(self, S, dst, src_rows, n, key, tmpname):
        PS, ident = self.PS, self.ident
        done = 0
        i = 0
        while done < n:
            m = min(128, n - done)
            tmp = self.T(f"{tmpname}{i}", [128, 128])
            k1 = (tmpname, i)
            S.dma("sp", lambda h, tmp=tmp, m=m, d=done: h.dma_start(out=tmp[0:m, :], in_=src_rows[d:d + m, :]),
                  writes=[k1])
            S.op("pe", lambda h, tmp=tmp, m=m: h.transpose(out=PS[:, 7, 0:m], in_=tmp[0:m, :], identity=ident[0:m, 0:m]),
                 reads=[k1, "ident"], writes=[("ps", 7)])
            S.op("dve", lambda h, m=m, d=done: h.tensor_copy(out=dst[:, d:d + m], in_=PS[:, 7, 0:m]),
                 reads=[("ps", 7)], writes=[key])
            done += m
            i += 1

    def fm_matmul(self, S, psb, wv, nk, rhs_fn, rkeys, slot):
        PS = self.PS
        for kt in range(nk):
            for b in range(3):
                S.op("pe", lambda h, kt=kt, b=b: h.matmul(PS[:, psb + b, 0:CW], wv[:, kt, :], rhs_fn(kt, b),
                                                          start=(kt == 0), stop=(kt == nk - 1)),
                     reads=[("wa", slot)] + list(rkeys), writes=[("ps", psb + b)])

    def big(self):
        b = 3 * (self.ps_n % 2)
        self.ps_n += 1
        return b

    def ln_rows(self, S, X, M, tag, affine=None):
        J = self.lnJ
        J2 = self.lnJ2
        st = self.lnst
        kx = ("X", tag)
        S.op("dve", lambda h: h.tensor_scalar(out=J[0:M, :], in0=X[0:M, :], scalar1=1.0 / D, scalar2=None,
                                              op0=ALU.mult, op1=ALU.add, accum_out=st[0:M, 0:1]),
             reads=[kx], writes=["lnJ", "st0"])
        S.op("dve", lambda h: h.tensor_scalar(out=X[0:M, :], in0=X[0:M, :], scalar1=st[0:M, 0:1], scalar2=None,
                                              op0=ALU.subtract),
             reads=[kx, "st0"], writes=[kx])
        S.op("act", lambda h: h.activation(out=J2[0:M, :], in_=X[0:M, :], func=AF.Square),
             reads=[kx], writes=["lnJ2"])
        S.op("dve", lambda h: h.tensor_scalar(out=J[0:M, :], in0=J2[0:M, :], scalar1=1.0 / D, scalar2=None,
                                              op0=ALU.mult, op1=ALU.add, accum_out=st[0:M, 1:2]),
             reads=["lnJ2"], writes=["lnJ", "st1"])
        S.op("act", lambda h: h.activation(out=st[0:M, 2:3], in_=st[0:M, 1:2], func=AF.Sqrt, bias=self.epsc[0:M, :]),
             reads=["st1"], writes=["st2"])
        S.op("dve", lambda h: h.reciprocal(out=st[0:M, 3:4], in_=st[0:M, 2:3]), reads=["st2"], writes=["st3"])
        S.op("act", lambda h: h.activation(out=X[0:M, :], in_=X[0:M, :], func=AF.Identity, scale=st[0:M, 3:4]),
             reads=[kx, "st3"], writes=[kx])
        if affine is not None:
            G, Bt = affine
            S.op("dve", lambda h: h.tensor_tensor(out=X[0:M, :], in0=X[0:M, :], in1=G[0:M, :], op=ALU.mult),
                 reads=[kx, "lnG"], writes=[kx])
            S.op("pool", lambda h: h.tensor_tensor(out=X[0:M, :], in0=X[0:M, :], in1=Bt[0:M, :], op=ALU.add),
                 reads=[kx, "lnB"], writes=[kx])

    def rows_to_T(self, S, X, M, tag, c0, sc_off, sh_off, n_evac):
        PS, ident, BIG1, COND = self.PS, self.ident, self.BIG1, self.COND
        kx = ("X", tag)
        for f4 in range(8):
            bank = 6 + (f4 % 2)
            for q in range(4):
                ft = f4 * 4 + q
                S.op("pe", lambda h, ft=ft, q=q, bank=bank: h.transpose(out=PS[:, bank, q * 128:q * 128 + M],
                                                                     in_=X[0:M, ft * 128:(ft + 1) * 128],
                                                                     identity=ident[0:M, 0:M]),
                     reads=[kx, "ident"], writes=[("ps", bank)])
            for q in range(4):
                ft = f4 * 4 + q
                if bank == 7:
                    S.op("act", lambda h, ft=ft, q=q, bank=bank: h.activation(
                        out=BIG1[:, ft, c0:c0 + M], in_=PS[:, bank, q * 128:q * 128 + M], func=AF.Identity,
                        scale=COND[:, sc_off + ft:sc_off + ft + 1], bias=COND[:, sh_off + ft:sh_off + ft + 1]),
                         reads=[("ps", bank), "COND"], writes=[("big1", ft)])
                else:
                    S.op("dve", lambda h, ft=ft, q=q, bank=bank: h.tensor_scalar(
                        out=BIG1[:, ft, c0:c0 + M], in0=PS[:, bank, q * 128:q * 128 + M],
                        scalar1=COND[:, sc_off + ft:sc_off + ft + 1], scalar2=COND[:, sh_off + ft:sh_off + ft + 1],
                        op0=ALU.mult, op1=ALU.add),
                         reads=[("ps", bank), "COND"], writes=[("big1", ft)])

    def bcast_cond(self, S, GB, off):
        PS, ident, ones, COND = self.PS, self.ident, self.ones, self.COND
        Dg = self.T("bc_diag", [128, 2, 128])
        for ft in range(KT):
            s = ft % 2
            S.op("dve", lambda h, ft=ft, s=s: h.tensor_scalar(out=Dg[:, s, :], in0=ident[:, :],
                                                              scalar1=COND[:, off + ft:off + ft + 1], scalar2=None,
                                                              op0=ALU.mult),
                 reads=["ident", "COND"], writes=[("bcd", s)])
            S.op("pe", lambda h, s=s: h.matmul(PS[:, 6 + s, 0:128], ones[:, :], Dg[:, s, :], start=True, stop=True),
                 reads=[("bcd", s), "ones"], writes=[("ps", 6 + s)])
            S.op("act", lambda h, ft=ft, s=s: h.activation(out=GB[:, ft * 128:(ft + 1) * 128], in_=PS[:, 6 + s, 0:128],
                                                           func=AF.Identity),
                 reads=[("ps", 6 + s)], writes=["GB"])

    def build(self):
        nc = self.nc
        stage = self.stage
        self.PS = PS = nc.alloc_psum_tensor("PS", [128, 8, 512], F32)
        self.ident = ident = self.T("ident", [128, 128])
        self.ones = ones = self.T("ones", [128, 128])
        self.FL = FL = self.T("FL", [128, 18])
        self.COND = COND = self.T("COND", [128, 192])
        self.GATH = GATH = self.T("GATH", [128, 288])
        self.epsc = self.T("epsc", [128, 1])
        self.CW4 = self.T("CW4", [128, 64])
        self.CWB = self.T("CWB", [128, 16])
        self.BR = self.T("BR", [128, 16])
        self.BI = self.T("BI", [128, 16])
        self.C1 = self.T("C1", [128, 32])
        self.WR = self.T("WR", [128, 16, 128], BF16)
        self.WI = self.T("WI", [128, 16, 128], BF16)
        self.H0L = self.T("H0L", [128, 16])
        self.BIG1 = BIG1 = self.T("BIG1", [128, KT, NT], BF16)
        self.WA = WA = self.T("WA", [128, 2, 4096], BF16)

        self.S = Sched(nc)
        self.p_consts_ada()
        if stage >= 1:
            self.p_ln1()
        if stage >= 2:
            self.p_lru(1)
        if stage >= 3:
            with self.scope():
                self.U = self.T("U", [128, 16, NT], BF16)
                with self.scope():
                    self.s5_alloc()
                    self.p_s5_prep()
                    self.p_xs()
                    if stage >= 4:
                        self.p_s5_state()
                    if stage >= 5:
                        self.p_gather()
                    if stage >= 6:
                        self.p_s5(2)
                if stage >= 7:
                    self.p_glu()
                    self.ys = self.U
                    self.ya = self.T("ya", [128, 16, NT], BF16)
                    self.p_lru(2)
                if stage >= 8:
                    self.p_merge()
        if stage >= 9:
            self.p_wout()
        if stage >= 10:
            self.p_ln_mid()
        if stage >= 11:
            self.p_ffn()
        if stage >= 12:
            self.p_ln_final()
        self.S.emit()
        return nc

    def p_consts_ada(self):
        nc = self.nc
        PS, ident, ones, FL, COND, WA = self.PS, self.ident, self.ones, self.FL, self.COND, self.WA
        with self.phase() as S:
            S.op("pool", lambda h: h.memset(ident[:, :], 0.0), writes=["ident"])
            S.op("pool", lambda h: h.affine_select(out=ident[:, :], in_=ident[:, :], pattern=[[-1, 128]],
                                                   compare_op=ALU.not_equal, fill=1.0, base=0, channel_multiplier=1),
                 reads=["ident"], writes=["ident"])
            S.op("pool", lambda h: h.memset(ones[:, :], 1.0), writes=["ones"])
            S.op("pool", lambda h: h.memset(self.epsc[:, :], LN_EPS), writes=["epsc"])
            S.dma("sp", lambda h: h.dma_start(out=FL[:, :], in_=self.flags.partition_broadcast(128)), writes=["FL"])
            ct32 = self.T("ct32", [128, KT])
            cact = self.T("cact", [128, KT], BF16)
            S.dma("sp", lambda h: h.dma_start(out=ct32[:, :], in_=self.cT), writes=["ct32"])
            S.op("act", lambda h: h.activation(out=cact[:, :], in_=ct32[:, :], func=AF.Silu), reads=["ct32"], writes=["cact"])
            bada = self.T("bada", [128, 192])
            self.load_cols(S, bada, self.b_ada[0].rearrange("(o p) -> o p", p=128), 192, "bada", "tb")
            self.load_cols(S, self.CW4, self.conv_lru_w[0].rearrange("k (c p) -> (k c) p", p=128), 64, "CW4", "tcw")
            self.load_cols(S, self.CWB, self.conv_lru_b[0].rearrange("(c p) -> c p", p=128), 16, "CWB", "tcb")
            self.load_cols(S, self.BR, self.lru_br[0], 16, "BR", "tbr")
            self.load_cols(S, self.BI, self.lru_bi[0], 16, "BI", "tbi")
            lam = self.T("lamc", [128, 16])
            self.load_cols(S, lam, self.lru_lambda[0].rearrange("(c p) -> c p", p=128), 16, "lamc", "tlm")
            e1 = self.T("e1", [128, 16])
            S.op("act", lambda h: h.activation(out=e1[:, :], in_=lam[:, :], func=AF.Exp, scale=-1.0), reads=["lamc"], writes=["e1"])
            S.op("act", lambda h: h.activation(out=e1[:, :], in_=e1[:, :], func=AF.Ln, bias=ones[:, 0:1]), reads=["e1", "ones"], writes=["e1"])
            S.op("dve", lambda h: h.tensor_scalar(out=self.C1[:, 0:16], in0=e1[:, :], scalar1=-8.0, scalar2=None, op0=ALU.mult),
                 reads=["e1"], writes=["C1"])
            S.op("dve", lambda h: h.tensor_scalar(out=self.C1[:, 16:32], in0=e1[:, :], scalar1=-16.0, scalar2=None, op0=ALU.mult),
                 reads=["e1"], writes=["C1"])
            S.dma("pool", lambda h: h.dma_start(out=self.WR[:, :, :], in_=self.lru_wr[0].rearrange("h i j -> i h j")), writes=["WR"])
            S.dma("pool", lambda h: h.dma_start(out=self.WI[:, :, :], in_=self.lru_wi[0].rearrange("h i j -> i h j")), writes=["WI"])
            for og in range(48):
                for kg in range(4):
                    slot = self.wslot()
                    wv = WA[:, slot, :].rearrange("p (k c) -> p k c", c=512)
                    self.wload(S, wv, self.w_ada[0][kg * 1024:(kg + 1) * 1024, og * 512:(og + 1) * 512].rearrange("(k p) c -> p k c", p=128), slot)
                    for q in range(4):
                        for kt in range(8):
                            k = kg * 8 + kt
                            S.op("pe", lambda h, wv=wv, kt=kt, k=k, q=q, og=og: h.matmul(PS[:, q, og:og + 1], wv[:, kt, q * 128:(q + 1) * 128],
                                                                                       cact[:, k:k + 1], start=(k == 0), stop=(k == KT - 1)),
                                 reads=[("wa", slot), "cact"], writes=[("ps", q)])
            CONDv = COND[:, :].rearrange("p (o q) -> p o q", q=4)
            badav = bada[:, :].rearrange("p (o q) -> p o q", q=4)
            for q in range(4):
                S.op("dve", lambda h, q=q: h.tensor_tensor(out=CONDv[:, :, q], in0=PS[:, q, 0:48], in1=badav[:, :, q], op=ALU.add),
                     reads=[("ps", q), "bada"], writes=["COND"])
            for sec in (1, 2, 4, 5):
                S.op("dve", lambda h, sec=sec: h.tensor_scalar(out=COND[:, sec * 32:(sec + 1) * 32], in0=COND[:, sec * 32:(sec + 1) * 32],
                                                               scalar1=1.0, scalar2=None, op0=ALU.add),
                     reads=["COND"], writes=["COND"])
            self.dbg(S, "cond", COND[:, :], [128, 192], reads=["COND"])

    def tok_tiles(self):
        tiles = [(0, HALO)]
        for i in range(8):
            tiles.append((HALO + 128 * i, 128))
        return tiles

    def p_ln1(self):
        BIG1, FL = self.BIG1, self.FL
        with self.phase() as S:
            self.lnJ = self.T("lnJ", [128, D], BF16)
            self.lnJ2 = self.T("lnJ2", [128, D])
            self.lnst = self.T("lnst", [128, 4])
            XT = [self.T(f"XT{i}", [128, D]) for i in range(2)]
            nev = [0]
            for ti, (r0, M) in enumerate(self.tok_tiles()):
                X = XT[ti % 2]
                tag = ti % 2
                S.dma("act", lambda h, X=X, r0=r0, M=M: h.dma_start(out=X[0:M, :], in_=self.xh[r0:r0 + M, :]),
                      writes=[("X", tag)])
                import os
                cut = int(os.environ.get("LN1_CUT", "9"))
                if cut >= 1:
                    self.ln_rows(S, X, M, tag)
                if cut >= 2:
                    self.rows_to_T(S, X, M, tag, r0, 32, 0, nev)
            if cut >= 3:
              S.op("dve", lambda h: h.tensor_scalar(out=BIG1[:, :, 0:HALO], in0=BIG1[:, :, 0:HALO], scalar1=FL[:, 0:1],
                                                  scalar2=None, op0=ALU.mult),
                 reads=[("big1", ft) for ft in range(KT)] + ["FL"], writes=[("big1", ft) for ft in range(KT)])
            self.dbg(S, "hT", BIG1[:, :, :], [128, KT, NT], BF16, reads=[("big1", ft) for ft in range(KT)])

    def p_lru(self, pas):
        nc = self.nc
        PS, BIG1, WA, FL, GATH = self.PS, self.BIG1, self.WA, self.FL, self.GATH
        w_in = self.w_in[0]
        with self.phase() as S:
            xap = self.T("xap", [128, NT + 3])
            xc = self.T("xc", [128, NT])
            xcb = self.T("xcb", [128, NT], BF16)
            rr = self.T("rr", [128, NT])
            ii = self.T("ii", [128, NT])
            aa = self.T("aa", [128, NT])
            bb = self.T("bb", [128, NT])
            sm = self.T("sm", [128, 2])
            S.op("pool", lambda h: h.memset(xap[:, 0:3], 0.0), writes=["xap_pad"])
            hk = [("big1", ft) for ft in range(KT)]
            hh = xap[:, 3:NT + 3]
            gg = rr
            for ct in range(16):
                slot = self.wslot()
                wv = WA[:, slot, :].rearrange("p (k c) -> p k c", c=128)
                self.wload(S, wv, w_in[:, ct * 128:(ct + 1) * 128].rearrange("(k p) c -> p k c", p=128), slot)
                pb = self.big()
                self.fm_matmul(S, pb, wv, KT, lambda kt, b: BIG1[:, kt, b * CW:(b + 1) * CW], hk, slot)
                pk = [("ps", pb + b) for b in range(3)]
                S.op("act", lambda h, pb=pb: h.activation(out=v3(xap[:, 3:NT + 3]), in_=PS[:, pb:pb + 3, 0:CW], func=AF.Identity),
                     reads=pk, writes=["xap"])
                cw = lambda k, ct=ct: self.CW4[:, k * 16 + ct:k * 16 + ct + 1]
                S.op("dve", lambda h, ct=ct, cw=cw: h.tensor_scalar(out=xc[:, :], in0=xap[:, 0:NT], scalar1=cw(0),
                                                                    scalar2=self.CWB[:, ct:ct + 1], op0=ALU.mult, op1=ALU.add),
                     reads=["xap", "xap_pad", "CW4", "CWB"], writes=["xc"])
                for k in (1, 2, 3):
                    S.op("dve", lambda h, k=k, cw=cw: h.scalar_tensor_tensor(out=xc[:, :], in0=xap[:, k:k + NT], scalar=cw(k),
                                                                             in1=xc[:, :], op0=ALU.mult, op1=ALU.add),
                         reads=["xap", "xap_pad", "xc", "CW4"], writes=["xc"])
                S.op("act", lambda h: h.activation(out=xcb[:, :], in_=xc[:, :], func=AF.Identity), reads=["xc"], writes=["xcb"])
                pr = self.big()
                for b in range(3):
                    S.op("pe", lambda h, b=b, ct=ct, pr=pr: h.matmul(PS[:, pr + b, 0:CW], self.WR[:, ct, :], xcb[:, b * CW:(b + 1) * CW],
                                                                     start=True, stop=True),
                         reads=["xcb", "WR"], writes=[("ps", pr + b)])
                S.op("act", lambda h, ct=ct, pr=pr: h.activation(out=v3(rr[:, :]), in_=PS[:, pr:pr + 3, 0:CW], func=AF.Sigmoid,
                                                                 bias=self.BR[:, ct:ct + 1]),
                     reads=[("ps", pr + b) for b in range(3)] + ["BR"], writes=["rr"])
                pi = self.big()
                for b in range(3):
                    S.op("pe", lambda h, b=b, ct=ct, pi=pi: h.matmul(PS[:, pi + b, 0:CW], self.WI[:, ct, :], xcb[:, b * CW:(b + 1) * CW],
                                                                     start=True, stop=True),
                         reads=["xcb", "WI"], writes=[("ps", pi + b)])
                S.op("act", lambda h, ct=ct, pi=pi: h.activation(out=v3(ii[:, :]), in_=PS[:, pi:pi + 3, 0:CW], func=AF.Sigmoid,
                                                                 bias=self.BI[:, ct:ct + 1]),
                     reads=[("ps", pi + b) for b in range(3)] + ["BI"], writes=["ii"])
                S.op("act", lambda h, ct=ct: h.activation(out=aa[:, :], in_=rr[:, :], func=AF.Exp, scale=self.C1[:, ct:ct + 1]),
                     reads=["rr", "C1"], writes=["aa"])
                S.op("act", lambda h, ct=ct: h.activation(out=bb[:, :], in_=rr[:, :], func=AF.Exp, scale=self.C1[:, 16 + ct:17 + ct]),
                     reads=["rr", "C1"], writes=["bb"])
                S.op("act", lambda h: h.activation(out=bb[:, :], in_=bb[:, :], func=AF.Sqrt, scale=-1.0, bias=self.ones[:, 0:1]),
                     reads=["bb", "ones"], writes=["bb"])
                S.op("dve", lambda h: h.tensor_tensor(out=ii[:, :], in0=ii[:, :], in1=xc[:, :], op=ALU.mult),
                     reads=["ii", "xc"], writes=["ii"])
                S.op("dve", lambda h: h.tensor_tensor(out=bb[:, :], in0=bb[:, :], in1=ii[:, :], op=ALU.mult),
                     reads=["bb", "ii"], writes=["bb"])
                S.op("dve", lambda h: h.tensor_tensor(out=bb[:, 0:8], in0=bb[:, 0:8], in1=FL[:, 1:9], op=ALU.mult),
                     reads=["bb", "FL"], writes=["bb"])
                if pas == 2:
                    S.op("dve", lambda h, ct=ct: h.tensor_copy(out=bb[:, 2:3], in_=self.H0L[:, ct:ct + 1]),
                         reads=["bb", "H0L"], writes=["bb"])
                S.op("dve", lambda h: h.tensor_tensor_scan(out=hh, data0=aa[:, :], data1=bb[:, :], initial=0.0,
                                                           op0=ALU.mult, op1=ALU.add),
                     reads=["aa", "bb", "xc"], writes=["xap"])
                if pas == 1:
                    S.op("dve", lambda h, ct=ct: h.tensor_copy(out=GATH[:, 16 + ct:17 + ct], in_=xap[:, 3 + EXP_POS:4 + EXP_POS]),
                         reads=["xap"], writes=["GATH"])
                    S.op("dve", lambda h: h.tensor_reduce(out=sm[:, 0:1], in_=rr[:, 3:EXP_POS + 1], axis=AX.X, op=ALU.add),
                         reads=["rr"], writes=["sm"])
                    S.op("act", lambda h, ct=ct: h.activation(out=GATH[:, ct:ct + 1], in_=sm[:, 0:1], func=AF.Exp,
                                                              scale=self.C1[:, ct:ct + 1]),
                         reads=["sm", "C1"], writes=["GATH"])
                    if ct == 0:
                        self.dbg(S, "lru_h0", hh, [128, NT], reads=["xap"])
                        self.dbg(S, "lru_a0", aa[:, :], [128, NT], reads=["aa"])
                        self.dbg(S, "lru_b0", bb[:, :], [128, NT], reads=["bb"])
                        self.dbg(S, "lru_xc0", xc[:, :], [128, NT], reads=["xc"])
                else:
                    slot = self.wslot()
                    wv2 = WA[:, slot, :].rearrange("p (k c) -> p k c", c=128)
                    self.wload(S, wv2, w_in[:, 2048 + ct * 128:2048 + (ct + 1) * 128].rearrange("(k p) c -> p k c", p=128), slot)
                    pg = self.big()
                    self.fm_matmul(S, pg, wv2, KT, lambda kt, b: BIG1[:, kt, b * CW:(b + 1) * CW], hk, slot)
                    S.op("act", lambda h, pg=pg: h.activation(out=v3(gg[:, :]), in_=PS[:, pg:pg + 3, 0:CW], func=AF.Gelu),
                         reads=[("ps", pg + b) for b in range(3)], writes=["rr"])
                    S.op("dve", lambda h, ct=ct: h.tensor_tensor(out=self.ya[:, ct, :], in0=hh, in1=gg[:, :], op=ALU.mult),
                         reads=["xap", "rr"], writes=[("ya", ct)])
            if pas == 1:
                self.dbg(S, "gath_lru", GATH[:, 0:32], [128, 32], reads=["GATH"])
            else:
                self.dbg(S, "ya", self.ya[:, :, :], [128, 16, NT], BF16, reads=[("ya", ct) for ct in range(16)])

    def s5_alloc(self):
        self.CT = self.T("CT", [64, 2, 128, 16])
        self.BTb = self.T("BTb", [128, 16, 128], BF16)
        self.LD1 = self.T("LD1", [64, 2, 128])
        self.LD2 = self.T("LD2", [64, 2, 128])
        self.PM = self.T("PM", [128, 8])
        self.DCOL = self.T("DCOL", [128, 16])
        self.QRE = self.T("QRE", [64, 128])
        self.QIM = self.T("QIM", [64, 128])
        self.H0S = self.T("H0S", [64, 2, 128])
        self.LBAR = self.T("LBAR", [64, 2, 128])

    def p_s5_prep(self):
        PS, ident = self.PS, self.ident
        with self.phase() as S:
            T = self.T
            lr_n = T("lr_n", [128, 64]); li_n = T("li_n", [128, 64])
            S.dma("sp", lambda h: h.dma_start(out=lr_n[:, :], in_=self.ssm_lam_re[0]), writes=["lr_n"])
            S.dma("sp", lambda h: h.dma_start(out=li_n[:, :], in_=self.ssm_lam_im[0]), writes=["li_n"])
            LRE = T("LRE", [64, 128]); LIM = T("LIM", [64, 128])
            for src, dst, k in ((lr_n, LRE, "LRE"), (li_n, LIM, "LIM")):
                S.op("pe", lambda h, src=src: h.transpose(out=PS[0:64, 7, 0:128], in_=src[:, :], identity=ident[:, :]),
                     reads=[src is lr_n and "lr_n" or "li_n", "ident"], writes=[("ps", 7)])
                S.op("dve", lambda h, dst=dst: h.tensor_copy(out=dst[:, :], in_=PS[0:64, 7, 0:128]), reads=[("ps", 7)], writes=[k])
            STEP = T("STEP", [64, 128])
            S.dma("sp", lambda h: h.dma_start(out=STEP[:, :], in_=self.ssm_log_step.partition_broadcast(64)), writes=["STEP"])
            S.op("act", lambda h: h.activation(out=STEP[:, :], in_=STEP[:, :], func=AF.Exp), reads=["STEP"], writes=["STEP"])
            A_ = T("A_", [64, 128]); TH = T("TH", [64, 128]); MAG = T("MAG", [64, 128])
            S.op("dve", lambda h: h.tensor_tensor(out=A_[:, :], in0=LRE[:, :], in1=STEP[:, :], op=ALU.mult), reads=["LRE", "STEP"], writes=["A_"])
            S.op("dve", lambda h: h.tensor_tensor(out=TH[:, :], in0=LIM[:, :], in1=STEP[:, :], op=ALU.mult), reads=["LIM", "STEP"], writes=["TH"])
            S.op("act", lambda h: h.activation(out=MAG[:, :], in_=A_[:, :], func=AF.Exp), reads=["A_"], writes=["MAG"])
            hp = T("hpi", [64, 1])
            S.op("pool", lambda h: h.memset(hp[:, :], math.pi / 2), writes=["hpi"])
            sn = [T("sn0", [64, 128]), T("sn1", [64, 128])]
            cs = [T("cs0", [64, 128]), T("cs1", [64, 128])]
            tq = T("tq", [64, 128])
            S.op("act", lambda h: h.activation(out=sn[0][:, :], in_=TH[:, :], func=AF.Sin, scale=1.0 / 32), reads=["TH"], writes=["sn0"])
            S.op("act", lambda h: h.activation(out=cs[0][:, :], in_=TH[:, :], func=AF.Sin, scale=1.0 / 32, bias=hp[:, :]),
                 reads=["TH", "hpi"], writes=["cs0"])
            cur = 0
            for it in range(5):
                nx = 1 - cur
                S.op("dve", lambda h, cur=cur: h.tensor_tensor(out=tq[:, :], in0=sn[cur][:, :], in1=sn[cur][:, :], op=ALU.mult),
                     reads=[f"sn{cur}"], writes=["tq"])
                S.op("dve", lambda h, nx=nx: h.tensor_scalar(out=cs[nx][:, :], in0=tq[:, :], scalar1=-2.0, scalar2=1.0,
                                                             op0=ALU.mult, op1=ALU.add),
                     reads=["tq"], writes=[f"cs{nx}"])
                S.op("dve", lambda h, cur=cur, nx=nx: h.scalar_tensor_tensor(out=sn[nx][:, :], in0=sn[cur][:, :], scalar=2.0,
                                                                             in1=cs[cur][:, :], op0=ALU.mult, op1=ALU.mult),
                     reads=[f"sn{cur}", f"cs{cur}"], writes=[f"sn{nx}"])
                cur = nx
            LB = self.LBAR
            S.op("dve", lambda h, cur=cur: h.tensor_tensor(out=LB[:, 0, :], in0=MAG[:, :], in1=cs[cur][:, :], op=ALU.mult),
                 reads=["MAG", f"cs{cur}"], writes=["LB"])
            S.op("dve", lambda h, cur=cur: h.tensor_tensor(out=LB[:, 1, :], in0=MAG[:, :], in1=sn[cur][:, :], op=ALU.mult),
                 reads=["MAG", f"sn{cur}"], writes=["LB"])
            LD1, LD2 = self.LD1, self.LD2
            S.op("dve", lambda h: h.tensor_copy(out=LD1[:, 0, :], in_=LB[:, 0, :]), reads=["LB"], writes=["LD1"])
            S.op("dve", lambda h: h.tensor_copy(out=LD1[:, 1, :], in_=LB[:, 0, :]), reads=["LB"], writes=["LD1"])
            S.op("dve", lambda h: h.tensor_scalar(out=LD2[:, 0, :], in0=LB[:, 1, :], scalar1=-1.0, scalar2=None, op0=ALU.mult),
                 reads=["LB"], writes=["LD2"])
            S.op("dve", lambda h: h.tensor_copy(out=LD2[:, 1, :], in_=LB[:, 1, :]), reads=["LB"], writes=["LD2"])
            den = T("den", [64, 128]); t2 = T("t2", [64, 128]); xr = T("xr", [64, 128])
            qre, qim = self.QRE, self.QIM
            S.op("dve", lambda h: h.tensor_tensor(out=den[:, :], in0=LRE[:, :], in1=LRE[:, :], op=ALU.mult), reads=["LRE"], writes=["den"])
            S.op("dve", lambda h: h.tensor_tensor(out=t2[:, :], in0=LIM[:, :], in1=LIM[:, :], op=ALU.mult), reads=["LIM"], writes=["t2"])
            S.op("dve", lambda h: h.tensor_tensor(out=den[:, :], in0=den[:, :], in1=t2[:, :], op=ALU.add), reads=["den", "t2"], writes=["den"])
            S.op("dve", lambda h: h.reciprocal(out=den[:, :], in_=den[:, :]), reads=["den"], writes=["den"])
            S.op("dve", lambda h: h.tensor_scalar(out=xr[:, :], in0=LB[:, 0, :], scalar1=-1.0, scalar2=None, op0=ALU.add), reads=["LB"], writes=["xr"])
            S.op("dve", lambda h: h.tensor_tensor(out=qre[:, :], in0=xr[:, :], in1=LRE[:, :], op=ALU.mult), reads=["xr", "LRE"], writes=["qre"])
            S.op("dve", lambda h: h.tensor_tensor(out=t2[:, :], in0=LB[:, 1, :], in1=LIM[:, :], op=ALU.mult), reads=["LB", "LIM", "den"], writes=["t2"])
            S.op("dve", lambda h: h.tensor_tensor(out=qre[:, :], in0=qre[:, :], in1=t2[:, :], op=ALU.add), reads=["qre", "t2"], writes=["qre"])
            S.op("dve", lambda h: h.tensor_tensor(out=qre[:, :], in0=qre[:, :], in1=den[:, :], op=ALU.mult), reads=["qre", "den"], writes=["qre"])
            S.op("dve", lambda h: h.tensor_tensor(out=qim[:, :], in0=LB[:, 1, :], in1=LRE[:, :], op=ALU.mult), reads=["LB", "LRE"], writes=["qim"])
            S.op("dve", lambda h: h.tensor_tensor(out=t2[:, :], in0=xr[:, :], in1=LIM[:, :], op=ALU.mult), reads=["xr", "LIM", "qre"], writes=["t2"])
            S.op("dve", lambda h: h.tensor_tensor(out=qim[:, :], in0=qim[:, :], in1=t2[:, :], op=ALU.subtract), reads=["qim", "t2"], writes=["qim"])
            S.op("dve", lambda h: h.tensor_tensor(out=qim[:, :], in0=qim[:, :], in1=den[:, :], op=ALU.mult), reads=["qim", "den"], writes=["qim"])
            self.dbg(S, "lbar", LB[:, :, :], [64, 2, 128], reads=["LB"])
            self.dbg(S, "qre", qre[:, :], [64, 128], reads=["qre"])
            self.dbg(S, "qim", qim[:, :], [64, 128], reads=["qim"])
        for hb in range(2):
          with self.phase() as S:
            T = self.T
            qre, qim = self.QRE, self.QIM
            g0 = hb * 64
            bre = T("bre", [64, 64, 16]); bim = T("bim", [64, 64, 16])
            bnat = [T("bnat0", [128, 1024]), T("bnat1", [128, 1024])]
            for a, (srcd, dst, k) in enumerate(((self.ssm_b_re, bre, "bre"), (self.ssm_b_im, bim, "bim"))):
                S.dma("sp" if a == 0 else "act", lambda h, a=a, srcd=srcd: h.dma_start(out=bnat[a][:, :], in_=srcd[0].rearrange("g p n -> g (p n)")),
                      writes=[f"bnat{a}"])
                for n in range(16):
                    bank = 6 + (n % 2)
                    S.op("pe", lambda h, a=a, n=n, bank=bank: h.transpose(
                        out=PS[0:64, bank, 0:64], in_=bnat[a][g0:g0 + 64, :].rearrange("g (p n) -> g p n", n=16)[:, :, n],
                        identity=ident[g0:g0 + 64, g0:g0 + 64]),
                         reads=[f"bnat{a}", "ident"], writes=[("ps", bank)])
                    S.op("act", lambda h, dst=dst, n=n, bank=bank: h.activation(out=dst[:, :, n], in_=PS[0:64, bank, 0:64], func=AF.Identity),
                         reads=[("ps", bank)], writes=[k])
            BB = T("BB", [64, 2, 64, 16]); tb = T("tbb", [64, 64, 16])
            qb = lambda q: q[:, g0:g0 + 64, None].broadcast_to([64, 64, 16])
            S.op("dve", lambda h: h.tensor_tensor(out=BB[:, 0, :, :], in0=bre[:, :, :], in1=qb(qre), op=ALU.mult), reads=["bre"], writes=["BB0"])
            S.op("dve", lambda h: h.tensor_tensor(out=tb[:, :, :], in0=bim[:, :, :], in1=qb(qim), op=ALU.mult), reads=["bim"], writes=["tbb"])
            S.op("dve", lambda h: h.tensor_tensor(out=BB[:, 0, :, :], in0=BB[:, 0, :, :], in1=tb[:, :, :], op=ALU.subtract), reads=["BB0", "tbb"], writes=["BB0"])
            S.op("dve", lambda h: h.tensor_tensor(out=BB[:, 1, :, :], in0=bim[:, :, :], in1=qb(qre), op=ALU.mult), reads=["bim"], writes=["BB1"])
            S.op("dve", lambda h: h.tensor_tensor(out=tb[:, :, :], in0=bre[:, :, :], in1=qb(qim), op=ALU.mult), reads=["bre", "BB0"], writes=["tbb"])
            S.op("dve", lambda h: h.tensor_tensor(out=BB[:, 1, :, :], in0=BB[:, 1, :, :], in1=tb[:, :, :], op=ALU.add), reads=["BB1", "tbb"], writes=["BB1"])
            for gbl in range(8):
                gb = hb * 8 + gbl
                for ri in range(2):
                    bank = 6 + (ri % 2)
                    S.op("pe", lambda h, gbl=gbl, ri=ri, bank=bank: h.transpose(
                        out=PS[:, bank, 0:64], in_=BB[:, ri, gbl * 8:(gbl + 1) * 8, :].rearrange("p g n -> p (g n)"),
                        identity=ident[0:64, 0:64]),
                         reads=[f"BB{ri}", "ident"], writes=[("ps", bank)])
                    S.op("act", lambda h, gb=gb, ri=ri, bank=bank: h.activation(out=self.BTb[:, gb, ri * 64:(ri + 1) * 64],
                                                                                 in_=PS[:, bank, 0:64], func=AF.Identity),
                         reads=[("ps", bank)], writes=["BTb"])
            self.dbg(S, f"BB{hb}", BB[:, :, :, :], [64, 2, 64, 16], reads=["BB0", "BB1"])
            self.dbg(S, f"bre{hb}", bre[:, :, :], [64, 64, 16], reads=["bre"])
        with self.phase() as S:
            T = self.T
            cnat = [T("cnat0", [128, 1024]), T("cnat1", [128, 1024])]
            for ri, srcd in enumerate((self.ssm_c_re, self.ssm_c_im)):
                S.dma("sp" if ri == 0 else "act", lambda h, ri=ri, srcd=srcd: h.dma_start(out=cnat[ri][:, :], in_=srcd[0].rearrange("g n p -> g (n p)")),
                      writes=[f"cnat{ri}"])
                for n in range(16):
                    bank = 6 + (n % 2)
                    S.op("pe", lambda h, ri=ri, n=n, bank=bank: h.transpose(out=PS[0:64, bank, 0:128], in_=cnat[ri][:, n * 64:(n + 1) * 64],
                                                                            identity=ident[:, :]),
                         reads=[f"cnat{ri}", "ident"], writes=[("ps", bank)])
                    S.op("act", lambda h, ri=ri, n=n, bank=bank: h.activation(out=self.CT[:, ri, :, n], in_=PS[0:64, bank, 0:128],
                                                                              func=AF.Identity, scale=(1.0 if ri == 0 else -1.0)),
                         reads=[("ps", bank)], writes=["CT"])
            S.op("dve", lambda h: h.tensor_reduce(out=self.PM[:, :], in_=ident[:, :].rearrange("p (a b) -> p a b", b=16), axis=AX.X, op=ALU.add),
                 reads=["ident"], writes=["PM"])
            self.load_cols(S, self.DCOL, self.ssm_d[0].rearrange("(c p) -> c p", p=128), 16, "DCOL", "tsd")
            self.dbg(S, "CT", self.CT[:, :, :, :], [64, 2, 128, 16], reads=["CT"])

    def p_xs(self):
        PS, BIG1, WA, U = self.PS, self.BIG1, self.WA, self.U
        w_in = self.w_in[0]
        hk = [("big1", ft) for ft in range(KT)]
        with self.phase() as S:
            for ct in range(16):
                slot = self.wslot()
                wv = WA[:, slot, :].rearrange("p (k c) -> p k c", c=128)
                self.wload(S, wv, w_in[:, 4096 + ct * 128:4096 + (ct + 1) * 128].rearrange("(k p) c -> p k c", p=128), slot)
                pb = self.big()
                self.fm_matmul(S, pb, wv, KT, lambda kt, b: BIG1[:, kt, b * CW:(b + 1) * CW], hk, slot)
                pk = [("ps", pb + b) for b in range(3)]
                if ct % 2 == 0:
                    S.op("act", lambda h, pb=pb, ct=ct: h.activation(out=v3(U[:, ct, :]), in_=PS[:, pb:pb + 3, 0:CW], func=AF.Identity),
                         reads=pk, writes=[("U", ct)])
                else:
                    S.op("dve", lambda h, pb=pb, ct=ct: h.tensor_copy(out=v3(U[:, ct, :]), in_=PS[:, pb:pb + 3, 0:CW]),
                         reads=pk, writes=[("U", ct)])
            self.dbg(S, "U", U[:, :, :], [128, 16, NT], BF16, reads=[("U", ct) for ct in range(16)])

    def p_s5_state(self):
        PS, U, BTb, PM, LB = self.PS, self.U, self.BTb, self.PM, self.LBAR
        C1 = 12
        with self.phase() as S:
            T = self.T
            POW = T("POW", [64, 2, 128, C1]); BUb = T("BUb", [64, 2, 128, C1])
            P1 = T("P1", [64, 128, C1]); P2 = T("P2", [64, 128, C1]); P3 = T("P3", [64, 128, C1]); P4 = T("P4", [64, 128, C1])
            HS = T("HS", [64, 2, 128]); SS = T("SS", [64, 2, 128]); T1 = T("T1", [64, 2, 128]); T2 = T("T2", [64, 2, 128])
            M1 = T("M1", [64, 2, 128]); M2 = T("M2", [64, 2, 128]); M1p = T("M1p", [64, 2, 128]); M2p = T("M2p", [64, 2, 128])
            ta = T("pta", [64, 128]); tb2 = T("ptb", [64, 128])
            Um = [T(f"Um{i}", [128, 8, C1], BF16) for i in range(4)]
            S.op("dve", lambda h: h.memset(POW[:, 0, :, C1 - 1], 1.0), writes=[("pw", C1 - 1)])
            S.op("dve", lambda h: h.memset(POW[:, 1, :, C1 - 1], 0.0), writes=[("pw", C1 - 1)])
            S.op("dve", lambda h: h.memset(HS[:, :, :], 0.0), writes=["HS"])

            def cmul(dre, dim, are, aim, kin, kout):
                S.op("dve", lambda h: h.tensor_tensor(out=dre, in0=are, in1=LB[:, 0, :], op=ALU.mult), reads=[kin], writes=[kout + ("r",)])
                S.op("dve", lambda h: h.tensor_tensor(out=ta[:, :], in0=aim, in1=LB[:, 1, :], op=ALU.mult), reads=[kin], writes=["pta"])
                S.op("dve", lambda h: h.tensor_tensor(out=dre, in0=dre, in1=ta[:, :], op=ALU.subtract), reads=[kout + ("r",), "pta"], writes=[kout + ("r",)])
                S.op("dve", lambda h: h.tensor_tensor(out=dim, in0=are, in1=LB[:, 1, :], op=ALU.mult), reads=[kin], writes=[kout + ("i",)])
                S.op("dve", lambda h: h.tensor_tensor(out=tb2[:, :], in0=aim, in1=LB[:, 0, :], op=ALU.mult), reads=[kin], writes=["ptb"])
                S.op("dve", lambda h: h.tensor_tensor(out=dim, in0=dim, in1=tb2[:, :], op=ALU.add), reads=[kout + ("i",), "ptb"], writes=[kout + ("i",)])

            for jj in range(C1 - 1, 0, -1):
                cmul(POW[:, 0, :, jj - 1], POW[:, 1, :, jj - 1], POW[:, 0, :, jj], POW[:, 1, :, jj], ("pw", jj), ("pw", jj - 1))
                S.op("dve", lambda h, jj=jj: h.tensor_copy(out=ta[:, 0:1], in_=ta[:, 0:1]), reads=[("pw", jj - 1, "r"), ("pw", jj - 1, "i")], writes=[("pw", jj - 1)])
            cmul(M1[:, 0, :], M2[:, 1, :], POW[:, 0, :, 0], POW[:, 1, :, 0], ("pw", 0), ("mm",))
            S.op("dve", lambda h: h.tensor_copy(out=M1[:, 1, :], in_=M1[:, 0, :]), reads=[("mm", "r")], writes=["M1"])
            S.op("dve", lambda h: h.tensor_scalar(out=M2[:, 0, :], in0=M2[:, 1, :], scalar1=-1.0, scalar2=None, op0=ALU.mult), reads=[("mm", "i")], writes=["M2"])
            S.op("dve", lambda h: h.tensor_copy(out=M1p[:, 0, :], in_=POW[:, 0, :, C1 - 5]), reads=[("pw", C1 - 5)], writes=["M1p"])
            S.op("dve", lambda h: h.tensor_copy(out=M1p[:, 1, :], in_=POW[:, 0, :, C1 - 5]), reads=[("pw", C1 - 5)], writes=["M1p"])
            S.op("dve", lambda h: h.tensor_scalar(out=M2p[:, 0, :], in0=POW[:, 1, :, C1 - 5], scalar1=-1.0, scalar2=None, op0=ALU.mult), reads=[("pw", C1 - 5)], writes=["M2p"])
            S.op("dve", lambda h: h.tensor_copy(out=M2p[:, 1, :], in_=POW[:, 1, :, C1 - 5]), reads=[("pw", C1 - 5)], writes=["M2p"])
            pwk = [("pw", jj) for jj in range(C1)]
            nfull = (S5_EXP + 1) // C1
            for blk in range(nfull + 1):
                t0 = blk * C1
                n = C1 if blk < nfull else (S5_EXP + 1 - t0)
                for gb in range(16):
                    um = Um[gb % 4]
                    uk1 = ("Um", gb % 4)
                    S.op("pool", lambda h, um=um, gb=gb, t0=t0, n=n: h.tensor_tensor(
                        out=um[:, :, 0:n], in0=U[:, gb, None, t0:t0 + n].broadcast_to([128, 8, n]),
                        in1=PM[:, :, None].broadcast_to([128, 8, n]), op=ALU.mult),
                         reads=[("U", gb), "PM"], writes=[uk1])
                    bank = gb % 6
                    for ri in range(2):
                        for gl in range(8):
                            col = (ri * 8 + gl) * C1
                            S.op("pe", lambda h, um=um, gb=gb, ri=ri, gl=gl, col=col, bank=bank, n=n: h.matmul(
                                PS[0:64, bank, col:col + n], BTb[:, gb, ri * 64:(ri + 1) * 64], um[:, gl, 0:n], start=True, stop=True),
                                 reads=[uk1, "BTb"], writes=[("ps", bank)])
                    S.op("act", lambda h, gb=gb, bank=bank, n=n: h.activation(
                        out=BUb[:, :, gb * 8:(gb + 1) * 8, 0:n],
                        in_=PS[0:64, bank, 0:16 * C1].rearrange("p (r g t) -> p r g t", r=2, g=8)[:, :, :, 0:n], func=AF.Identity),
                         reads=[("ps", bank)], writes=["BUb"])
                pr = POW[:, 0, :, C1 - n:C1]; pi_ = POW[:, 1, :, C1 - n:C1]
                br = BUb[:, 0, :, 0:n]; bi = BUb[:, 1, :, 0:n]
                S.op("dve", lambda h, pr=pr, br=br, n=n: h.tensor_tensor(out=P1[:, :, 0:n], in0=pr, in1=br, op=ALU.mult), reads=pwk + ["BUb"], writes=["P1"])
                S.op("pool", lambda h, pi_=pi_, bi=bi, n=n: h.tensor_tensor(out=P2[:, :, 0:n], in0=pi_, in1=bi, op=ALU.mult), reads=pwk + ["BUb"], writes=["P2"])
                S.op("dve", lambda h, pr=pr, bi=bi, n=n: h.tensor_tensor(out=P3[:, :, 0:n], in0=pr, in1=bi, op=ALU.mult), reads=pwk + ["BUb"], writes=["P3"])
                S.op("pool", lambda h, pi_=pi_, br=br, n=n: h.tensor_tensor(out=P4[:, :, 0:n], in0=pi_, in1=br, op=ALU.mult), reads=pwk + ["BUb"], writes=["P4"])
                S.op("dve", lambda h, n=n: h.tensor_tensor(out=P1[:, :, 0:n], in0=P1[:, :, 0:n], in1=P2[:, :, 0:n], op=ALU.subtract), reads=["P1", "P2"], writes=["P1"])
                S.op("dve", lambda h, n=n: h.tensor_tensor(out=P3[:, :, 0:n], in0=P3[:, :, 0:n], in1=P4[:, :, 0:n], op=ALU.add), reads=["P3", "P4"], writes=["P3"])
                S.op("dve", lambda h, n=n: h.tensor_reduce(out=SS[:, 0, :], in_=P1[:, :, 0:n], axis=AX.X, op=ALU.add), reads=["P1"], writes=["SS0"])
                S.op("dve", lambda h, n=n: h.tensor_reduce(out=SS[:, 1, :], in_=P3[:, :, 0:n], axis=AX.X, op=ALU.add), reads=["P3"], writes=["SS1"])
                A1, A2 = (M1, M2) if blk < nfull else (M1p, M2p)
                a1k, a2k = ("M1", "M2") if blk < nfull else ("M1p", "M2p")
                S.op("dve", lambda h, A1=A1: h.tensor_tensor(out=T1[:, :, :], in0=HS[:, :, :], in1=A1[:, :, :], op=ALU.mult), reads=["HS", a1k], writes=["T1"])
                S.op("dve", lambda h, A2=A2: h.tensor_tensor(out=T2[:, 0, :], in0=HS[:, 1, :], in1=A2[:, 0, :], op=ALU.mult), reads=["HS", a2k], writes=["T2a"])
                S.op("dve", lambda h, A2=A2: h.tensor_tensor(out=T2[:, 1, :], in0=HS[:, 0, :], in1=A2[:, 1, :], op=ALU.mult), reads=["HS", a2k], writes=["T2b"])
                S.op("dve", lambda h: h.tensor_tensor(out=T1[:, :, :], in0=T1[:, :, :], in1=T2[:, :, :], op=ALU.add), reads=["T1", "T2a", "T2b"], writes=["T1"])
                S.op("dve", lambda h: h.tensor_tensor(out=HS[:, :, :], in0=T1[:, :, :], in1=SS[:, :, :], op=ALU.add), reads=["T1", "SS0", "SS1"], writes=["HS"])
            S.op("dve", lambda h: h.tensor_copy(out=self.GATH[0:64, 32:288].rearrange("p (r g) -> p r g", r=2), in_=HS[:, :, :]),
                 reads=["HS"], writes=["GATH"])
            self.dbg(S, "gath_s5", self.GATH[0:64, 32:288], [64, 256], reads=["GATH"])

    def p_s5(self, pas):
        PS, ident, U, CT, BTb, PM = self.PS, self.ident, self.U, self.CT, self.BTb, self.PM
        LD1, LD2 = self.LD1, self.LD2
        with self.phase() as S:
            hist = self.T("hist", [64, 2, 128, CB + 1])
            T1 = self.T("T1", [64, 2, 128]); T2 = self.T("T2", [64, 2, 128])
            Um = [self.T(f"Um{i}", [128, 8, CB], BF16) for i in range(4)]
            ytm = self.T("ytm", [CB, WL])
            ysb = self.T("ysb", [128, 16, CB])
            yv = self.T("yv", [128, 16, CB])
            uk = [("U", ct) for ct in range(16)]
            if pas == 1:
                S.op("dve", lambda h: h.memset(hist[:, :, :, 0], 0.0), writes=[("hs", 0, 0), ("hs", 0, 1)])
            else:
                S.op("dve", lambda h: h.tensor_copy(out=hist[:, :, :, 0], in_=self.H0S[:, :, :]), reads=["H0S"], writes=[("hs", 0, 0), ("hs", 0, 1)])
            for blk in range(NBK):
                t0 = blk * CB
                slots = [("hs", t + 1, hv) for t in range(CB) for hv in range(2)]
                for gb in range(16):
                    um = Um[gb % 4]
                    uk1 = ("Um", gb % 4)
                    S.op("pool", lambda h, um=um, gb=gb, t0=t0: h.tensor_tensor(
                        out=um[:, :, :], in0=U[:, gb, None, t0:t0 + CB].broadcast_to([128, 8, CB]),
                        in1=PM[:, :, None].broadcast_to([128, 8, CB]), op=ALU.mult),
                         reads=[("U", gb), "PM"], writes=[uk1])
                    bank = gb % 6
                    for ri in range(2):
                        for gl in range(8):
                            col = (ri * 8 + gl) * CB
                            S.op("pe", lambda h, um=um, gb=gb, ri=ri, gl=gl, col=col, bank=bank: h.matmul(
                                PS[0:64, bank, col:col + CB], BTb[:, gb, ri * 64:(ri + 1) * 64], um[:, gl, :], start=True, stop=True),
                                 reads=[uk1, "BTb"], writes=[("ps", bank)])
                    S.op("act", lambda h, gb=gb, bank=bank: h.activation(
                        out=hist[:, :, gb * 8:(gb + 1) * 8, 1:CB + 1],
                        in_=PS[0:64, bank, 0:16 * CB].rearrange("p (r g t) -> p r g t", r=2, g=8), func=AF.Identity),
                         reads=[("ps", bank)], writes=[("hs", t + 1, gb // 8) for t in range(CB)])
                for t in range(CB):
                    for hv in range(2):
                        gs = slice(hv * 64, (hv + 1) * 64)
                        S.op("dve", lambda h, t=t, hv=hv, gs=gs: h.tensor_tensor(out=T1[:, :, gs], in0=hist[:, :, gs, t], in1=LD1[:, :, gs], op=ALU.mult),
                             reads=[("hs", t, hv), "LD1"], writes=[("T1", hv)])
                    for hv in range(2):
                        gs = slice(hv * 64, (hv + 1) * 64)
                        S.op("dve", lambda h, t=t, hv=hv, gs=gs: h.tensor_tensor(out=T2[:, 0, gs], in0=hist[:, 1, gs, t], in1=LD2[:, 0, gs], op=ALU.mult),
                             reads=[("hs", t, hv), "LD2"], writes=[("T2a", hv)])
                    for hv in range(2):
                        gs = slice(hv * 64, (hv + 1) * 64)
                        S.op("dve", lambda h, t=t, hv=hv, gs=gs: h.tensor_tensor(out=T2[:, 1, gs], in0=hist[:, 0, gs, t], in1=LD2[:, 1, gs], op=ALU.mult),
                             reads=[("hs", t, hv), "LD2"], writes=[("T2b", hv)])
                    for hv in range(2):
                        gs = slice(hv * 64, (hv + 1) * 64)
                        S.op("dve", lambda h, hv=hv, gs=gs: h.tensor_tensor(out=T1[:, :, gs], in0=T1[:, :, gs], in1=T2[:, :, gs], op=ALU.add),
                             reads=[("T1", hv), ("T2a", hv), ("T2b", hv)], writes=[("T1", hv)])
                    for hv in range(2):
                        gs = slice(hv * 64, (hv + 1) * 64)
                        S.op("dve", lambda h, t=t, hv=hv, gs=gs: h.tensor_tensor(out=hist[:, :, gs, t + 1], in0=hist[:, :, gs, t + 1], in1=T1[:, :, gs], op=ALU.add),
                             reads=[("hs", t + 1, hv), ("T1", hv)], writes=[("hs", t + 1, hv)])
                if pas == 1:
                    if blk == NBK - 1:
                        sl = S5_EXP - t0 + 1
                        S.op("dve", lambda h, sl=sl: h.tensor_copy(out=self.GATH[0:64, 32:288].rearrange("p (r g) -> p r g", r=2),
                                                                   in_=hist[:, :, :, sl]),
                             reads=[("hs", sl, 0), ("hs", sl, 1)], writes=["GATH"])
                else:
                    for g in range(128):
                        bank = g // 32
                        for ri in range(2):
                            S.op("pe", lambda h, g=g, ri=ri, bank=bank: h.matmul(
                                PS[0:CB, bank, (g % 32) * 16:(g % 32) * 16 + 16], hist[:, ri, g, 1:CB + 1], CT[:, ri, g, :],
                                start=(ri == 0), stop=(ri == 1)),
                                 reads=slots + ["CT"], writes=[("ps", bank)])
                    S.op("act", lambda h: h.activation(out=ytm[:, :].rearrange("p (b c) -> p b c", c=512), in_=PS[0:CB, 0:4, :],
                                                       func=AF.Identity),
                         reads=[("ps", b) for b in range(4)], writes=["ytm"])
                    for ct in range(16):
                        S.op("pe", lambda h, ct=ct: h.transpose(out=PS[:, 4, ct * CB:(ct + 1) * CB], in_=ytm[0:CB, ct * 128:(ct + 1) * 128],
                                                                identity=ident[0:CB, 0:CB]),
                             reads=["ytm", "ident"], writes=[("ps", 4)])
                    S.op("act", lambda h: h.activation(out=ysb[:, :, :], in_=PS[:, 4, 0:16 * CB].rearrange("p (c t) -> p c t", t=CB),
                                                       func=AF.Identity),
                         reads=[("ps", 4)], writes=["ysb"])
                    S.op("pool", lambda h, t0=t0: h.tensor_tensor(out=yv[:, :, :], in0=U[:, :, t0:t0 + CB],
                                                                   in1=self.DCOL[:, :, None].broadcast_to([128, 16, CB]), op=ALU.mult),
                         reads=uk + ["DCOL"], writes=["yv"])
                    S.op("pool", lambda h: h.tensor_tensor(out=yv[:, :, :], in0=yv[:, :, :], in1=ysb[:, :, :], op=ALU.add),
                         reads=["yv", "ysb"], writes=["yv"])
                    S.op("act", lambda h, t0=t0: h.activation(out=U[:, :, t0:t0 + CB], in_=yv[:, :, :], func=AF.Gelu),
                         reads=["yv"], writes=uk)
                if blk < NBK - 1:
                    S.op("dve", lambda h: h.tensor_copy(out=hist[:, :, :, 0], in_=hist[:, :, :, CB]),
                         reads=[("hs", CB, 0), ("hs", CB, 1)], writes=[("hs", 0, 0), ("hs", 0, 1)])
            if pas == 1:
                self.dbg(S, "gath_s5", self.GATH[0:64, 32:288], [64, 256], reads=["GATH"])
            else:
                self.dbg(S, "yg", U[:, :, :], [128, 16, NT], BF16, reads=uk)

    def p_gather(self):
        nc = self.nc
        GATH, FL = self.GATH, self.FL
        with self.phase() as S:
            NCORE = self.ncore
            G8 = self.T("G8", [128, NCORE, 288])
            S.dma("pool", lambda h: h.dma_start(out=self.ag_in.ap(), in_=GATH[:, :]), writes=["ag_in"])
            S.op("pool", lambda h: h.collective_compute("AllGather", ALU.bypass, replica_groups=[list(range(NCORE))],
                                                        ins=[self.ag_in.ap().opt()], outs=[self.ag_out.ap().opt()]),
                 reads=["ag_in"], writes=["ag_out"])
            S.dma("pool", lambda h: h.dma_start(out=G8[:, :, :], in_=self.ag_out.ap().rearrange("(r p) c -> p r c", p=128)),
                  reads=["ag_out"], writes=["G8"])
            H0L = self.H0L
            tl = self.T("tl", [128, 16])
            S.op("dve", lambda h: h.memset(H0L[:, :], 0.0), writes=["H0L"])
            for r in range(NCORE):
                S.op("dve", lambda h, r=r: h.tensor_tensor(out=tl[:, :], in0=G8[:, r, 0:16], in1=H0L[:, :], op=ALU.mult),
                     reads=["G8", "H0L"], writes=["tl"])
                S.op("dve", lambda h, r=r: h.tensor_tensor(out=tl[:, :], in0=tl[:, :], in1=G8[:, r, 16:32], op=ALU.add),
                     reads=["tl", "G8"], writes=["tl"])
                S.op("dve", lambda h: h.tensor_tensor(out=tl[:, :], in0=tl[:, :], in1=H0L[:, :], op=ALU.subtract),
                     reads=["tl", "H0L"], writes=["tl"])
                S.op("dve", lambda h, r=r: h.scalar_tensor_tensor(out=H0L[:, :], in0=tl[:, :], scalar=FL[:, 9 + r:10 + r], in1=H0L[:, :],
                                                                  op0=ALU.mult, op1=ALU.add),
                     reads=["tl", "H0L", "FL"], writes=["H0L"])
            AP_ = [self.T("AP0", [64, 2, 128]), self.T("AP1", [64, 2, 128])]
            tq = self.T("tqs", [64, 128]); tq2 = self.T("tqs2", [64, 128])
            S.op("dve", lambda h: h.tensor_copy(out=AP_[0][:, :, :], in_=self.LBAR[:, :, :]), writes=["AP0"])
            cur = 0
            for it in range(10):
                nx = 1 - cur
                a, b = AP_[cur], AP_[nx]
                S.op("dve", lambda h, a=a: h.tensor_tensor(out=tq[:, :], in0=a[:, 0, :], in1=a[:, 0, :], op=ALU.mult), reads=[f"AP{cur}"], writes=["tq"])
                S.op("dve", lambda h, a=a: h.tensor_tensor(out=tq2[:, :], in0=a[:, 1, :], in1=a[:, 1, :], op=ALU.mult), reads=[f"AP{cur}"], writes=["tq2"])
                S.op("dve", lambda h, b=b: h.tensor_tensor(out=b[:, 0, :], in0=tq[:, :], in1=tq2[:, :], op=ALU.subtract), reads=["tq", "tq2"], writes=[f"AP{nx}"])
                S.op("dve", lambda h, a=a, b=b: h.scalar_tensor_tensor(out=b[:, 1, :], in0=a[:, 0, :], scalar=2.0, in1=a[:, 1, :],
                                                                       op0=ALU.mult, op1=ALU.mult), reads=[f"AP{cur}"], writes=[f"AP{nx}"])
                cur = nx
            A = AP_[cur]
            ak = f"AP{cur}"
            H0S = self.H0S
            ts = self.T("ts5", [64, 2, 128]); tu = self.T("tu5", [64, 128])
            S.op("dve", lambda h: h.memset(H0S[:, :, :], 0.0), writes=["H0S"])
            for r in range(NCORE):
                E = G8[0:64, r, 32:288].rearrange("p (r g) -> p r g", r=2)
                S.op("dve", lambda h: h.tensor_tensor(out=ts[:, 0, :], in0=A[:, 0, :], in1=H0S[:, 0, :], op=ALU.mult), reads=[ak, "H0S"], writes=["ts0"])
                S.op("dve", lambda h: h.tensor_tensor(out=tu[:, :], in0=A[:, 1, :], in1=H0S[:, 1, :], op=ALU.mult), reads=[ak, "H0S"], writes=["tu"])
                S.op("dve", lambda h: h.tensor_tensor(out=ts[:, 0, :], in0=ts[:, 0, :], in1=tu[:, :], op=ALU.subtract), reads=["ts0", "tu"], writes=["ts0"])
                S.op("dve", lambda h: h.tensor_tensor(out=ts[:, 1, :], in0=A[:, 0, :], in1=H0S[:, 1, :], op=ALU.mult), reads=[ak, "H0S"], writes=["ts1"])
                S.op("dve", lambda h: h.tensor_tensor(out=tu[:, :], in0=A[:, 1, :], in1=H0S[:, 0, :], op=ALU.mult), reads=[ak, "H0S", "ts0"], writes=["tu"])
                S.op("dve", lambda h: h.tensor_tensor(out=ts[:, 1, :], in0=ts[:, 1, :], in1=tu[:, :], op=ALU.add), reads=["ts1", "tu"], writes=["ts1"])
                S.op("dve", lambda h, E=E: h.tensor_tensor(out=ts[:, :, :], in0=ts[:, :, :], in1=E, op=ALU.add), reads=["ts0", "ts1", "G8"], writes=["ts0", "ts1"])
                S.op("dve", lambda h: h.tensor_tensor(out=ts[:, :, :], in0=ts[:, :, :], in1=H0S[:, :, :], op=ALU.subtract), reads=["ts0", "ts1", "H0S"], writes=["ts0", "ts1"])
                S.op("dve", lambda h, r=r: h.scalar_tensor_tensor(out=H0S[:, :, :], in0=ts[:, :, :], scalar=FL[0:64, 9 + r:10 + r], in1=H0S[:, :, :],
                                                                  op0=ALU.mult, op1=ALU.add), reads=["ts0", "ts1", "H0S", "FL"], writes=["H0S"])
            self.dbg(S, "h0l", H0L[:, :], [128, 16], reads=["H0L"])
            self.dbg(S, "h0s", H0S[:, :, :], [64, 2, 128], reads=["H0S"])

    def p_glu(self):
        PS, WA, U = self.PS, self.WA, self.U
        ys_d = self.nc.dram_tensor("ys_d", [16, 128, NT], BF16).ap()
        with self.phase() as S:
            yst = [self.T("yst0", [128, NT], BF16), self.T("yst1", [128, NT], BF16)]
            BG = self.T("BG", [128, 16])
            self.load_cols(S, BG, self.b_glu[0].rearrange("(c p) -> c p", p=128), 16, "BG", "tbg")
            sg = [self.T("sg0", [128, NT]), self.T("sg1", [128, NT])]
            uk = [("U", ct) for ct in range(16)]
            for ot in range(16):
                slot = self.wslot()
                wv = WA[:, slot, :].rearrange("p (k c) -> p k c", c=128)
                self.wload(S, wv[:, 0:16, :], self.w_glu[0][:, ot * 128:(ot + 1) * 128].rearrange("(k p) c -> p k c", p=128), slot)
                pb = self.big()
                self.fm_matmul(S, pb, wv, 16, lambda kt, b: U[:, kt, b * CW:(b + 1) * CW], uk, slot)
                s = sg[ot % 2]
                S.op("act", lambda h, pb=pb, ot=ot, s=s: h.activation(out=v3(s[:, :]), in_=PS[:, pb:pb + 3, 0:CW], func=AF.Sigmoid,
                                                                      bias=BG[:, ot:ot + 1]),
                     reads=[("ps", pb + b) for b in range(3)] + ["BG"], writes=[("sg", ot % 2)])
                yt = yst[ot % 2]
                S.op("dve", lambda h, ot=ot, s=s, yt=yt: h.tensor_tensor(out=yt[:, :], in0=s[:, :], in1=U[:, ot, :], op=ALU.mult),
                     reads=[("sg", ot % 2), ("U", ot)], writes=[("yst", ot % 2)])
                S.dma("sp", lambda h, ot=ot, yt=yt: h.dma_start(out=ys_d[ot], in_=yt[:, :]), reads=[("yst", ot % 2)], writes=["ys_d"])
        with self.phase() as S:
            S.dma("sp", lambda h: h.dma_start(out=U[:, :, :], in_=ys_d.rearrange("k p t -> p k t")), writes=[("ys", ct) for ct in range(16)])
            self.dbg(S, "ys", U[:, :, :], [128, 16, NT], BF16, reads=[("ys", ct) for ct in range(16)])

    def p_merge(self):
        PS, WA, BIG1, ya, ys = self.PS, self.WA, self.BIG1, self.ya, self.ys
        w_in = self.w_in[0]
        hk = [("big1", ft) for ft in range(KT)]
        with self.phase() as S:
            sa = self.T("sa", [128, NT]); sb_ = self.T("sb", [128, NT])
            m1 = self.T("m1", [128, NT]); m2 = self.T("m2", [128, NT])
            mg = [self.T("mg0", [128, NT], BF16), self.T("mg1", [128, NT], BF16)]
            yak = [("ya", ct) for ct in range(16)]
            ysk = [("ys", ct) for ct in range(16)]
            for mt in range(KT):
                def gate(col0, dst, key):
                    slot = self.wslot()
                    wv = WA[:, slot, :].rearrange("p (k c) -> p k c", c=128)
                    self.wload(S, wv, w_in[:, col0 + mt * 128:col0 + (mt + 1) * 128].rearrange("(k p) c -> p k c", p=128), slot)
                    pb = self.big()
                    self.fm_matmul(S, pb, wv, KT, lambda kt, b: BIG1[:, kt, b * CW:(b + 1) * CW], hk, slot)
                    S.op("act", lambda h, pb=pb: h.activation(out=v3(dst[:, :]), in_=PS[:, pb:pb + 3, 0:CW], func=AF.Sigmoid),
                         reads=[("ps", pb + b) for b in range(3)], writes=[key])

                def proj(wd, src, skeys, gt, gkey, dst, dkey):
                    slot = self.wslot()
                    wv = WA[:, slot, :].rearrange("p (k c) -> p k c", c=128)
                    self.wload(S, wv[:, 0:16, :], wd[:, mt * 128:(mt + 1) * 128].rearrange("(k p) c -> p k c", p=128), slot)
                    pb = self.big()
                    self.fm_matmul(S, pb, wv, 16, lambda kt, b: src[:, kt, b * CW:(b + 1) * CW], skeys, slot)
                    S.op("dve", lambda h, pb=pb: h.tensor_tensor(out=v3(dst[:, :]), in0=PS[:, pb:pb + 3, 0:CW], in1=v3(gt[:, :]), op=ALU.mult),
                         reads=[("ps", pb + b) for b in range(3)] + [gkey], writes=[dkey])

                gate(6144, sa, "sa")
                proj(self.w_proj_lru[0], ya, yak, sa, "sa", m1, "m1")
                gate(10240, sb_, "sb")
                proj(self.w_proj_ssm[0], ys, ysk, sb_, "sb", m2, "m2")
                mo = mg[mt % 2]
                S.op("dve", lambda h, mo=mo: h.tensor_tensor(out=mo[:, :], in0=m1[:, :], in1=m2[:, :], op=ALU.add),
                     reads=["m1", "m2"], writes=[("mg", mt % 2)])
                S.dma("sp", lambda h, mo=mo, mt=mt: h.dma_start(out=self.mg_d[mt], in_=mo[:, :]), reads=[("mg", mt % 2)], writes=["mg_d"])

    def p_wout(self):
        PS, WA, BIG1 = self.PS, self.WA, self.BIG1
        w_out = self.w_out[0]
        with self.phase() as S:
            S.dma("sp", lambda h: h.dma_start(out=BIG1[:, :, :], in_=self.mg_d.rearrange("k p t -> p k t")), writes=["mgall"])
            GB = self.T("GB", [128, D])
            self.bcast_cond(S, GB, 64)
            xr = [self.T(f"xres{i}", [128, 512]) for i in range(3)]
            yo = [self.T(f"yo{i}", [128, 512]) for i in range(3)]
            tiles = [(6, 2)] + [(HALO + 128 * i, 128) for i in range(8)]
            cnt = 0
            for batch in (tiles[0:5], tiles[5:9]):
                for nch in range(8):
                    for kg in range(4):
                        slot = self.wslot()
                        wv = WA[:, slot, :].rearrange("p (k c) -> p k c", c=512)
                        self.wload(S, wv, w_out[kg * 1024:(kg + 1) * 1024, nch * 512:(nch + 1) * 512].rearrange("(k p) c -> p k c", p=128), slot)
                        for ti, (r0, M) in enumerate(batch):
                            for kt in range(8):
                                k = kg * 8 + kt
                                S.op("pe", lambda h, ti=ti, r0=r0, M=M, k=k, kt=kt, wv=wv, kg=kg: h.matmul(
                                    PS[0:M, ti, :], BIG1[:, k, r0:r0 + M], wv[:, kt, :], start=(k == 0), stop=(k == KT - 1)),
                                     reads=["mgall", ("wa", slot)], writes=[("ps", ti)])
                    for ti, (r0, M) in enumerate(batch):
                        i3 = cnt % 3
                        cnt += 1
                        xt, yt = xr[i3], yo[i3]
                        S.dma("act", lambda h, xt=xt, r0=r0, M=M, nch=nch: h.dma_start(out=xt[0:M, :], in_=self.xh[r0:r0 + M, nch * 512:(nch + 1) * 512]),
                              writes=[("xres", i3)])
                        S.op("dve", lambda h, yt=yt, ti=ti, M=M, nch=nch: h.tensor_tensor(out=yt[0:M, :], in0=PS[0:M, ti, :],
                                                                                          in1=GB[0:M, nch * 512:(nch + 1) * 512], op=ALU.mult),
                             reads=[("ps", ti), "GB"], writes=[("yo", i3)])
                        S.op("dve", lambda h, yt=yt, xt=xt, M=M: h.scalar_tensor_tensor(out=yt[0:M, :], in0=xt[0:M, :], scalar=ALPHA, in1=yt[0:M, :],
                                                                                          op0=ALU.mult, op1=ALU.add),
                             reads=[("yo", i3), ("xres", i3)], writes=[("yo", i3)])
                        S.dma("sp", lambda h, yt=yt, r0=r0, M=M, nch=nch: h.dma_start(out=self.y1_d[r0:r0 + M, nch * 512:(nch + 1) * 512], in_=yt[0:M, :]),
                              reads=[("yo", i3)], writes=["y1_d"])

    def p_ln_mid(self):
        with self.phase() as S:
            self.lnJ = self.T("lnJ", [128, D], BF16)
            self.lnJ2 = self.T("lnJ2", [128, D])
            self.lnst = self.T("lnst", [128, 4])
            G = self.T("lnG", [128, D]); Bt = self.T("lnB", [128, D])
            S.dma("sp", lambda h: h.dma_start(out=G[:, :], in_=self.ln1_g.partition_broadcast(128)), writes=["lnG"])
            S.dma("sp", lambda h: h.dma_start(out=Bt[:, :], in_=self.ln1_b.partition_broadcast(128)), writes=["lnB"])
            XT = [self.T(f"XT{i}", [128, D]) for i in range(2)]
            nev = [0]
            tiles = [(6, 2)] + [(HALO + 128 * i, 128) for i in range(8)]
            for ti, (r0, M) in enumerate(tiles):
                X = XT[ti % 2]
                tag = ti % 2
                S.dma("act", lambda h, X=X, r0=r0, M=M: h.dma_start(out=X[0:M, :], in_=self.y1_d[r0:r0 + M, :]), writes=[("X", tag)])
                self.ln_rows(S, X, M, tag, affine=(G, Bt))
                S.dma("sp", lambda h, X=X, r0=r0, M=M: h.dma_start(out=self.x1_d[r0:r0 + M, :], in_=X[0:M, :]), reads=[("X", tag)], writes=["x1_d"])
                self.ln_rows(S, X, M, tag)
                self.rows_to_T(S, X, M, tag, r0, 128, 96, nev)
            self.dbg(S, "h2T", self.BIG1[:, :, :], [128, KT, NT], BF16, reads=[("big1", ft) for ft in range(KT)])

    def p_ffn(self):
        PS, WA, BIG1, FL = self.PS, self.WA, self.BIG1, self.FL
        w_up = self.ffn_w_up[0]
        w_dn = self.ffn_w_down[0]
        NH = 514
        HC = 257
        hk = [("big1", ft) for ft in range(KT)]
        with self.phase() as S:
            FCW = self.T("FCW", [128, 516])
            FCB = self.T("FCB", [128, 172])
            self.load_cols(S, FCW, self.ffn_conv_w[0].rearrange("k (c p) -> (k c) p", p=128), 516, "FCW", "tfw")
            self.load_cols(S, FCB, self.ffn_conv_b[0].rearrange("(c p) -> c p", p=128), 172, "FCB", "tfb")
            GBc = self.T("GBc", [128, 512])
            Dg = self.T("ffn_diag", [128, 2, 128])
            aT = self.T("aT", [128, NFT, 512], BF16)
            us = [self.T(f"us{i}", [128, NH]) for i in range(2)]
            cgs = [self.T("cg0", [128, 512]), self.T("cg1", [128, 512])]; cv = self.T("cv", [128, 512])
            xr = [self.T(f"xres{i}", [128, 512]) for i in range(2)]
            yo = [self.T(f"yo{i}", [128, 512]) for i in range(2)]
            cnt = 0
            for hf in range(2):
                c0 = 6 + 512 * hf
                for jp in range(NFT // 2):
                    for which in range(2):
                        col0 = which * DFF + jp * 256
                        pbase = 4 * which
                        for kg in range(2):
                            slot = self.wslot()
                            wv = WA[:, slot, :].rearrange("p (k c) -> p k c", c=256)
                            self.wload(S, wv, w_up[kg * 2048:(kg + 1) * 2048, col0:col0 + 256].rearrange("(k p) c -> p k c", p=128), slot)
                            for tt in range(2):
                                for kt in range(16):
                                    k = kg * 16 + kt
                                    for b in range(2):
                                        bank = pbase + tt * 2 + b
                                        S.op("pe", lambda h, kt=kt, k=k, b=b, bank=bank, wv=wv, tt=tt: h.matmul(
                                            PS[:, bank, 0:HC], wv[:, kt, tt * 128:(tt + 1) * 128], BIG1[:, k, c0 + b * HC:c0 + (b + 1) * HC],
                                            start=(k == 0), stop=(k == KT - 1)),
                                             reads=[("wa", slot)] + hk, writes=[("ps", bank)])
                        for tt in range(2):
                            j = jp * 2 + tt
                            tile = which * NFT + j
                            pb = pbase + tt * 2
                            u = us[which]
                            uk = ("us", which)
                            S.op("act", lambda h, u=u, pb=pb: h.activation(out=u[:, :].rearrange("p (b c) -> p b c", c=HC), in_=PS[:, pb:pb + 2, 0:HC],
                                                                           func=AF.Identity),
                                 reads=[("ps", pb), ("ps", pb + 1)], writes=[uk])
                            if hf == 0:
                                S.op("dve", lambda h, u=u: h.tensor_scalar(out=u[:, 0:2], in0=u[:, 0:2], scalar1=FL[:, 0:1], scalar2=None, op0=ALU.mult),
                                     reads=[uk, "FL"], writes=[uk])
                            dst = cgs[tt] if which == 0 else cv
                            dk = ("cg", tt) if which == 0 else "cv"
                            fw = lambda k, tile=tile: FCW[:, k * 172 + tile:k * 172 + tile + 1]
                            S.op("dve", lambda h, u=u, dst=dst, fw=fw, tile=tile: h.tensor_scalar(out=dst[:, :], in0=u[:, 0:512], scalar1=fw(0),
                                                                                                  scalar2=FCB[:, tile:tile + 1], op0=ALU.mult, op1=ALU.add),
                                 reads=[uk, "FCW", "FCB"], writes=[dk])
                            for k in (1, 2):
                                S.op("dve", lambda h, u=u, dst=dst, fw=fw, k=k: h.scalar_tensor_tensor(out=dst[:, :], in0=u[:, k:k + 512], scalar=fw(k),
                                                                                                       in1=dst[:, :], op0=ALU.mult, op1=ALU.add),
                                     reads=[uk, dk, "FCW"], writes=[dk])
                            if which == 0:
                                S.op("act", lambda h, dst=dst: h.activation(out=dst[:, :], in_=dst[:, :], func=AF.Gelu), reads=[dk], writes=[dk])
                            else:
                                S.op("dve", lambda h, j=j, tt=tt: h.tensor_tensor(out=aT[:, j, :], in0=cgs[tt][:, :], in1=cv[:, :], op=ALU.mult),
                                     reads=[("cg", tt), "cv"], writes=[("aT", j)])
                ak = [("aT", j) for j in range(NFT)]
                for nch in range(8):
                    ngr = (NFT + 7) // 8
                    for kg in range(ngr):
                        nk = min(8, NFT - kg * 8)
                        slot = self.wslot()
                        wv = WA[:, slot, :].rearrange("p (k c) -> p k c", c=512)
                        self.wload(S, wv[:, 0:nk, :], w_dn[kg * 1024:kg * 1024 + nk * 128, nch * 512:(nch + 1) * 512].rearrange("(k p) c -> p k c", p=128), slot)
                        for ti in range(4):
                            for kt in range(nk):
                                k = kg * 8 + kt
                                S.op("pe", lambda h, ti=ti, k=k, kt=kt, wv=wv: h.matmul(
                                    PS[:, 4 + ti, :], aT[:, k, ti * 128:(ti + 1) * 128], wv[:, kt, :], start=(k == 0), stop=(k == NFT - 1)),
                                     reads=ak + [("wa", slot)], writes=[("ps", 4 + ti)])
                    for q4 in range(4):
                        ft = nch * 4 + q4
                        s2 = q4 % 2
                        S.op("dve", lambda h, ft=ft, s2=s2: h.tensor_scalar(out=Dg[:, s2, :], in0=self.ident[:, :],
                                                                             scalar1=self.COND[:, 160 + ft:161 + ft], scalar2=None, op0=ALU.mult),
                             reads=["ident", "COND"], writes=[("fdg", s2)])
                        S.op("pe", lambda h, s2=s2: h.matmul(PS[:, 2 + s2, 0:128], self.ones[:, :], Dg[:, s2, :], start=True, stop=True),
                             reads=[("fdg", s2), "ones"], writes=[("ps", 2 + s2)])
                        S.op("act", lambda h, q4=q4, s2=s2: h.activation(out=GBc[:, q4 * 128:(q4 + 1) * 128], in_=PS[:, 2 + s2, 0:128], func=AF.Identity),
                             reads=[("ps", 2 + s2)], writes=["GBc"])
                    for ti in range(4):
                        r0 = HALO + 512 * hf + 128 * ti
                        i3 = cnt % 2
                        cnt += 1
                        xt, yt = xr[i3], yo[i3]
                        S.dma("act", lambda h, xt=xt, r0=r0, nch=nch: h.dma_start(out=xt[:, :], in_=self.x1_d[r0:r0 + 128, nch * 512:(nch + 1) * 512]),
                              writes=[("xres", i3)])
                        S.op("dve", lambda h, yt=yt, ti=ti, nch=nch: h.tensor_tensor(out=yt[:, :], in0=PS[:, 4 + ti, :],
                                                                                     in1=GBc[:, :], op=ALU.mult),
                             reads=[("ps", 4 + ti), "GBc"], writes=[("yo", i3)])
                        S.op("dve", lambda h, yt=yt, xt=xt: h.scalar_tensor_tensor(out=yt[:, :], in0=xt[:, :], scalar=ALPHA, in1=yt[:, :],
                                                                                     op0=ALU.mult, op1=ALU.add),
                             reads=[("yo", i3), ("xres", i3)], writes=[("yo", i3)])
                        S.dma("sp", lambda h, yt=yt, r0=r0, nch=nch: h.dma_start(out=self.y2_d[r0:r0 + 128, nch * 512:(nch + 1) * 512], in_=yt[:, :]),
                              reads=[("yo", i3)], writes=["y2_d"])

    def p_ln_final(self):
        with self.phase() as S:
            self.lnJ = self.T("lnJ", [128, D], BF16)
            self.lnJ2 = self.T("lnJ2", [128, D])
            self.lnst = self.T("lnst", [128, 4])
            G = self.T("lnG", [128, D]); Bt = self.T("lnB", [128, D])
            S.dma("sp", lambda h: h.dma_start(out=G[:, :], in_=self.ln2_g.partition_broadcast(128)), writes=["lnG"])
            S.dma("sp", lambda h: h.dma_start(out=Bt[:, :], in_=self.ln2_b.partition_broadcast(128)), writes=["lnB"])
            XT = [self.T(f"XT{i}", [128, D]) for i in range(2)]
            for ti in range(8):
                r0 = HALO + 128 * ti
                X = XT[ti % 2]
                tag = ti % 2
                S.dma("act", lambda h, X=X, r0=r0: h.dma_start(out=X[:, :], in_=self.y2_d[r0:r0 + 128, :]), writes=[("X", tag)])
                self.ln_rows(S, X, 128, tag, affine=(G, Bt))
                S.dma("sp", lambda h, X=X, ti=ti: h.dma_start(out=self.out[ti * 128:(ti + 1) * 128, :], in_=X[:, :]), reads=[("X", tag)], writes=["out"])


WEIGHT_NAMES = ["w_ada", "b_ada", "w_in", "conv_lru_w", "conv_lru_b", "lru_wr", "lru_br", "lru_wi", "lru_bi",
                "lru_lambda", "ssm_lam_re", "ssm_lam_im", "ssm_b_re", "ssm_b_im", "ssm_c_re", "ssm_c_im", "ssm_d",
                "ssm_log_step", "w_glu", "b_glu", "w_proj_lru", "w_proj_ssm", "w_out", "ln1_g", "ln1_b",
                "ffn_w_up", "ffn_conv_w", "ffn_conv_b", "ffn_w_down", "ln2_g", "ln2_b"]


def make_in_maps(inputs, used=None, ncore=NCORE):
    x = np.asarray(inputs["x"], dtype=np.float32)
    c = np.asarray(inputs["c"], dtype=np.float32)
    used = set(WEIGHT_NAMES + ["xh", "cT", "flags"]) if used is None else set(used)
    shared = {k: np.ascontiguousarray(np.asarray(inputs[k], dtype=np.float32)) for k in WEIGHT_NAMES if k in used}
    in_maps = []
    for k in range(ncore):
        b, j = k // 4, k % 4
        s = 1024 * j
        xh = np.zeros((NT, D), np.float32)
        if j == 0:
            xh[HALO:] = x[b, 0:1024]
        else:
            xh[:] = x[b, s - HALO:s + 1024]
        cT = np.ascontiguousarray(c[b].reshape(KT, 128).T)
        flags = np.zeros((1, 18), np.float32)
        flags[0, 17] = float(b)
        flags[0, 0] = 0.0 if j == 0 else 1.0
        nmask = 8 if j == 0 else 3
        flags[0, 1:9] = 1.0
        flags[0, 1:1 + nmask] = 0.0
        for r in range(NCORE):
            if r // 4 == b and r < k:
                flags[0, 9 + r] = 1.0
        m = dict(shared)
        for nm, arr in (("xh", xh), ("cT", cT), ("flags", flags)):
            if nm in used:
                m[nm] = arr
        in_maps.append(m)
    return in_maps


def kernel(**inputs):
    bld = Builder()
    nc = bld.build()
    in_maps = make_in_maps(inputs)
    res = run_bass_kernel_spmd(nc, in_maps, core_ids=list(range(NCORE)))
    out = np.zeros((2, 4096, D), np.float32)
    for k in range(NCORE):
        b, j = k // 4, k % 4
        out[b, 1024 * j:1024 * (j + 1)] = np.asarray(res.results[k]["out"], dtype=np.float32)
    return out
```

```python
import contextlib
import math
import numpy as np
import concourse.bass as bass
import concourse.mybir as mybir
from concourse.bass_utils import run_bass_kernel_spmd

F32 = mybir.dt.float32
BF16 = mybir.dt.bfloat16
AF = mybir.ActivationFunctionType
ALU = mybir.AluOpType
AX = mybir.AxisListType

NCORE = 8
D = 4096
KT = 32
NT = 1032
HALO = 8
CW = 344
WL = 2048
DFF = 11008
NFT = 86
NIN = 14336
LN_EPS = 1e-5
ALPHA = 2.0 ** 0.25
CB = 24
NBK = NT // CB
EXP_POS = 1026
S5_EXP = 1023

ENGS = ("pe", "act", "dve", "pool", "sp")
SEG = 12000
NDSEM = 12


class _Rec:
    def __init__(self):
        self.call = None

    def __getattr__(self, name):
        def f(*a, **k):
            assert self.__dict__["call"] is None
            self.__dict__["call"] = (name, a, k)
            return self
        return f


def _freeze(fn):
    if fn is None:
        return None
    rec = _Rec()
    fn(rec)
    name, a, k = rec.call
    return lambda h: getattr(h, name)(*a, **k)


class Sched:
    def __init__(self, nc):
        self.nc = nc
        self.ops = {e: [] for e in ENGS}
        self.lastw = {}
        self.readers = {}
        self.ndma = {e: 0 for e in ENGS}

    def _add(self, eng, fn, reads, writes, dma):
        writes = list(writes)
        for k in reads:
            if isinstance(k, tuple) and k[0] == "ps" and k not in writes:
                writes.append(k)
        deps = []
        rawset = set()
        for k in reads:
            t = self.lastw.get(k)
            if t is not None:
                deps.append(t)
                rawset.add(t)
        for k in writes:
            t = self.lastw.get(k)
            if t is not None:
                deps.append(t)
            deps.extend(self.readers.get(k, ()))
        idx = len(self.ops[eng])
        if dma:
            k = self.ndma[eng]
            self.ndma[eng] += 1
            tok = ("d", eng, k)
            if k >= NDSEM:
                deps.append(("d", eng, k - NDSEM))
        else:
            tok = ("e", eng, idx)
        d2 = []
        for t in set(deps):
            if t[0] == "e" and t[1] == eng:
                if dma:
                    d2.append(t)
                    continue
                if t not in rawset or eng == "pe":
                    continue
            d2.append(t)
        self.ops[eng].append(dict(fn=_freeze(fn), deps=d2, dma=dma, inc=False, tok=tok))
        for k in reads:
            self.readers.setdefault(k, []).append(tok)
        for k in writes:
            self.lastw[k] = tok
            self.readers[k] = []
        return tok

    def op(self, eng, fn, reads=(), writes=()):
        return self._add(eng, fn, reads, writes, False)

    def barrier(self):
        toks = []
        for e in ENGS:
            for i in range(len(self.ops[e]) - 1, -1, -1):
                o = self.ops[e][i]
                if not o["dma"] and o["fn"] is not None:
                    toks.append(("e", e, i))
                    break
            n = self.ndma[e]
            for k in range(max(0, n - NDSEM), n):
                toks.append(("d", e, k))
        for e in ENGS:
            self.ops[e].append(dict(fn=None, deps=[t for t in toks if not (t[0] == "e" and t[1] == e)],
                                    dma=False, inc=False, tok=None))
        self.lastw = {}
        self.readers = {}

    def dma(self, eng, fn, reads=(), writes=()):
        return self._add(eng, fn, reads, writes, True)

    def emit(self, final_wait_eng="sp"):
        nc = self.nc
        for e in ENGS:
            for o in self.ops[e]:
                for t in o["deps"]:
                    if t[0] == "e":
                        self.ops[t[1]][t[2]]["inc"] = True
        for e in ENGS:
            for o in reversed(self.ops[e]):
                if not o["dma"] and o["fn"] is not None:
                    o["inc"] = True
                    break
        cnt = {}
        nseg = {}
        for e in ENGS:
            c = 0
            for i, o in enumerate(self.ops[e]):
                if o["inc"] and not o["dma"]:
                    cnt[(e, i)] = c
                    c += 1
            nseg[e] = max(1, (c + SEG - 1) // SEG)
        esem = {e: [nc.alloc_semaphore(name=nc.make_name(f"s_{e}_{j}", True)) for j in range(nseg[e])]
                for e in ENGS}
        dsem = {e: [nc.alloc_semaphore(name=nc.make_name(f"d_{e}_{j}", True)) for j in range(NDSEM)]
                for e in ENGS if self.ndma[e] > 0}

        def target(t):
            if t[0] == "e":
                c = cnt[(t[1], t[2])]
                return esem[t[1]][c // SEG], (c % SEG) + 1, ("e", t[1], c // SEG)
            k = t[2]
            return dsem[t[1]][k % NDSEM], 16 * (k // NDSEM + 1), ("d", t[1], k % NDSEM)

        def run(e, h):
            waited = {}
            for i, o in enumerate(self.ops[e]):
                for t in o["deps"]:
                    s, v, key = target(t)
                    if waited.get(key, 0) >= v:
                        continue
                    waited[key] = v
                    h.wait_ge(s, v)
                if o["fn"] is None:
                    continue
                ins = o["fn"](h)
                if o["dma"]:
                    s, v, _ = target(o["tok"])
                    ins.then_inc(s, 16)
                elif o["inc"]:
                    c = cnt[(e, i)]
                    ins.then_inc(esem[e][c // SEG], 1)
            if e == final_wait_eng:
                for e2 in ENGS:
                    for i2 in range(len(self.ops[e2]) - 1, -1, -1):
                        if not self.ops[e2][i2]["dma"] and self.ops[e2][i2]["fn"] is not None:
                            s, v, _ = target(("e", e2, i2))
                            h.wait_ge(s, v)
                            break
                    n = self.ndma[e2]
                    for k in range(max(0, n - NDSEM), n):
                        s, v, _ = target(("d", e2, k))
                        h.wait_ge(s, v)

        with nc.Block() as block:
            hmap = {"pe": block.tensor, "act": block.scalar, "dve": block.vector,
                    "pool": block.gpsimd, "sp": block.sync}
            for e in ENGS:
                if not self.ops[e] and e != final_wait_eng:
                    continue
                hmap[e](lambda h, e=e: run(e, h))


def v3(ap2d):
    return ap2d.rearrange("p (b c) -> p b c", c=CW)


class Builder:
    def __init__(self, stage=99, debug=(), ncore=NCORE):
        self.ncore = ncore
        self.stage = stage
        self.debug = set(debug)
        self.nc = nc = bass.Bass("TRN2", target_bir_lowering=False)
        self.dbg_out = {}
        self._ishape = {
            "xh": [NT, D],
            "cT": [128, KT],
            "flags": [1, 18],
            "w_ada": [1, D, 6 * D],
            "b_ada": [1, 6 * D],
            "w_in": [1, D, NIN],
            "conv_lru_w": [1, 4, WL],
            "conv_lru_b": [1, WL],
            "lru_wr": [1, 16, 128, 128],
            "lru_br": [1, 16, 128],
            "lru_wi": [1, 16, 128, 128],
            "lru_bi": [1, 16, 128],
            "lru_lambda": [1, WL],
            "ssm_lam_re": [1, 128, 64],
            "ssm_lam_im": [1, 128, 64],
            "ssm_b_re": [1, 128, 64, 16],
            "ssm_b_im": [1, 128, 64, 16],
            "ssm_c_re": [1, 128, 16, 64],
            "ssm_c_im": [1, 128, 16, 64],
            "ssm_d": [1, WL],
            "ssm_log_step": [1, 128],
            "w_glu": [1, WL, WL],
            "b_glu": [1, WL],
            "w_proj_lru": [1, WL, D],
            "w_proj_ssm": [1, WL, D],
            "w_out": [1, D, D],
            "ln1_g": [1, D],
            "ln1_b": [1, D],
            "ffn_w_up": [1, D, 2 * DFF],
            "ffn_conv_w": [1, 3, 2 * DFF],
            "ffn_conv_b": [1, 2 * DFF],
            "ffn_w_down": [1, DFF, D],
            "ln2_g": [1, D],
            "ln2_b": [1, D],
        }
        self._idecl = {}
        self.out = nc.dram_tensor("out", [1024, D], F32, kind="ExternalOutput").ap()
        self.ag_in = nc.dram_tensor("ag_in", [128, 288], F32)
        self.ag_out = nc.dram_tensor("ag_out", [ncore * 128, 288], F32)
        self.mg_d = nc.dram_tensor("mg_d", [KT, 128, NT], BF16).ap()
        self.y1_d = nc.dram_tensor("y1_d", [NT, D], F32).ap()
        self.x1_d = nc.dram_tensor("x1_d", [NT, D], F32).ap()
        self.y2_d = nc.dram_tensor("y2_d", [NT, D], F32).ap()
        self.wa_n = 0
        self.nslots = 2
        self.ps_n = 0

    def __getattr__(self, name):
        ish = self.__dict__.get("_ishape", {})
        if name in ish:
            d = self.__dict__["_idecl"]
            if name not in d:
                d[name] = self.nc.dram_tensor(name, list(ish[name]), F32, kind="ExternalInput").ap()
            return d[name]
        raise AttributeError(name)

    def T(self, name, shape, dt=F32):
        return self.nc.alloc_sbuf_tensor(self.nc.make_name(name, True), list(shape), dt)

    def dbg(self, S, name, src_ap, shape, dt=F32, reads=()):
        if name not in self.debug:
            return
        t = self.nc.dram_tensor("dbg_" + name, list(shape), dt, kind="ExternalOutput").ap()
        self.dbg_out[name] = "dbg_" + name
        if len(shape) == 3 and shape[1] * shape[2] > 4096:
            for i in range(shape[1]):
                S.dma("sp", lambda h, i=i: h.dma_start(out=t[:, i, :], in_=src_ap[:, i, :]), reads=list(reads))
        else:
            S.dma("sp", lambda h: h.dma_start(out=t, in_=src_ap), reads=list(reads))

    @contextlib.contextmanager
    def scope(self):
        nc = self.nc
        saved = (nc.sbuf_base, nc.sbuf_top)
        yield
        self.S.barrier()
        nc.sbuf_base, nc.sbuf_top = saved

    @contextlib.contextmanager
    def phase(self):
        with self.scope():
            yield self.S

    def wslot(self):
        s = self.wa_n % self.nslots
        self.wa_n += 1
        return s

    def slot_ap(self, slot):
        if slot < 2:
            return self.WA[:, slot, :]
        return self.WB[:, slot - 2, :]

    @contextlib.contextmanager
    def more_slots(self):
        self.WB = self.T("WB", [128, 2, 4096], BF16)
        self.nslots = 4
        yield
        self.nslots = 2

    def wload(self, S, dst, src, slot):
        S.dma("pool", lambda h: h.dma_start(out=dst, in_=src), writes=[("wa", slot)])

    def load_cols(self, S, dst, src_rows, n, key, tmpname):
        PS, ident = self.PS, self.ident
        done = 0
        i = 0
        while done < n:
            m = min(128, n - done)
            tmp = self.T(f"{tmpname}{i}", [128, 128])
            k1 = (tmpname, i)
            S.dma("sp", lambda h, tmp=tmp, m=m, d=done: h.dma_start(out=tmp[0:m, :], in_=src_rows[d:d + m, :]),
                  writes=[k1])
            S.op("pe", lambda h, tmp=tmp, m=m: h.transpose(out=PS[:, 7, 0:m], in_=tmp[0:m, :], identity=ident[0:m, 0:m]),
                 reads=[k1, "ident"], writes=[("ps", 7)])
            S.op("dve", lambda h, m=m, d=done: h.tensor_copy(out=dst[:, d:d + m], in_=PS[:, 7, 0:m]),
                 reads=[("ps", 7)], writes=[key])
            done += m
            i += 1

    def fm_matmul(self, S, psb, wv, nk, rhs_fn, rkeys, slot):
        PS = self.PS
        for kt in range(nk):
            for b in range(3):
                S.op("pe", lambda h, kt=kt, b=b: h.matmul(PS[:, psb + b, 0:CW], wv[:, kt, :], rhs_fn(kt, b),
                                                          start=(kt == 0), stop=(kt == nk - 1)),
                     reads=[("wa", slot)] + list(rkeys), writes=[("ps", psb + b)])

    def big(self):
        b = 3 * (self.ps_n % 2)
        self.ps_n += 1
        return b

    def ln_rows(self, S, X, M, tag, affine=None):
        J = self.lnJ
        J2 = self.lnJ2
        st = self.lnst
        kx = ("X", tag)
        S.op("dve", lambda h: h.tensor_scalar(out=J[0:M, :], in0=X[0:M, :], scalar1=1.0 / D, scalar2=None,
                                              op0=ALU.mult, op1=ALU.add, accum_out=st[0:M, 0:1]),
             reads=[kx], writes=["lnJ", "st0"])
        S.op("dve", lambda h: h.tensor_scalar(out=X[0:M, :], in0=X[0:M, :], scalar1=st[0:M, 0:1], scalar2=None,
                                              op0=ALU.subtract),
             reads=[kx, "st0"], writes=[kx])
        S.op("act", lambda h: h.activation(out=J2[0:M, :], in_=X[0:M, :], func=AF.Square),
             reads=[kx], writes=["lnJ2"])
        S.op("dve", lambda h: h.tensor_scalar(out=J[0:M, :], in0=J2[0:M, :], scalar1=1.0 / D, scalar2=None,
                                              op0=ALU.mult, op1=ALU.add, accum_out=st[0:M, 1:2]),
             reads=["lnJ2"], writes=["lnJ", "st1"])
        S.op("act", lambda h: h.activation(out=st[0:M, 2:3], in_=st[0:M, 1:2], func=AF.Sqrt, bias=self.epsc[0:M, :]),
             reads=["st1"], writes=["st2"])
        S.op("dve", lambda h: h.reciprocal(out=st[0:M, 3:4], in_=st[0:M, 2:3]), reads=["st2"], writes=["st3"])
        S.op("act", lambda h: h.activation(out=X[0:M, :], in_=X[0:M, :], func=AF.Identity, scale=st[0:M, 3:4]),
             reads=[kx, "st3"], writes=[kx])
        if affine is not None:
            G, Bt = affine
            S.op("dve", lambda h: h.tensor_tensor(out=X[0:M, :], in0=X[0:M, :], in1=G[0:M, :], op=ALU.mult),
                 reads=[kx, "lnG"], writes=[kx])
            S.op("pool", lambda h: h.tensor_tensor(out=X[0:M, :], in0=X[0:M, :], in1=Bt[0:M, :], op=ALU.add),
                 reads=[kx, "lnB"], writes=[kx])

    def rows_to_T(self, S, X, M, tag, c0, sc_off, sh_off, n_evac):
        PS, ident, BIG1, COND = self.PS, self.ident, self.BIG1, self.COND
        kx = ("X", tag)
        for f4 in range(8):
            bank = 6 + (f4 % 2)
            for q in range(4):
                ft = f4 * 4 + q
                S.op("pe", lambda h, ft=ft, q=q, bank=bank: h.transpose(out=PS[:, bank, q * 128:q * 128 + M],
                                                                     in_=X[0:M, ft * 128:(ft + 1) * 128],
                                                                     identity=ident[0:M, 0:M]),
                     reads=[kx, "ident"], writes=[("ps", bank)])
            for q in range(4):
                ft = f4 * 4 + q
                if bank == 7:
                    S.op("act", lambda h, ft=ft, q=q, bank=bank: h.activation(
                        out=BIG1[:, ft, c0:c0 + M], in_=PS[:, bank, q * 128:q * 128 + M], func=AF.Identity,
                        scale=COND[:, sc_off + ft:sc_off + ft + 1], bias=COND[:, sh_off + ft:sh_off + ft + 1]),
                         reads=[("ps", bank), "COND"], writes=[("big1", ft)])
                else:
                    S.op("dve", lambda h, ft=ft, q=q, bank=bank: h.tensor_scalar(
                        out=BIG1[:, ft, c0:c0 + M], in0=PS[:, bank, q * 128:q * 128 + M],
                        scalar1=COND[:, sc_off + ft:sc_off + ft + 1], scalar2=COND[:, sh_off + ft:sh_off + ft + 1],
                        op0=ALU.mult, op1=ALU.add),
                         reads=[("ps", bank), "COND"], writes=[("big1", ft)])

    def bcast_cond(self, S, GB, off):
        PS, ident, ones, COND = self.PS, self.ident, self.ones, self.COND
        Dg = self.T("bc_diag", [128, 2, 128])
        for ft in range(KT):
            s = ft % 2
            S.op("dve", lambda h, ft=ft, s=s: h.tensor_scalar(out=Dg[:, s, :], in0=ident[:, :],
                                                              scalar1=COND[:, off + ft:off + ft + 1], scalar2=None,
                                                              op0=ALU.mult),
                 reads=["ident", "COND"], writes=[("bcd", s)])
            S.op("pe", lambda h, s=s: h.matmul(PS[:, 6 + s, 0:128], ones[:, :], Dg[:, s, :], start=True, stop=True),
                 reads=[("bcd", s), "ones"], writes=[("ps", 6 + s)])
            S.op("act", lambda h, ft=ft, s=s: h.activation(out=GB[:, ft * 128:(ft + 1) * 128], in_=PS[:, 6 + s, 0:128],
                                                           func=AF.Identity),
                 reads=[("ps", 6 + s)], writes=["GB"])

    def build(self):
        nc = self.nc
        stage = self.stage
        self.PS = PS = nc.alloc_psum_tensor("PS", [128, 8, 512], F32)
        self.ident = ident = self.T("ident", [128, 128])
        self.ones = ones = self.T("ones", [128, 128])
        self.FL = FL = self.T("FL", [128, 18])
        self.COND = COND = self.T("COND", [128, 192])
        self.GATH = GATH = self.T("GATH", [128, 288])
        self.epsc = self.T("epsc", [128, 1])
        self.CW4 = self.T("CW4", [128, 64])
        self.CWB = self.T("CWB", [128, 16])
        self.BR = self.T("BR", [128, 16])
        self.BI = self.T("BI", [128, 16])
        self.C1 = self.T("C1", [128, 32])
        self.WR = self.T("WR", [128, 16, 128], BF16)
        self.WI = self.T("WI", [128, 16, 128], BF16)
        self.H0L = self.T("H0L", [128, 16])
        self.BIG1 = BIG1 = self.T("BIG1", [128, KT, NT], BF16)
        self.WA = WA = self.T("WA", [128, 2, 4096], BF16)

        self.S = Sched(nc)
        self.p_consts_ada()
        if stage >= 1:
            self.p_ln1()
        if stage >= 2:
            self.p_lru(1)
        if stage >= 3:
            with self.scope():
                self.U = self.T("U", [128, 16, NT], BF16)
                with self.scope():
                    self.s5_alloc()
                    self.p_s5_prep()
                    self.p_xs()
                    if stage >= 4:
                        self.p_s5_state()
                    if stage >= 5:
                        self.p_gather()
                    if stage >= 6:
                        self.p_s5(2)
                if stage >= 7:
                    self.p_glu()
                    self.ys = self.U
                    self.ya = self.T("ya", [128, 16, NT], BF16)
                    self.p_lru(2)
                if stage >= 8:
                    self.p_merge()
        if stage >= 9:
            self.p_wout()
        if stage >= 10:
            self.p_ln_mid()
        if stage >= 11:
            self.p_ffn()
        if stage >= 12:
            self.p_ln_final()
        self.S.emit()
        return nc

    def p_consts_ada(self):
        nc = self.nc
        PS, ident, ones, FL, COND, WA = self.PS, self.ident, self.ones, self.FL, self.COND, self.WA
        with self.phase() as S:
            S.op("pool", lambda h: h.memset(ident[:, :], 0.0), writes=["ident"])
            S.op("pool", lambda h: h.affine_select(out=ident[:, :], in_=ident[:, :], pattern=[[-1, 128]],
                                                   compare_op=ALU.not_equal, fill=1.0, base=0, channel_multiplier=1),
                 reads=["ident"], writes=["ident"])
            S.op("pool", lambda h: h.memset(ones[:, :], 1.0), writes=["ones"])
            S.op("pool", lambda h: h.memset(self.epsc[:, :], LN_EPS), writes=["epsc"])
            S.dma("sp", lambda h: h.dma_start(out=FL[:, :], in_=self.flags.partition_broadcast(128)), writes=["FL"])
            ct32 = self.T("ct32", [128, KT])
            cact = self.T("cact", [128, KT], BF16)
            S.dma("sp", lambda h: h.dma_start(out=ct32[:, :], in_=self.cT), writes=["ct32"])
            S.op("act", lambda h: h.activation(out=cact[:, :], in_=ct32[:, :], func=AF.Silu), reads=["ct32"], writes=["cact"])
            bada = self.T("bada", [128, 192])
            self.load_cols(S, bada, self.b_ada[0].rearrange("(o p) -> o p", p=128), 192, "bada", "tb")
            self.load_cols(S, self.CW4, self.conv_lru_w[0].rearrange("k (c p) -> (k c) p", p=128), 64, "CW4", "tcw")
            self.load_cols(S, self.CWB, self.conv_lru_b[0].rearrange("(c p) -> c p", p=128), 16, "CWB", "tcb")
            self.load_cols(S, self.BR, self.lru_br[0], 16, "BR", "tbr")
            self.load_cols(S, self.BI, self.lru_bi[0], 16, "BI", "tbi")
            lam = self.T("lamc", [128, 16])
            self.load_cols(S, lam, self.lru_lambda[0].rearrange("(c p) -> c p", p=128), 16, "lamc", "tlm")
            e1 = self.T("e1", [128, 16])
            S.op("act", lambda h: h.activation(out=e1[:, :], in_=lam[:, :], func=AF.Exp, scale=-1.0), reads=["lamc"], writes=["e1"])
            S.op("act", lambda h: h.activation(out=e1[:, :], in_=e1[:, :], func=AF.Ln, bias=ones[:, 0:1]), reads=["e1", "ones"], writes=["e1"])
            S.op("dve", lambda h: h.tensor_scalar(out=self.C1[:, 0:16], in0=e1[:, :], scalar1=-8.0, scalar2=None, op0=ALU.mult),
                 reads=["e1"], writes=["C1"])
            S.op("dve", lambda h: h.tensor_scalar(out=self.C1[:, 16:32], in0=e1[:, :], scalar1=-16.0, scalar2=None, op0=ALU.mult),
                 reads=["e1"], writes=["C1"])
            S.dma("pool", lambda h: h.dma_start(out=self.WR[:, :, :], in_=self.lru_wr[0].rearrange("h i j -> i h j")), writes=["WR"])
            S.dma("pool", lambda h: h.dma_start(out=self.WI[:, :, :], in_=self.lru_wi[0].rearrange("h i j -> i h j")), writes=["WI"])
            for og in range(48):
                for kg in range(4):
                    slot = self.wslot()
                    wv = self.slot_ap(slot).rearrange("p (k c) -> p k c", c=512)
                    self.wload(S, wv, self.w_ada[0][kg * 1024:(kg + 1) * 1024, og * 512:(og + 1) * 512].rearrange("(k p) c -> p k c", p=128), slot)
                    for q in range(4):
                        for kt in range(8):
                            k = kg * 8 + kt
                            S.op("pe", lambda h, wv=wv, kt=kt, k=k, q=q, og=og: h.matmul(PS[:, q, og:og + 1], wv[:, kt, q * 128:(q + 1) * 128],
                                                                                       cact[:, k:k + 1], start=(k == 0), stop=(k == KT - 1)),
                                 reads=[("wa", slot), "cact"], writes=[("ps", q)])
            CONDv = COND[:, :].rearrange("p (o q) -> p o q", q=4)
            badav = bada[:, :].rearrange("p (o q) -> p o q", q=4)
            for q in range(4):
                S.op("dve", lambda h, q=q: h.tensor_tensor(out=CONDv[:, :, q], in0=PS[:, q, 0:48], in1=badav[:, :, q], op=ALU.add),
                     reads=[("ps", q), "bada"], writes=["COND"])
            for sec in (1, 2, 4, 5):
                S.op("dve", lambda h, sec=sec: h.tensor_scalar(out=COND[:, sec * 32:(sec + 1) * 32], in0=COND[:, sec * 32:(sec + 1) * 32],
                                                               scalar1=1.0, scalar2=None, op0=ALU.add),
                     reads=["COND"], writes=["COND"])
            self.dbg(S, "cond", COND[:, :], [128, 192], reads=["COND"])

    def tok_tiles(self):
        tiles = [(0, HALO)]
        for i in range(8):
            tiles.append((HALO + 128 * i, 128))
        return tiles

    def p_ln1(self):
        BIG1, FL = self.BIG1, self.FL
        with self.phase() as S:
            self.lnJ = self.T("lnJ", [128, D], BF16)
            self.lnJ2 = self.T("lnJ2", [128, D])
            self.lnst = self.T("lnst", [128, 4])
            XT = [self.T(f"XT{i}", [128, D]) for i in range(2)]
            nev = [0]
            for ti, (r0, M) in enumerate(self.tok_tiles()):
                X = XT[ti % 2]
                tag = ti % 2
                S.dma("act", lambda h, X=X, r0=r0, M=M: h.dma_start(out=X[0:M, :], in_=self.xh[r0:r0 + M, :]),
                      writes=[("X", tag)])
                import os
                cut = int(os.environ.get("LN1_CUT", "9"))
                if cut >= 1:
                    self.ln_rows(S, X, M, tag)
                if cut >= 2:
                    self.rows_to_T(S, X, M, tag, r0, 32, 0, nev)
            if cut >= 3:
              S.op("dve", lambda h: h.tensor_scalar(out=BIG1[:, :, 0:HALO], in0=BIG1[:, :, 0:HALO], scalar1=FL[:, 0:1],
                                                  scalar2=None, op0=ALU.mult),
                 reads=[("big1", ft) for ft in range(KT)] + ["FL"], writes=[("big1", ft) for ft in range(KT)])
            self.dbg(S, "hT", BIG1[:, :, :], [128, KT, NT], BF16, reads=[("big1", ft) for ft in range(KT)])

    def p_lru(self, pas):
        nc = self.nc
        PS, BIG1, WA, FL, GATH = self.PS, self.BIG1, self.WA, self.FL, self.GATH
        w_in = self.w_in[0]
        with self.phase() as S, self.more_slots():
            xap = self.T("xap", [128, NT + 3])
            xc = self.T("xc", [128, NT])
            xcb = self.T("xcb", [128, NT], BF16)
            rr = self.T("rr", [128, NT])
            ii = self.T("ii", [128, NT])
            aa = self.T("aa", [128, NT])
            bb = self.T("bb", [128, NT])
            sm = self.T("sm", [128, 2])
            S.op("pool", lambda h: h.memset(xap[:, 0:3], 0.0), writes=["xap_pad"])
            hk = [("big1", ft) for ft in range(KT)]
            hh = xap[:, 3:NT + 3]
            gg = rr
            for ct in range(16):
                slot = self.wslot()
                wv = self.slot_ap(slot).rearrange("p (k c) -> p k c", c=128)
                self.wload(S, wv, w_in[:, ct * 128:(ct + 1) * 128].rearrange("(k p) c -> p k c", p=128), slot)
                pb = self.big()
                self.fm_matmul(S, pb, wv, KT, lambda kt, b: BIG1[:, kt, b * CW:(b + 1) * CW], hk, slot)
                pk = [("ps", pb + b) for b in range(3)]
                S.op("act", lambda h, pb=pb: h.activation(out=v3(xap[:, 3:NT + 3]), in_=PS[:, pb:pb + 3, 0:CW], func=AF.Identity),
                     reads=pk, writes=["xap"])
                cw = lambda k, ct=ct: self.CW4[:, k * 16 + ct:k * 16 + ct + 1]
                S.op("dve", lambda h, ct=ct, cw=cw: h.tensor_scalar(out=xc[:, :], in0=xap[:, 0:NT], scalar1=cw(0),
                                                                    scalar2=self.CWB[:, ct:ct + 1], op0=ALU.mult, op1=ALU.add),
                     reads=["xap", "xap_pad", "CW4", "CWB"], writes=["xc"])
                for k in (1, 2, 3):
                    S.op("dve", lambda h, k=k, cw=cw: h.scalar_tensor_tensor(out=xc[:, :], in0=xap[:, k:k + NT], scalar=cw(k),
                                                                             in1=xc[:, :], op0=ALU.mult, op1=ALU.add),
                         reads=["xap", "xap_pad", "xc", "CW4"], writes=["xc"])
                S.op("act", lambda h: h.activation(out=xcb[:, :], in_=xc[:, :], func=AF.Identity), reads=["xc"], writes=["xcb"])
                pr = self.big()
                for b in range(3):
                    S.op("pe", lambda h, b=b, ct=ct, pr=pr: h.matmul(PS[:, pr + b, 0:CW], self.WR[:, ct, :], xcb[:, b * CW:(b + 1) * CW],
                                                                     start=True, stop=True),
                         reads=["xcb", "WR"], writes=[("ps", pr + b)])
                S.op("act", lambda h, ct=ct, pr=pr: h.activation(out=v3(rr[:, :]), in_=PS[:, pr:pr + 3, 0:CW], func=AF.Sigmoid,
                                                                 bias=self.BR[:, ct:ct + 1]),
                     reads=[("ps", pr + b) for b in range(3)] + ["BR"], writes=["rr"])
                pi = self.big()
                for b in range(3):
                    S.op("pe", lambda h, b=b, ct=ct, pi=pi: h.matmul(PS[:, pi + b, 0:CW], self.WI[:, ct, :], xcb[:, b * CW:(b + 1) * CW],
                                                                     start=True, stop=True),
                         reads=["xcb", "WI"], writes=[("ps", pi + b)])
                S.op("act", lambda h, ct=ct, pi=pi: h.activation(out=v3(ii[:, :]), in_=PS[:, pi:pi + 3, 0:CW], func=AF.Sigmoid,
                                                                 bias=self.BI[:, ct:ct + 1]),
                     reads=[("ps", pi + b) for b in range(3)] + ["BI"], writes=["ii"])
                S.op("act", lambda h, ct=ct: h.activation(out=aa[:, :], in_=rr[:, :], func=AF.Exp, scale=self.C1[:, ct:ct + 1]),
                     reads=["rr", "C1"], writes=["aa"])
                S.op("act", lambda h, ct=ct: h.activation(out=bb[:, :], in_=rr[:, :], func=AF.Exp, scale=self.C1[:, 16 + ct:17 + ct]),
                     reads=["rr", "C1"], writes=["bb"])
                S.op("act", lambda h: h.activation(out=bb[:, :], in_=bb[:, :], func=AF.Sqrt, scale=-1.0, bias=self.ones[:, 0:1]),
                     reads=["bb", "ones"], writes=["bb"])
                S.op("dve", lambda h: h.tensor_tensor(out=ii[:, :], in0=ii[:, :], in1=xc[:, :], op=ALU.mult),
                     reads=["ii", "xc"], writes=["ii"])
                S.op("dve", lambda h: h.tensor_tensor(out=bb[:, :], in0=bb[:, :], in1=ii[:, :], op=ALU.mult),
                     reads=["bb", "ii"], writes=["bb"])
                S.op("dve", lambda h: h.tensor_tensor(out=bb[:, 0:8], in0=bb[:, 0:8], in1=FL[:, 1:9], op=ALU.mult),
                     reads=["bb", "FL"], writes=["bb"])
                if pas == 2:
                    S.op("dve", lambda h, ct=ct: h.tensor_copy(out=bb[:, 2:3], in_=self.H0L[:, ct:ct + 1]),
                         reads=["bb", "H0L"], writes=["bb"])
                S.op("dve", lambda h: h.tensor_tensor_scan(out=hh, data0=aa[:, :], data1=bb[:, :], initial=0.0,
                                                           op0=ALU.mult, op1=ALU.add),
                     reads=["aa", "bb", "xc"], writes=["xap"])
                if pas == 1:
                    S.op("dve", lambda h, ct=ct: h.tensor_copy(out=GATH[:, 16 + ct:17 + ct], in_=xap[:, 3 + EXP_POS:4 + EXP_POS]),
                         reads=["xap"], writes=["GATH"])
                    S.op("dve", lambda h: h.tensor_reduce(out=sm[:, 0:1], in_=rr[:, 3:EXP_POS + 1], axis=AX.X, op=ALU.add),
                         reads=["rr"], writes=["sm"])
                    S.op("act", lambda h, ct=ct: h.activation(out=GATH[:, ct:ct + 1], in_=sm[:, 0:1], func=AF.Exp,
                                                              scale=self.C1[:, ct:ct + 1]),
                         reads=["sm", "C1"], writes=["GATH"])
                    if ct == 0:
                        self.dbg(S, "lru_h0", hh, [128, NT], reads=["xap"])
                        self.dbg(S, "lru_a0", aa[:, :], [128, NT], reads=["aa"])
                        self.dbg(S, "lru_b0", bb[:, :], [128, NT], reads=["bb"])
                        self.dbg(S, "lru_xc0", xc[:, :], [128, NT], reads=["xc"])
                else:
                    slot = self.wslot()
                    wv2 = self.slot_ap(slot).rearrange("p (k c) -> p k c", c=128)
                    self.wload(S, wv2, w_in[:, 2048 + ct * 128:2048 + (ct + 1) * 128].rearrange("(k p) c -> p k c", p=128), slot)
                    pg = self.big()
                    self.fm_matmul(S, pg, wv2, KT, lambda kt, b: BIG1[:, kt, b * CW:(b + 1) * CW], hk, slot)
                    S.op("act", lambda h, pg=pg: h.activation(out=v3(gg[:, :]), in_=PS[:, pg:pg + 3, 0:CW], func=AF.Gelu),
                         reads=[("ps", pg + b) for b in range(3)], writes=["rr"])
                    S.op("dve", lambda h, ct=ct: h.tensor_tensor(out=self.ya[:, ct, :], in0=hh, in1=gg[:, :], op=ALU.mult),
                         reads=["xap", "rr"], writes=[("ya", ct)])
            if pas == 1:
                self.dbg(S, "gath_lru", GATH[:, 0:32], [128, 32], reads=["GATH"])
            else:
                self.dbg(S, "ya", self.ya[:, :, :], [128, 16, NT], BF16, reads=[("ya", ct) for ct in range(16)])

    def s5_alloc(self):
        self.CT = self.T("CT", [64, 2, 128, 16])
        self.BTb = self.T("BTb", [128, 16, 128], BF16)
        self.LD1 = self.T("LD1", [64, 2, 128])
        self.LD2 = self.T("LD2", [64, 2, 128])
        self.PM = self.T("PM", [128, 8])
        self.DCOL = self.T("DCOL", [128, 16])
        self.QRE = self.T("QRE", [64, 128])
        self.QIM = self.T("QIM", [64, 128])
        self.H0S = self.T("H0S", [64, 2, 128])
        self.LBAR = self.T("LBAR", [64, 2, 128])

    def p_s5_prep(self):
        PS, ident = self.PS, self.ident
        with self.phase() as S:
            T = self.T
            lr_n = T("lr_n", [128, 64]); li_n = T("li_n", [128, 64])
            S.dma("sp", lambda h: h.dma_start(out=lr_n[:, :], in_=self.ssm_lam_re[0]), writes=["lr_n"])
            S.dma("sp", lambda h: h.dma_start(out=li_n[:, :], in_=self.ssm_lam_im[0]), writes=["li_n"])
            LRE = T("LRE", [64, 128]); LIM = T("LIM", [64, 128])
            for src, dst, k in ((lr_n, LRE, "LRE"), (li_n, LIM, "LIM")):
                S.op("pe", lambda h, src=src: h.transpose(out=PS[0:64, 7, 0:128], in_=src[:, :], identity=ident[:, :]),
                     reads=[src is lr_n and "lr_n" or "li_n", "ident"], writes=[("ps", 7)])
                S.op("dve", lambda h, dst=dst: h.tensor_copy(out=dst[:, :], in_=PS[0:64, 7, 0:128]), reads=[("ps", 7)], writes=[k])
            STEP = T("STEP", [64, 128])
            S.dma("sp", lambda h: h.dma_start(out=STEP[:, :], in_=self.ssm_log_step.partition_broadcast(64)), writes=["STEP"])
            S.op("act", lambda h: h.activation(out=STEP[:, :], in_=STEP[:, :], func=AF.Exp), reads=["STEP"], writes=["STEP"])
            A_ = T("A_", [64, 128]); TH = T("TH", [64, 128]); MAG = T("MAG", [64, 128])
            S.op("dve", lambda h: h.tensor_tensor(out=A_[:, :], in0=LRE[:, :], in1=STEP[:, :], op=ALU.mult), reads=["LRE", "STEP"], writes=["A_"])
            S.op("dve", lambda h: h.tensor_tensor(out=TH[:, :], in0=LIM[:, :], in1=STEP[:, :], op=ALU.mult), reads=["LIM", "STEP"], writes=["TH"])
            S.op("act", lambda h: h.activation(out=MAG[:, :], in_=A_[:, :], func=AF.Exp), reads=["A_"], writes=["MAG"])
            hp = T("hpi", [64, 1])
            S.op("pool", lambda h: h.memset(hp[:, :], math.pi / 2), writes=["hpi"])
            sn = [T("sn0", [64, 128]), T("sn1", [64, 128])]
            cs = [T("cs0", [64, 128]), T("cs1", [64, 128])]
            tq = T("tq", [64, 128])
            S.op("act", lambda h: h.activation(out=sn[0][:, :], in_=TH[:, :], func=AF.Sin, scale=1.0 / 32), reads=["TH"], writes=["sn0"])
            S.op("act", lambda h: h.activation(out=cs[0][:, :], in_=TH[:, :], func=AF.Sin, scale=1.0 / 32, bias=hp[:, :]),
                 reads=["TH", "hpi"], writes=["cs0"])
            cur = 0
            for it in range(5):
                nx = 1 - cur
                S.op("dve", lambda h, cur=cur: h.tensor_tensor(out=tq[:, :], in0=sn[cur][:, :], in1=sn[cur][:, :], op=ALU.mult),
                     reads=[f"sn{cur}"], writes=["tq"])
                S.op("dve", lambda h, nx=nx: h.tensor_scalar(out=cs[nx][:, :], in0=tq[:, :], scalar1=-2.0, scalar2=1.0,
                                                             op0=ALU.mult, op1=ALU.add),
                     reads=["tq"], writes=[f"cs{nx}"])
                S.op("dve", lambda h, cur=cur, nx=nx: h.scalar_tensor_tensor(out=sn[nx][:, :], in0=sn[cur][:, :], scalar=2.0,
                                                                             in1=cs[cur][:, :], op0=ALU.mult, op1=ALU.mult),
                     reads=[f"sn{cur}", f"cs{cur}"], writes=[f"sn{nx}"])
                cur = nx
            LB = self.LBAR
            S.op("dve", lambda h, cur=cur: h.tensor_tensor(out=LB[:, 0, :], in0=MAG[:, :], in1=cs[cur][:, :], op=ALU.mult),
                 reads=["MAG", f"cs{cur}"], writes=["LB"])
            S.op("dve", lambda h, cur=cur: h.tensor_tensor(out=LB[:, 1, :], in0=MAG[:, :], in1=sn[cur][:, :], op=ALU.mult),
                 reads=["MAG", f"sn{cur}"], writes=["LB"])
            LD1, LD2 = self.LD1, self.LD2
            S.op("dve", lambda h: h.tensor_copy(out=LD1[:, 0, :], in_=LB[:, 0, :]), reads=["LB"], writes=["LD1"])
            S.op("dve", lambda h: h.tensor_copy(out=LD1[:, 1, :], in_=LB[:, 0, :]), reads=["LB"], writes=["LD1"])
            S.op("dve", lambda h: h.tensor_scalar(out=LD2[:, 0, :], in0=LB[:, 1, :], scalar1=-1.0, scalar2=None, op0=ALU.mult),
                 reads=["LB"], writes=["LD2"])
            S.op("dve", lambda h: h.tensor_copy(out=LD2[:, 1, :], in_=LB[:, 1, :]), reads=["LB"], writes=["LD2"])
            den = T("den", [64, 128]); t2 = T("t2", [64, 128]); xr = T("xr", [64, 128])
            qre, qim = self.QRE, self.QIM
            S.op("dve", lambda h: h.tensor_tensor(out=den[:, :], in0=LRE[:, :], in1=LRE[:, :], op=ALU.mult), reads=["LRE"], writes=["den"])
            S.op("dve", lambda h: h.tensor_tensor(out=t2[:, :], in0=LIM[:, :], in1=LIM[:, :], op=ALU.mult), reads=["LIM"], writes=["t2"])
            S.op("dve", lambda h: h.tensor_tensor(out=den[:, :], in0=den[:, :], in1=t2[:, :], op=ALU.add), reads=["den", "t2"], writes=["den"])
            S.op("dve", lambda h: h.reciprocal(out=den[:, :], in_=den[:, :]), reads=["den"], writes=["den"])
            S.op("dve", lambda h: h.tensor_scalar(out=xr[:, :], in0=LB[:, 0, :], scalar1=-1.0, scalar2=None, op0=ALU.add), reads=["LB"], writes=["xr"])
            S.op("dve", lambda h: h.tensor_tensor(out=qre[:, :], in0=xr[:, :], in1=LRE[:, :], op=ALU.mult), reads=["xr", "LRE"], writes=["qre"])
            S.op("dve", lambda h: h.tensor_tensor(out=t2[:, :], in0=LB[:, 1, :], in1=LIM[:, :], op=ALU.mult), reads=["LB", "LIM", "den"], writes=["t2"])
            S.op("dve", lambda h: h.tensor_tensor(out=qre[:, :], in0=qre[:, :], in1=t2[:, :], op=ALU.add), reads=["qre", "t2"], writes=["qre"])
            S.op("dve", lambda h: h.tensor_tensor(out=qre[:, :], in0=qre[:, :], in1=den[:, :], op=ALU.mult), reads=["qre", "den"], writes=["qre"])
            S.op("dve", lambda h: h.tensor_tensor(out=qim[:, :], in0=LB[:, 1, :], in1=LRE[:, :], op=ALU.mult), reads=["LB", "LRE"], writes=["qim"])
            S.op("dve", lambda h: h.tensor_tensor(out=t2[:, :], in0=xr[:, :], in1=LIM[:, :], op=ALU.mult), reads=["xr", "LIM", "qre"], writes=["t2"])
            S.op("dve", lambda h: h.tensor_tensor(out=qim[:, :], in0=qim[:, :], in1=t2[:, :], op=ALU.subtract), reads=["qim", "t2"], writes=["qim"])
            S.op("dve", lambda h: h.tensor_tensor(out=qim[:, :], in0=qim[:, :], in1=den[:, :], op=ALU.mult), reads=["qim", "den"], writes=["qim"])
            self.dbg(S, "lbar", LB[:, :, :], [64, 2, 128], reads=["LB"])
            self.dbg(S, "qre", qre[:, :], [64, 128], reads=["qre"])
            self.dbg(S, "qim", qim[:, :], [64, 128], reads=["qim"])
        for hb in range(2):
          with self.phase() as S:
            T = self.T
            qre, qim = self.QRE, self.QIM
            g0 = hb * 64
            bre = T("bre", [64, 64, 16]); bim = T("bim", [64, 64, 16])
            bnat = [T("bnat0", [128, 1024]), T("bnat1", [128, 1024])]
            for a, (srcd, dst, k) in enumerate(((self.ssm_b_re, bre, "bre"), (self.ssm_b_im, bim, "bim"))):
                S.dma("sp" if a == 0 else "act", lambda h, a=a, srcd=srcd: h.dma_start(out=bnat[a][:, :], in_=srcd[0].rearrange("g p n -> g (p n)")),
                      writes=[f"bnat{a}"])
                for n in range(16):
                    bank = 6 + (n % 2)
                    S.op("pe", lambda h, a=a, n=n, bank=bank: h.transpose(
                        out=PS[0:64, bank, 0:64], in_=bnat[a][g0:g0 + 64, :].rearrange("g (p n) -> g p n", n=16)[:, :, n],
                        identity=ident[g0:g0 + 64, g0:g0 + 64]),
                         reads=[f"bnat{a}", "ident"], writes=[("ps", bank)])
                    S.op("act", lambda h, dst=dst, n=n, bank=bank: h.activation(out=dst[:, :, n], in_=PS[0:64, bank, 0:64], func=AF.Identity),
                         reads=[("ps", bank)], writes=[k])
            BB = T("BB", [64, 2, 64, 16]); tb = T("tbb", [64, 64, 16])
            qb = lambda q: q[:, g0:g0 + 64, None].broadcast_to([64, 64, 16])
            S.op("dve", lambda h: h.tensor_tensor(out=BB[:, 0, :, :], in0=bre[:, :, :], in1=qb(qre), op=ALU.mult), reads=["bre"], writes=["BB0"])
            S.op("dve", lambda h: h.tensor_tensor(out=tb[:, :, :], in0=bim[:, :, :], in1=qb(qim), op=ALU.mult), reads=["bim"], writes=["tbb"])
            S.op("dve", lambda h: h.tensor_tensor(out=BB[:, 0, :, :], in0=BB[:, 0, :, :], in1=tb[:, :, :], op=ALU.subtract), reads=["BB0", "tbb"], writes=["BB0"])
            S.op("dve", lambda h: h.tensor_tensor(out=BB[:, 1, :, :], in0=bim[:, :, :], in1=qb(qre), op=ALU.mult), reads=["bim"], writes=["BB1"])
            S.op("dve", lambda h: h.tensor_tensor(out=tb[:, :, :], in0=bre[:, :, :], in1=qb(qim), op=ALU.mult), reads=["bre", "BB0"], writes=["tbb"])
            S.op("dve", lambda h: h.tensor_tensor(out=BB[:, 1, :, :], in0=BB[:, 1, :, :], in1=tb[:, :, :], op=ALU.add), reads=["BB1", "tbb"], writes=["BB1"])
            for gbl in range(8):
                gb = hb * 8 + gbl
                for ri in range(2):
                    bank = 6 + (ri % 2)
                    S.op("pe", lambda h, gbl=gbl, ri=ri, bank=bank: h.transpose(
                        out=PS[:, bank, 0:64], in_=BB[:, ri, gbl * 8:(gbl + 1) * 8, :].rearrange("p g n -> p (g n)"),
                        identity=ident[0:64, 0:64]),
                         reads=[f"BB{ri}", "ident"], writes=[("ps", bank)])
                    S.op("act", lambda h, gb=gb, ri=ri, bank=bank: h.activation(out=self.BTb[:, gb, ri * 64:(ri + 1) * 64],
                                                                                 in_=PS[:, bank, 0:64], func=AF.Identity),
                         reads=[("ps", bank)], writes=["BTb"])
            self.dbg(S, f"BB{hb}", BB[:, :, :, :], [64, 2, 64, 16], reads=["BB0", "BB1"])
            self.dbg(S, f"bre{hb}", bre[:, :, :], [64, 64, 16], reads=["bre"])
        with self.phase() as S:
            T = self.T
            cnat = [T("cnat0", [128, 1024]), T("cnat1", [128, 1024])]
            for ri, srcd in enumerate((self.ssm_c_re, self.ssm_c_im)):
                S.dma("sp" if ri == 0 else "act", lambda h, ri=ri, srcd=srcd: h.dma_start(out=cnat[ri][:, :], in_=srcd[0].rearrange("g n p -> g (n p)")),
                      writes=[f"cnat{ri}"])
                for n in range(16):
                    bank = 6 + (n % 2)
                    S.op("pe", lambda h, ri=ri, n=n, bank=bank: h.transpose(out=PS[0:64, bank, 0:128], in_=cnat[ri][:, n * 64:(n + 1) * 64],
                                                                            identity=ident[:, :]),
                         reads=[f"cnat{ri}", "ident"], writes=[("ps", bank)])
                    S.op("act", lambda h, ri=ri, n=n, bank=bank: h.activation(out=self.CT[:, ri, :, n], in_=PS[0:64, bank, 0:128],
                                                                              func=AF.Identity, scale=(1.0 if ri == 0 else -1.0)),
                         reads=[("ps", bank)], writes=["CT"])
            S.op("dve", lambda h: h.tensor_reduce(out=self.PM[:, :], in_=ident[:, :].rearrange("p (a b) -> p a b", b=16), axis=AX.X, op=ALU.add),
                 reads=["ident"], writes=["PM"])
            self.load_cols(S, self.DCOL, self.ssm_d[0].rearrange("(c p) -> c p", p=128), 16, "DCOL", "tsd")
            self.dbg(S, "CT", self.CT[:, :, :, :], [64, 2, 128, 16], reads=["CT"])

    def p_xs(self):
        PS, BIG1, WA, U = self.PS, self.BIG1, self.WA, self.U
        w_in = self.w_in[0]
        hk = [("big1", ft) for ft in range(KT)]
        with self.phase() as S, self.more_slots():
            for ct in range(16):
                slot = self.wslot()
                wv = self.slot_ap(slot).rearrange("p (k c) -> p k c", c=128)
                self.wload(S, wv, w_in[:, 4096 + ct * 128:4096 + (ct + 1) * 128].rearrange("(k p) c -> p k c", p=128), slot)
                pb = self.big()
                self.fm_matmul(S, pb, wv, KT, lambda kt, b: BIG1[:, kt, b * CW:(b + 1) * CW], hk, slot)
                pk = [("ps", pb + b) for b in range(3)]
                if ct % 2 == 0:
                    S.op("act", lambda h, pb=pb, ct=ct: h.activation(out=v3(U[:, ct, :]), in_=PS[:, pb:pb + 3, 0:CW], func=AF.Identity),
                         reads=pk, writes=[("U", ct)])
                else:
                    S.op("dve", lambda h, pb=pb, ct=ct: h.tensor_copy(out=v3(U[:, ct, :]), in_=PS[:, pb:pb + 3, 0:CW]),
                         reads=pk, writes=[("U", ct)])
            self.dbg(S, "U", U[:, :, :], [128, 16, NT], BF16, reads=[("U", ct) for ct in range(16)])

    def p_s5_state(self):
        PS, U, BTb, PM, LB = self.PS, self.U, self.BTb, self.PM, self.LBAR
        C1 = 12
        with self.phase() as S:
            T = self.T
            POW = T("POW", [64, 2, 128, C1]); BUb = T("BUb", [64, 2, 128, C1])
            P1 = T("P1", [64, 128, C1]); P2 = T("P2", [64, 128, C1]); P3 = T("P3", [64, 128, C1]); P4 = T("P4", [64, 128, C1])
            HS = T("HS", [64, 2, 128]); SS = T("SS", [64, 2, 128]); T1 = T("T1", [64, 2, 128]); T2 = T("T2", [64, 2, 128])
            M1 = T("M1", [64, 2, 128]); M2 = T("M2", [64, 2, 128]); M1p = T("M1p", [64, 2, 128]); M2p = T("M2p", [64, 2, 128])
            ta = T("pta", [64, 128]); tb2 = T("ptb", [64, 128])
            Um = [T(f"Um{i}", [128, 8, C1], BF16) for i in range(4)]
            S.op("dve", lambda h: h.memset(POW[:, 0, :, C1 - 1], 1.0), writes=[("pw", C1 - 1)])
            S.op("dve", lambda h: h.memset(POW[:, 1, :, C1 - 1], 0.0), writes=[("pw", C1 - 1)])
            S.op("dve", lambda h: h.memset(HS[:, :, :], 0.0), writes=["HS"])

            def cmul(dre, dim, are, aim, kin, kout):
                S.op("dve", lambda h: h.tensor_tensor(out=dre, in0=are, in1=LB[:, 0, :], op=ALU.mult), reads=[kin], writes=[kout + ("r",)])
                S.op("dve", lambda h: h.tensor_tensor(out=ta[:, :], in0=aim, in1=LB[:, 1, :], op=ALU.mult), reads=[kin], writes=["pta"])
                S.op("dve", lambda h: h.tensor_tensor(out=dre, in0=dre, in1=ta[:, :], op=ALU.subtract), reads=[kout + ("r",), "pta"], writes=[kout + ("r",)])
                S.op("dve", lambda h: h.tensor_tensor(out=dim, in0=are, in1=LB[:, 1, :], op=ALU.mult), reads=[kin], writes=[kout + ("i",)])
                S.op("dve", lambda h: h.tensor_tensor(out=tb2[:, :], in0=aim, in1=LB[:, 0, :], op=ALU.mult), reads=[kin], writes=["ptb"])
                S.op("dve", lambda h: h.tensor_tensor(out=dim, in0=dim, in1=tb2[:, :], op=ALU.add), reads=[kout + ("i",), "ptb"], writes=[kout + ("i",)])

            for jj in range(C1 - 1, 0, -1):
                cmul(POW[:, 0, :, jj - 1], POW[:, 1, :, jj - 1], POW[:, 0, :, jj], POW[:, 1, :, jj], ("pw", jj), ("pw", jj - 1))
                S.op("dve", lambda h, jj=jj: h.tensor_copy(out=ta[:, 0:1], in_=ta[:, 0:1]), reads=[("pw", jj - 1, "r"), ("pw", jj - 1, "i")], writes=[("pw", jj - 1)])
            cmul(M1[:, 0, :], M2[:, 1, :], POW[:, 0, :, 0], POW[:, 1, :, 0], ("pw", 0), ("mm",))
            S.op("dve", lambda h: h.tensor_copy(out=M1[:, 1, :], in_=M1[:, 0, :]), reads=[("mm", "r")], writes=["M1"])
            S.op("dve", lambda h: h.tensor_scalar(out=M2[:, 0, :], in0=M2[:, 1, :], scalar1=-1.0, scalar2=None, op0=ALU.mult), reads=[("mm", "i")], writes=["M2"])
            S.op("dve", lambda h: h.tensor_copy(out=M1p[:, 0, :], in_=POW[:, 0, :, C1 - 5]), reads=[("pw", C1 - 5)], writes=["M1p"])
            S.op("dve", lambda h: h.tensor_copy(out=M1p[:, 1, :], in_=POW[:, 0, :, C1 - 5]), reads=[("pw", C1 - 5)], writes=["M1p"])
            S.op("dve", lambda h: h.tensor_scalar(out=M2p[:, 0, :], in0=POW[:, 1, :, C1 - 5], scalar1=-1.0, scalar2=None, op0=ALU.mult), reads=[("pw", C1 - 5)], writes=["M2p"])
            S.op("dve", lambda h: h.tensor_copy(out=M2p[:, 1, :], in_=POW[:, 1, :, C1 - 5]), reads=[("pw", C1 - 5)], writes=["M2p"])
            pwk = [("pw", jj) for jj in range(C1)]
            nfull = (S5_EXP + 1) // C1
            for blk in range(nfull + 1):
                t0 = blk * C1
                n = C1 if blk < nfull else (S5_EXP + 1 - t0)
                for gb in range(16):
                    um = Um[gb % 4]
                    uk1 = ("Um", gb % 4)
                    S.op("pool", lambda h, um=um, gb=gb, t0=t0, n=n: h.tensor_tensor(
                        out=um[:, :, 0:n], in0=U[:, gb, None, t0:t0 + n].broadcast_to([128, 8, n]),
                        in1=PM[:, :, None].broadcast_to([128, 8, n]), op=ALU.mult),
                         reads=[("U", gb), "PM"], writes=[uk1])
                    bank = gb % 6
                    for ri in range(2):
                        for gl in range(8):
                            col = (ri * 8 + gl) * C1
                            S.op("pe", lambda h, um=um, gb=gb, ri=ri, gl=gl, col=col, bank=bank, n=n: h.matmul(
                                PS[0:64, bank, col:col + n], BTb[:, gb, ri * 64:(ri + 1) * 64], um[:, gl, 0:n], start=True, stop=True),
                                 reads=[uk1, "BTb"], writes=[("ps", bank)])
                    S.op("act", lambda h, gb=gb, bank=bank, n=n: h.activation(
                        out=BUb[:, :, gb * 8:(gb + 1) * 8, 0:n],
                        in_=PS[0:64, bank, 0:16 * C1].rearrange("p (r g t) -> p r g t", r=2, g=8)[:, :, :, 0:n], func=AF.Identity),
                         reads=[("ps", bank)], writes=["BUb"])
                pr = POW[:, 0, :, C1 - n:C1]; pi_ = POW[:, 1, :, C1 - n:C1]
                br = BUb[:, 0, :, 0:n]; bi = BUb[:, 1, :, 0:n]
                S.op("dve", lambda h, pr=pr, br=br, n=n: h.tensor_tensor(out=P1[:, :, 0:n], in0=pr, in1=br, op=ALU.mult), reads=pwk + ["BUb"], writes=["P1"])
                S.op("pool", lambda h, pi_=pi_, bi=bi, n=n: h.tensor_tensor(out=P2[:, :, 0:n], in0=pi_, in1=bi, op=ALU.mult), reads=pwk + ["BUb"], writes=["P2"])
                S.op("dve", lambda h, pr=pr, bi=bi, n=n: h.tensor_tensor(out=P3[:, :, 0:n], in0=pr, in1=bi, op=ALU.mult), reads=pwk + ["BUb"], writes=["P3"])
                S.op("pool", lambda h, pi_=pi_, br=br, n=n: h.tensor_tensor(out=P4[:, :, 0:n], in0=pi_, in1=br, op=ALU.mult), reads=pwk + ["BUb"], writes=["P4"])
                S.op("dve", lambda h, n=n: h.tensor_tensor(out=P1[:, :, 0:n], in0=P1[:, :, 0:n], in1=P2[:, :, 0:n], op=ALU.subtract), reads=["P1", "P2"], writes=["P1"])
                S.op("dve", lambda h, n=n: h.tensor_tensor(out=P3[:, :, 0:n], in0=P3[:, :, 0:n], in1=P4[:, :, 0:n], op=ALU.add), reads=["P3", "P4"], writes=["P3"])
                S.op("dve", lambda h, n=n: h.tensor_reduce(out=SS[:, 0, :], in_=P1[:, :, 0:n], axis=AX.X, op=ALU.add), reads=["P1"], writes=["SS0"])
                S.op("dve", lambda h, n=n: h.tensor_reduce(out=SS[:, 1, :], in_=P3[:, :, 0:n], axis=AX.X, op=ALU.add), reads=["P3"], writes=["SS1"])
                A1, A2 = (M1, M2) if blk < nfull else (M1p, M2p)
                a1k, a2k = ("M1", "M2") if blk < nfull else ("M1p", "M2p")
                S.op("dve", lambda h, A1=A1: h.tensor_tensor(out=T1[:, :, :], in0=HS[:, :, :], in1=A1[:, :, :], op=ALU.mult), reads=["HS", a1k], writes=["T1"])
                S.op("dve", lambda h, A2=A2: h.tensor_tensor(out=T2[:, 0, :], in0=HS[:, 1, :], in1=A2[:, 0, :], op=ALU.mult), reads=["HS", a2k], writes=["T2a"])
                S.op("dve", lambda h, A2=A2: h.tensor_tensor(out=T2[:, 1, :], in0=HS[:, 0, :], in1=A2[:, 1, :], op=ALU.mult), reads=["HS", a2k], writes=["T2b"])
                S.op("dve", lambda h: h.tensor_tensor(out=T1[:, :, :], in0=T1[:, :, :], in1=T2[:, :, :], op=ALU.add), reads=["T1", "T2a", "T2b"], writes=["T1"])
                S.op("dve", lambda h: h.tensor_tensor(out=HS[:, :, :], in0=T1[:, :, :], in1=SS[:, :, :], op=ALU.add), reads=["T1", "SS0", "SS1"], writes=["HS"])
            S.op("dve", lambda h: h.tensor_copy(out=self.GATH[0:64, 32:288].rearrange("p (r g) -> p r g", r=2), in_=HS[:, :, :]),
                 reads=["HS"], writes=["GATH"])
            self.dbg(S, "gath_s5", self.GATH[0:64, 32:288], [64, 256], reads=["GATH"])

    def p_s5(self, pas):
        PS, ident, U, CT, BTb, PM = self.PS, self.ident, self.U, self.CT, self.BTb, self.PM
        LD1, LD2 = self.LD1, self.LD2
        with self.phase() as S:
            hist = self.T("hist", [64, 2, 128, CB + 1])
            T1 = self.T("T1", [64, 2, 128]); T2 = self.T("T2", [64, 2, 128])
            Um = [self.T(f"Um{i}", [128, 8, CB], BF16) for i in range(4)]
            ytm = self.T("ytm", [CB, WL])
            ysb = self.T("ysb", [128, 16, CB])
            yv = self.T("yv", [128, 16, CB])
            uk = [("U", ct) for ct in range(16)]
            if pas == 1:
                S.op("dve", lambda h: h.memset(hist[:, :, :, 0], 0.0), writes=[("hs", 0, 0), ("hs", 0, 1)])
            else:
                S.op("dve", lambda h: h.tensor_copy(out=hist[:, :, :, 0], in_=self.H0S[:, :, :]), reads=["H0S"], writes=[("hs", 0, 0), ("hs", 0, 1)])
            for blk in range(NBK):
                t0 = blk * CB
                slots = [("hs", t + 1, hv) for t in range(CB) for hv in range(2)]
                for gb in range(16):
                    um = Um[gb % 4]
                    uk1 = ("Um", gb % 4)
                    S.op("pool", lambda h, um=um, gb=gb, t0=t0: h.tensor_tensor(
                        out=um[:, :, :], in0=U[:, gb, None, t0:t0 + CB].broadcast_to([128, 8, CB]),
                        in1=PM[:, :, None].broadcast_to([128, 8, CB]), op=ALU.mult),
                         reads=[("U", gb), "PM"], writes=[uk1])
                    bank = gb % 6
                    for ri in range(2):
                        for gl in range(8):
                            col = (ri * 8 + gl) * CB
                            S.op("pe", lambda h, um=um, gb=gb, ri=ri, gl=gl, col=col, bank=bank: h.matmul(
                                PS[0:64, bank, col:col + CB], BTb[:, gb, ri * 64:(ri + 1) * 64], um[:, gl, :], start=True, stop=True),
                                 reads=[uk1, "BTb"], writes=[("ps", bank)])
                    S.op("act", lambda h, gb=gb, bank=bank: h.activation(
                        out=hist[:, :, gb * 8:(gb + 1) * 8, 1:CB + 1],
                        in_=PS[0:64, bank, 0:16 * CB].rearrange("p (r g t) -> p r g t", r=2, g=8), func=AF.Identity),
                         reads=[("ps", bank)], writes=[("hs", t + 1, gb // 8) for t in range(CB)])
                for t in range(CB):
                    for hv in range(2):
                        gs = slice(hv * 64, (hv + 1) * 64)
                        S.op("dve", lambda h, t=t, hv=hv, gs=gs: h.tensor_tensor(out=T1[:, :, gs], in0=hist[:, :, gs, t], in1=LD1[:, :, gs], op=ALU.mult),
                             reads=[("hs", t, hv), "LD1"], writes=[("T1", hv)])
                    for hv in range(2):
                        gs = slice(hv * 64, (hv + 1) * 64)
                        S.op("dve", lambda h, t=t, hv=hv, gs=gs: h.tensor_tensor(out=T2[:, 0, gs], in0=hist[:, 1, gs, t], in1=LD2[:, 0, gs], op=ALU.mult),
                             reads=[("hs", t, hv), "LD2"], writes=[("T2a", hv)])
                    for hv in range(2):
                        gs = slice(hv * 64, (hv + 1) * 64)
                        S.op("dve", lambda h, t=t, hv=hv, gs=gs: h.tensor_tensor(out=T2[:, 1, gs], in0=hist[:, 0, gs, t], in1=LD2[:, 1, gs], op=ALU.mult),
                             reads=[("hs", t, hv), "LD2"], writes=[("T2b", hv)])
                    for hv in range(2):
                        gs = slice(hv * 64, (hv + 1) * 64)
                        S.op("dve", lambda h, hv=hv, gs=gs: h.tensor_tensor(out=T1[:, :, gs], in0=T1[:, :, gs], in1=T2[:, :, gs], op=ALU.add),
                             reads=[("T1", hv), ("T2a", hv), ("T2b", hv)], writes=[("T1", hv)])
                    for hv in range(2):
                        gs = slice(hv * 64, (hv + 1) * 64)
                        S.op("dve", lambda h, t=t, hv=hv, gs=gs: h.tensor_tensor(out=hist[:, :, gs, t + 1], in0=hist[:, :, gs, t + 1], in1=T1[:, :, gs], op=ALU.add),
                             reads=[("hs", t + 1, hv), ("T1", hv)], writes=[("hs", t + 1, hv)])
                if pas == 1:
                    if blk == NBK - 1:
                        sl = S5_EXP - t0 + 1
                        S.op("dve", lambda h, sl=sl: h.tensor_copy(out=self.GATH[0:64, 32:288].rearrange("p (r g) -> p r g", r=2),
                                                                   in_=hist[:, :, :, sl]),
                             reads=[("hs", sl, 0), ("hs", sl, 1)], writes=["GATH"])
                else:
                    for g in range(128):
                        bank = g // 32
                        for ri in range(2):
                            S.op("pe", lambda h, g=g, ri=ri, bank=bank: h.matmul(
                                PS[0:CB, bank, (g % 32) * 16:(g % 32) * 16 + 16], hist[:, ri, g, 1:CB + 1], CT[:, ri, g, :],
                                start=(ri == 0), stop=(ri == 1)),
                                 reads=slots + ["CT"], writes=[("ps", bank)])
                    S.op("act", lambda h: h.activation(out=ytm[:, :].rearrange("p (b c) -> p b c", c=512), in_=PS[0:CB, 0:4, :],
                                                       func=AF.Identity),
                         reads=[("ps", b) for b in range(4)], writes=["ytm"])
                    for ct in range(16):
                        S.op("pe", lambda h, ct=ct: h.transpose(out=PS[:, 4, ct * CB:(ct + 1) * CB], in_=ytm[0:CB, ct * 128:(ct + 1) * 128],
                                                                identity=ident[0:CB, 0:CB]),
                             reads=["ytm", "ident"], writes=[("ps", 4)])
                    S.op("act", lambda h: h.activation(out=ysb[:, :, :], in_=PS[:, 4, 0:16 * CB].rearrange("p (c t) -> p c t", t=CB),
                                                       func=AF.Identity),
                         reads=[("ps", 4)], writes=["ysb"])
                    S.op("pool", lambda h, t0=t0: h.tensor_tensor(out=yv[:, :, :], in0=U[:, :, t0:t0 + CB],
                                                                   in1=self.DCOL[:, :, None].broadcast_to([128, 16, CB]), op=ALU.mult),
                         reads=uk + ["DCOL"], writes=["yv"])
                    S.op("pool", lambda h: h.tensor_tensor(out=yv[:, :, :], in0=yv[:, :, :], in1=ysb[:, :, :], op=ALU.add),
                         reads=["yv", "ysb"], writes=["yv"])
                    S.op("act", lambda h, t0=t0: h.activation(out=U[:, :, t0:t0 + CB], in_=yv[:, :, :], func=AF.Gelu),
                         reads=["yv"], writes=uk)
                if blk < NBK - 1:
                    S.op("dve", lambda h: h.tensor_copy(out=hist[:, :, :, 0], in_=hist[:, :, :, CB]),
                         reads=[("hs", CB, 0), ("hs", CB, 1)], writes=[("hs", 0, 0), ("hs", 0, 1)])
            if pas == 1:
                self.dbg(S, "gath_s5", self.GATH[0:64, 32:288], [64, 256], reads=["GATH"])
            else:
                self.dbg(S, "yg", U[:, :, :], [128, 16, NT], BF16, reads=uk)

    def p_gather(self):
        nc = self.nc
        GATH, FL = self.GATH, self.FL
        with self.phase() as S:
            NCORE = self.ncore
            G8 = self.T("G8", [128, NCORE, 288])
            S.dma("pool", lambda h: h.dma_start(out=self.ag_in.ap(), in_=GATH[:, :]), writes=["ag_in"])
            S.op("pool", lambda h: h.collective_compute("AllGather", ALU.bypass, replica_groups=[list(range(NCORE))],
                                                        ins=[self.ag_in.ap().opt()], outs=[self.ag_out.ap().opt()]),
                 reads=["ag_in"], writes=["ag_out"])
            S.dma("pool", lambda h: h.dma_start(out=G8[:, :, :], in_=self.ag_out.ap().rearrange("(r p) c -> p r c", p=128)),
                  reads=["ag_out"], writes=["G8"])
            H0L = self.H0L
            tl = self.T("tl", [128, 16])
            S.op("dve", lambda h: h.memset(H0L[:, :], 0.0), writes=["H0L"])
            for r in range(NCORE):
                S.op("dve", lambda h, r=r: h.tensor_tensor(out=tl[:, :], in0=G8[:, r, 0:16], in1=H0L[:, :], op=ALU.mult),
                     reads=["G8", "H0L"], writes=["tl"])
                S.op("dve", lambda h, r=r: h.tensor_tensor(out=tl[:, :], in0=tl[:, :], in1=G8[:, r, 16:32], op=ALU.add),
                     reads=["tl", "G8"], writes=["tl"])
                S.op("dve", lambda h: h.tensor_tensor(out=tl[:, :], in0=tl[:, :], in1=H0L[:, :], op=ALU.subtract),
                     reads=["tl", "H0L"], writes=["tl"])
                S.op("dve", lambda h, r=r: h.scalar_tensor_tensor(out=H0L[:, :], in0=tl[:, :], scalar=FL[:, 9 + r:10 + r], in1=H0L[:, :],
                                                                  op0=ALU.mult, op1=ALU.add),
                     reads=["tl", "H0L", "FL"], writes=["H0L"])
            AP_ = [self.T("AP0", [64, 2, 128]), self.T("AP1", [64, 2, 128])]
            tq = self.T("tqs", [64, 128]); tq2 = self.T("tqs2", [64, 128])
            S.op("dve", lambda h: h.tensor_copy(out=AP_[0][:, :, :], in_=self.LBAR[:, :, :]), writes=["AP0"])
            cur = 0
            for it in range(10):
                nx = 1 - cur
                a, b = AP_[cur], AP_[nx]
                S.op("dve", lambda h, a=a: h.tensor_tensor(out=tq[:, :], in0=a[:, 0, :], in1=a[:, 0, :], op=ALU.mult), reads=[f"AP{cur}"], writes=["tq"])
                S.op("dve", lambda h, a=a: h.tensor_tensor(out=tq2[:, :], in0=a[:, 1, :], in1=a[:, 1, :], op=ALU.mult), reads=[f"AP{cur}"], writes=["tq2"])
                S.op("dve", lambda h, b=b: h.tensor_tensor(out=b[:, 0, :], in0=tq[:, :], in1=tq2[:, :], op=ALU.subtract), reads=["tq", "tq2"], writes=[f"AP{nx}"])
                S.op("dve", lambda h, a=a, b=b: h.scalar_tensor_tensor(out=b[:, 1, :], in0=a[:, 0, :], scalar=2.0, in1=a[:, 1, :],
                                                                       op0=ALU.mult, op1=ALU.mult), reads=[f"AP{cur}"], writes=[f"AP{nx}"])
                cur = nx
            A = AP_[cur]
            ak = f"AP{cur}"
            H0S = self.H0S
            ts = self.T("ts5", [64, 2, 128]); tu = self.T("tu5", [64, 128])
            S.op("dve", lambda h: h.memset(H0S[:, :, :], 0.0), writes=["H0S"])
            for r in range(NCORE):
                E = G8[0:64, r, 32:288].rearrange("p (r g) -> p r g", r=2)
                S.op("dve", lambda h: h.tensor_tensor(out=ts[:, 0, :], in0=A[:, 0, :], in1=H0S[:, 0, :], op=ALU.mult), reads=[ak, "H0S"], writes=["ts0"])
                S.op("dve", lambda h: h.tensor_tensor(out=tu[:, :], in0=A[:, 1, :], in1=H0S[:, 1, :], op=ALU.mult), reads=[ak, "H0S"], writes=["tu"])
                S.op("dve", lambda h: h.tensor_tensor(out=ts[:, 0, :], in0=ts[:, 0, :], in1=tu[:, :], op=ALU.subtract), reads=["ts0", "tu"], writes=["ts0"])
                S.op("dve", lambda h: h.tensor_tensor(out=ts[:, 1, :], in0=A[:, 0, :], in1=H0S[:, 1, :], op=ALU.mult), reads=[ak, "H0S"], writes=["ts1"])
                S.op("dve", lambda h: h.tensor_tensor(out=tu[:, :], in0=A[:, 1, :], in1=H0S[:, 0, :], op=ALU.mult), reads=[ak, "H0S", "ts0"], writes=["tu"])
                S.op("dve", lambda h: h.tensor_tensor(out=ts[:, 1, :], in0=ts[:, 1, :], in1=tu[:, :], op=ALU.add), reads=["ts1", "tu"], writes=["ts1"])
                S.op("dve", lambda h, E=E: h.tensor_tensor(out=ts[:, :, :], in0=ts[:, :, :], in1=E, op=ALU.add), reads=["ts0", "ts1", "G8"], writes=["ts0", "ts1"])
                S.op("dve", lambda h: h.tensor_tensor(out=ts[:, :, :], in0=ts[:, :, :], in1=H0S[:, :, :], op=ALU.subtract), reads=["ts0", "ts1", "H0S"], writes=["ts0", "ts1"])
                S.op("dve", lambda h, r=r: h.scalar_tensor_tensor(out=H0S[:, :, :], in0=ts[:, :, :], scalar=FL[0:64, 9 + r:10 + r], in1=H0S[:, :, :],
                                                                  op0=ALU.mult, op1=ALU.add), reads=["ts0", "ts1", "H0S", "FL"], writes=["H0S"])
            self.dbg(S, "h0l", H0L[:, :], [128, 16], reads=["H0L"])
            self.dbg(S, "h0s", H0S[:, :, :], [64, 2, 128], reads=["H0S"])

    def p_glu(self):
        PS, WA, U = self.PS, self.WA, self.U
        ys_d = self.nc.dram_tensor("ys_d", [16, 128, NT], BF16).ap()
        with self.phase() as S, self.more_slots():
            yst = [self.T("yst0", [128, NT], BF16), self.T("yst1", [128, NT], BF16)]
            BG = self.T("BG", [128, 16])
            self.load_cols(S, BG, self.b_glu[0].rearrange("(c p) -> c p", p=128), 16, "BG", "tbg")
            sg = [self.T("sg0", [128, NT]), self.T("sg1", [128, NT])]
            uk = [("U", ct) for ct in range(16)]
            for ot in range(16):
                slot = self.wslot()
                wv = self.slot_ap(slot).rearrange("p (k c) -> p k c", c=128)
                self.wload(S, wv[:, 0:16, :], self.w_glu[0][:, ot * 128:(ot + 1) * 128].rearrange("(k p) c -> p k c", p=128), slot)
                pb = self.big()
                self.fm_matmul(S, pb, wv, 16, lambda kt, b: U[:, kt, b * CW:(b + 1) * CW], uk, slot)
                s = sg[ot % 2]
                S.op("act", lambda h, pb=pb, ot=ot, s=s: h.activation(out=v3(s[:, :]), in_=PS[:, pb:pb + 3, 0:CW], func=AF.Sigmoid,
                                                                      bias=BG[:, ot:ot + 1]),
                     reads=[("ps", pb + b) for b in range(3)] + ["BG"], writes=[("sg", ot % 2)])
                yt = yst[ot % 2]
                S.op("dve", lambda h, ot=ot, s=s, yt=yt: h.tensor_tensor(out=yt[:, :], in0=s[:, :], in1=U[:, ot, :], op=ALU.mult),
                     reads=[("sg", ot % 2), ("U", ot)], writes=[("yst", ot % 2)])
                S.dma("sp", lambda h, ot=ot, yt=yt: h.dma_start(out=ys_d[ot], in_=yt[:, :]), reads=[("yst", ot % 2)], writes=["ys_d"])
        with self.phase() as S:
            S.dma("sp", lambda h: h.dma_start(out=U[:, :, :], in_=ys_d.rearrange("k p t -> p k t")), writes=[("ys", ct) for ct in range(16)])
            self.dbg(S, "ys", U[:, :, :], [128, 16, NT], BF16, reads=[("ys", ct) for ct in range(16)])

    def p_merge(self):
        PS, WA, BIG1, ya, ys = self.PS, self.WA, self.BIG1, self.ya, self.ys
        w_in = self.w_in[0]
        hk = [("big1", ft) for ft in range(KT)]
        with self.phase() as S, self.more_slots():
            sa = self.T("sa", [128, NT]); sb_ = self.T("sb", [128, NT])
            m1 = self.T("m1", [128, NT]); m2 = self.T("m2", [128, NT])
            mg = [self.T("mg0", [128, NT], BF16), self.T("mg1", [128, NT], BF16)]
            yak = [("ya", ct) for ct in range(16)]
            ysk = [("ys", ct) for ct in range(16)]
            for mt in range(KT):
                def gate(col0, dst, key):
                    slot = self.wslot()
                    wv = self.slot_ap(slot).rearrange("p (k c) -> p k c", c=128)
                    self.wload(S, wv, w_in[:, col0 + mt * 128:col0 + (mt + 1) * 128].rearrange("(k p) c -> p k c", p=128), slot)
                    pb = self.big()
                    self.fm_matmul(S, pb, wv, KT, lambda kt, b: BIG1[:, kt, b * CW:(b + 1) * CW], hk, slot)
                    S.op("act", lambda h, pb=pb: h.activation(out=v3(dst[:, :]), in_=PS[:, pb:pb + 3, 0:CW], func=AF.Sigmoid),
                         reads=[("ps", pb + b) for b in range(3)], writes=[key])

                def proj(wd, src, skeys, gt, gkey, dst, dkey):
                    slot = self.wslot()
                    wv = self.slot_ap(slot).rearrange("p (k c) -> p k c", c=128)
                    self.wload(S, wv[:, 0:16, :], wd[:, mt * 128:(mt + 1) * 128].rearrange("(k p) c -> p k c", p=128), slot)
                    pb = self.big()
                    self.fm_matmul(S, pb, wv, 16, lambda kt, b: src[:, kt, b * CW:(b + 1) * CW], skeys, slot)
                    S.op("dve", lambda h, pb=pb: h.tensor_tensor(out=v3(dst[:, :]), in0=PS[:, pb:pb + 3, 0:CW], in1=v3(gt[:, :]), op=ALU.mult),
                         reads=[("ps", pb + b) for b in range(3)] + [gkey], writes=[dkey])

                gate(6144, sa, "sa")
                proj(self.w_proj_lru[0], ya, yak, sa, "sa", m1, "m1")
                gate(10240, sb_, "sb")
                proj(self.w_proj_ssm[0], ys, ysk, sb_, "sb", m2, "m2")
                mo = mg[mt % 2]
                S.op("dve", lambda h, mo=mo: h.tensor_tensor(out=mo[:, :], in0=m1[:, :], in1=m2[:, :], op=ALU.add),
                     reads=["m1", "m2"], writes=[("mg", mt % 2)])
                S.dma("sp", lambda h, mo=mo, mt=mt: h.dma_start(out=self.mg_d[mt], in_=mo[:, :]), reads=[("mg", mt % 2)], writes=["mg_d"])

    def p_wout(self):
        PS, WA, BIG1 = self.PS, self.WA, self.BIG1
        w_out = self.w_out[0]
        with self.phase() as S, self.more_slots():
            S.dma("sp", lambda h: h.dma_start(out=BIG1[:, :, :], in_=self.mg_d.rearrange("k p t -> p k t")), writes=["mgall"])
            GB = self.T("GB", [128, D])
            self.bcast_cond(S, GB, 64)
            xr = [self.T(f"xres{i}", [128, 512]) for i in range(3)]
            yo = [self.T(f"yo{i}", [128, 512]) for i in range(3)]
            tiles = [(6, 2)] + [(HALO + 128 * i, 128) for i in range(8)]
            cnt = 0
            for batch in (tiles[0:5], tiles[5:9]):
                for nch in range(8):
                    for kg in range(4):
                        slot = self.wslot()
                        wv = self.slot_ap(slot).rearrange("p (k c) -> p k c", c=512)
                        self.wload(S, wv, w_out[kg * 1024:(kg + 1) * 1024, nch * 512:(nch + 1) * 512].rearrange("(k p) c -> p k c", p=128), slot)
                        for ti, (r0, M) in enumerate(batch):
                            for kt in range(8):
                                k = kg * 8 + kt
                                S.op("pe", lambda h, ti=ti, r0=r0, M=M, k=k, kt=kt, wv=wv, kg=kg: h.matmul(
                                    PS[0:M, ti, :], BIG1[:, k, r0:r0 + M], wv[:, kt, :], start=(k == 0), stop=(k == KT - 1)),
                                     reads=["mgall", ("wa", slot)], writes=[("ps", ti)])
                    for ti, (r0, M) in enumerate(batch):
                        i3 = cnt % 3
                        cnt += 1
                        xt, yt = xr[i3], yo[i3]
                        S.dma("act", lambda h, xt=xt, r0=r0, M=M, nch=nch: h.dma_start(out=xt[0:M, :], in_=self.xh[r0:r0 + M, nch * 512:(nch + 1) * 512]),
                              writes=[("xres", i3)])
                        S.op("dve", lambda h, yt=yt, ti=ti, M=M, nch=nch: h.tensor_tensor(out=yt[0:M, :], in0=PS[0:M, ti, :],
                                                                                          in1=GB[0:M, nch * 512:(nch + 1) * 512], op=ALU.mult),
                             reads=[("ps", ti), "GB"], writes=[("yo", i3)])
                        S.op("dve", lambda h, yt=yt, xt=xt, M=M: h.scalar_tensor_tensor(out=yt[0:M, :], in0=xt[0:M, :], scalar=ALPHA, in1=yt[0:M, :],
                                                                                          op0=ALU.mult, op1=ALU.add),
                             reads=[("yo", i3), ("xres", i3)], writes=[("yo", i3)])
                        S.dma("sp", lambda h, yt=yt, r0=r0, M=M, nch=nch: h.dma_start(out=self.y1_d[r0:r0 + M, nch * 512:(nch + 1) * 512], in_=yt[0:M, :]),
                              reads=[("yo", i3)], writes=["y1_d"])

    def p_ln_mid(self):
        with self.phase() as S:
            self.lnJ = self.T("lnJ", [128, D], BF16)
            self.lnJ2 = self.T("lnJ2", [128, D])
            self.lnst = self.T("lnst", [128, 4])
            G = self.T("lnG", [128, D]); Bt = self.T("lnB", [128, D])
            S.dma("sp", lambda h: h.dma_start(out=G[:, :], in_=self.ln1_g.partition_broadcast(128)), writes=["lnG"])
            S.dma("sp", lambda h: h.dma_start(out=Bt[:, :], in_=self.ln1_b.partition_broadcast(128)), writes=["lnB"])
            XT = [self.T(f"XT{i}", [128, D]) for i in range(2)]
            nev = [0]
            tiles = [(6, 2)] + [(HALO + 128 * i, 128) for i in range(8)]
            for ti, (r0, M) in enumerate(tiles):
                X = XT[ti % 2]
                tag = ti % 2
                S.dma("act", lambda h, X=X, r0=r0, M=M: h.dma_start(out=X[0:M, :], in_=self.y1_d[r0:r0 + M, :]), writes=[("X", tag)])
                self.ln_rows(S, X, M, tag, affine=(G, Bt))
                S.dma("sp", lambda h, X=X, r0=r0, M=M: h.dma_start(out=self.x1_d[r0:r0 + M, :], in_=X[0:M, :]), reads=[("X", tag)], writes=["x1_d"])
                self.ln_rows(S, X, M, tag)
                self.rows_to_T(S, X, M, tag, r0, 128, 96, nev)
            self.dbg(S, "h2T", self.BIG1[:, :, :], [128, KT, NT], BF16, reads=[("big1", ft) for ft in range(KT)])

    def p_ffn(self):
        PS, WA, BIG1, FL = self.PS, self.WA, self.BIG1, self.FL
        w_up = self.ffn_w_up[0]
        w_dn = self.ffn_w_down[0]
        NH = 514
        HC = 257
        hk = [("big1", ft) for ft in range(KT)]
        with self.phase() as S:
            FCW = self.T("FCW", [128, 516])
            FCB = self.T("FCB", [128, 172])
            self.load_cols(S, FCW, self.ffn_conv_w[0].rearrange("k (c p) -> (k c) p", p=128), 516, "FCW", "tfw")
            self.load_cols(S, FCB, self.ffn_conv_b[0].rearrange("(c p) -> c p", p=128), 172, "FCB", "tfb")
            GBc = self.T("GBc", [128, 512])
            Dg = self.T("ffn_diag", [128, 2, 128])
            aT = self.T("aT", [128, NFT, 512], BF16)
            us = [self.T(f"us{i}", [128, NH]) for i in range(2)]
            cgs = [self.T("cg0", [128, 512]), self.T("cg1", [128, 512])]; cv = self.T("cv", [128, 512])
            xr = [self.T(f"xres{i}", [128, 512]) for i in range(2)]
            yo = [self.T(f"yo{i}", [128, 512]) for i in range(2)]
            cnt = 0
            for hf in range(2):
                c0 = 6 + 512 * hf
                for jp in range(NFT // 2):
                    for which in range(2):
                        col0 = which * DFF + jp * 256
                        pbase = 4 * which
                        for kg in range(2):
                            slot = self.wslot()
                            wv = self.slot_ap(slot).rearrange("p (k c) -> p k c", c=256)
                            self.wload(S, wv, w_up[kg * 2048:(kg + 1) * 2048, col0:col0 + 256].rearrange("(k p) c -> p k c", p=128), slot)
                            for tt in range(2):
                                for kt in range(16):
                                    k = kg * 16 + kt
                                    for b in range(2):
                                        bank = pbase + tt * 2 + b
                                        S.op("pe", lambda h, kt=kt, k=k, b=b, bank=bank, wv=wv, tt=tt: h.matmul(
                                            PS[:, bank, 0:HC], wv[:, kt, tt * 128:(tt + 1) * 128], BIG1[:, k, c0 + b * HC:c0 + (b + 1) * HC],
                                            start=(k == 0), stop=(k == KT - 1)),
                                             reads=[("wa", slot)] + hk, writes=[("ps", bank)])
                        for tt in range(2):
                            j = jp * 2 + tt
                            tile = which * NFT + j
                            pb = pbase + tt * 2
                            u = us[which]
                            uk = ("us", which)
                            S.op("act", lambda h, u=u, pb=pb: h.activation(out=u[:, :].rearrange("p (b c) -> p b c", c=HC), in_=PS[:, pb:pb + 2, 0:HC],
                                                                           func=AF.Identity),
                                 reads=[("ps", pb), ("ps", pb + 1)], writes=[uk])
                            if hf == 0:
                                S.op("dve", lambda h, u=u: h.tensor_scalar(out=u[:, 0:2], in0=u[:, 0:2], scalar1=FL[:, 0:1], scalar2=None, op0=ALU.mult),
                                     reads=[uk, "FL"], writes=[uk])
                            dst = cgs[tt] if which == 0 else cv
                            dk = ("cg", tt) if which == 0 else "cv"
                            fw = lambda k, tile=tile: FCW[:, k * 172 + tile:k * 172 + tile + 1]
                            S.op("dve", lambda h, u=u, dst=dst, fw=fw, tile=tile: h.tensor_scalar(out=dst[:, :], in0=u[:, 0:512], scalar1=fw(0),
                                                                                                  scalar2=FCB[:, tile:tile + 1], op0=ALU.mult, op1=ALU.add),
                                 reads=[uk, "FCW", "FCB"], writes=[dk])
                            for k in (1, 2):
                                S.op("dve", lambda h, u=u, dst=dst, fw=fw, k=k: h.scalar_tensor_tensor(out=dst[:, :], in0=u[:, k:k + 512], scalar=fw(k),
                                                                                                       in1=dst[:, :], op0=ALU.mult, op1=ALU.add),
                                     reads=[uk, dk, "FCW"], writes=[dk])
                            if which == 0:
                                S.op("act", lambda h, dst=dst: h.activation(out=dst[:, :], in_=dst[:, :], func=AF.Gelu), reads=[dk], writes=[dk])
                            else:
                                S.op("dve", lambda h, j=j, tt=tt: h.tensor_tensor(out=aT[:, j, :], in0=cgs[tt][:, :], in1=cv[:, :], op=ALU.mult),
                                     reads=[("cg", tt), "cv"], writes=[("aT", j)])
                ak = [("aT", j) for j in range(NFT)]
                for nch in range(8):
                    ngr = (NFT + 7) // 8
                    for kg in range(ngr):
                        nk = min(8, NFT - kg * 8)
                        slot = self.wslot()
                        wv = self.slot_ap(slot).rearrange("p (k c) -> p k c", c=512)
                        self.wload(S, wv[:, 0:nk, :], w_dn[kg * 1024:kg * 1024 + nk * 128, nch * 512:(nch + 1) * 512].rearrange("(k p) c -> p k c", p=128), slot)
                        for ti in range(4):
                            for kt in range(nk):
                                k = kg * 8 + kt
                                S.op("pe", lambda h, ti=ti, k=k, kt=kt, wv=wv: h.matmul(
                                    PS[:, 4 + ti, :], aT[:, k, ti * 128:(ti + 1) * 128], wv[:, kt, :], start=(k == 0), stop=(k == NFT - 1)),
                                     reads=ak + [("wa", slot)], writes=[("ps", 4 + ti)])
                    for q4 in range(4):
                        ft = nch * 4 + q4
                        s2 = q4 % 2
                        S.op("dve", lambda h, ft=ft, s2=s2: h.tensor_scalar(out=Dg[:, s2, :], in0=self.ident[:, :],
                                                                             scalar1=self.COND[:, 160 + ft:161 + ft], scalar2=None, op0=ALU.mult),
                             reads=["ident", "COND"], writes=[("fdg", s2)])
                        S.op("pe", lambda h, s2=s2: h.matmul(PS[:, 2 + s2, 0:128], self.ones[:, :], Dg[:, s2, :], start=True, stop=True),
                             reads=[("fdg", s2), "ones"], writes=[("ps", 2 + s2)])
                        S.op("act", lambda h, q4=q4, s2=s2: h.activation(out=GBc[:, q4 * 128:(q4 + 1) * 128], in_=PS[:, 2 + s2, 0:128], func=AF.Identity),
                             reads=[("ps", 2 + s2)], writes=["GBc"])
                    for ti in range(4):
                        r0 = HALO + 512 * hf + 128 * ti
                        i3 = cnt % 2
                        cnt += 1
                        xt, yt = xr[i3], yo[i3]
                        S.dma("act", lambda h, xt=xt, r0=r0, nch=nch: h.dma_start(out=xt[:, :], in_=self.x1_d[r0:r0 + 128, nch * 512:(nch + 1) * 512]),
                              writes=[("xres", i3)])
                        S.op("dve", lambda h, yt=yt, ti=ti, nch=nch: h.tensor_tensor(out=yt[:, :], in0=PS[:, 4 + ti, :],
                                                                                     in1=GBc[:, :], op=ALU.mult),
                             reads=[("ps", 4 + ti), "GBc"], writes=[("yo", i3)])
                        S.op("dve", lambda h, yt=yt, xt=xt: h.scalar_tensor_tensor(out=yt[:, :], in0=xt[:, :], scalar=ALPHA, in1=yt[:, :],
                                                                                     op0=ALU.mult, op1=ALU.add),
                             reads=[("yo", i3), ("xres", i3)], writes=[("yo", i3)])
                        S.dma("sp", lambda h, yt=yt, r0=r0, nch=nch: h.dma_start(out=self.y2_d[r0:r0 + 128, nch * 512:(nch + 1) * 512], in_=yt[:, :]),
                              reads=[("yo", i3)], writes=["y2_d"])

    def p_ln_final(self):
        with self.phase() as S:
            self.lnJ = self.T("lnJ", [128, D], BF16)
            self.lnJ2 = self.T("lnJ2", [128, D])
            self.lnst = self.T("lnst", [128, 4])
            G = self.T("lnG", [128, D]); Bt = self.T("lnB", [128, D])
            S.dma("sp", lambda h: h.dma_start(out=G[:, :], in_=self.ln2_g.partition_broadcast(128)), writes=["lnG"])
            S.dma("sp", lambda h: h.dma_start(out=Bt[:, :], in_=self.ln2_b.partition_broadcast(128)), writes=["lnB"])
            XT = [self.T(f"XT{i}", [128, D]) for i in range(2)]
            for ti in range(8):
                r0 = HALO + 128 * ti
                X = XT[ti % 2]
                tag = ti % 2
                S.dma("act", lambda h, X=X, r0=r0: h.dma_start(out=X[:, :], in_=self.y2_d[r0:r0 + 128, :]), writes=[("X", tag)])
                self.ln_rows(S, X, 128, tag, affine=(G, Bt))
                S.dma("sp", lambda h, X=X, ti=ti: h.dma_start(out=self.out[ti * 128:(ti + 1) * 128, :], in_=X[:, :]), reads=[("X", tag)], writes=["out"])


WEIGHT_NAMES = ["w_ada", "b_ada", "w_in", "conv_lru_w", "conv_lru_b", "lru_wr", "lru_br", "lru_wi", "lru_bi",
                "lru_lambda", "ssm_lam_re", "ssm_lam_im", "ssm_b_re", "ssm_b_im", "ssm_c_re", "ssm_c_im", "ssm_d",
                "ssm_log_step", "w_glu", "b_glu", "w_proj_lru", "w_proj_ssm", "w_out", "ln1_g", "ln1_b",
                "ffn_w_up", "ffn_conv_w", "ffn_conv_b", "ffn_w_down", "ln2_g", "ln2_b"]


def make_in_maps(inputs, used=None, ncore=NCORE):
    x = np.asarray(inputs["x"], dtype=np.float32)
    c = np.asarray(inputs["c"], dtype=np.float32)
    used = set(WEIGHT_NAMES + ["xh", "cT", "flags"]) if used is None else set(used)
    shared = {k: np.ascontiguousarray(np.asarray(inputs[k], dtype=np.float32)) for k in WEIGHT_NAMES if k in used}
    in_maps = []
    for k in range(ncore):
        b, j = k // 4, k % 4
        s = 1024 * j
        xh = np.zeros((NT, D), np.float32)
        if j == 0:
            xh[HALO:] = x[b, 0:1024]
        else:
            xh[:] = x[b, s - HALO:s + 1024]
        cT = np.ascontiguousarray(c[b].reshape(KT, 128).T)
        flags = np.zeros((1, 18), np.float32)
        flags[0, 17] = float(b)
        flags[0, 0] = 0.0 if j == 0 else 1.0
        nmask = 8 if j == 0 else 3
        flags[0, 1:9] = 1.0
        flags[0, 1:1 + nmask] = 0.0
        for r in range(NCORE):
            if r // 4 == b and r < k:
                flags[0, 9 + r] = 1.0
        m = dict(shared)
        for nm, arr in (("xh", xh), ("cT", cT), ("flags", flags)):
            if nm in used:
                m[nm] = arr
        in_maps.append(m)
    return in_maps


def kernel(**inputs):
    bld = Builder()
    nc = bld.build()
    in_maps = make_in_maps(inputs)
    res = run_bass_kernel_spmd(nc, in_maps, core_ids=list(range(NCORE)))
    out = np.zeros((2, 4096, D), np.float32)
    for k in range(NCORE):
        b, j = k // 4, k % 4
        out[b, 1024 * j:1024 * (j + 1)] = np.asarray(res.results[k]["out"], dtype=np.float32)
    return out
```
